# Optimizing a Trainium2 kernel written in Bass

```python
import jax, jax.numpy as jnp
from jax import lax
import numpy as np

D_MODEL = 1024
BATCH = 2
SEQ = 8192
DEPTH = 2

N_META = 16
BLOCK = 128
META_PAD = BLOCK - N_META
RMS_EPS = 1e-6
NEG_INF = -1e30

LRU_WIDTH = D_MODEL // 2
LRU_HEADS = 8
LRU_BLOCK_W = LRU_WIDTH // LRU_HEADS
CONV_W = 4
LRU_C = 8.0

ATT_HEADS = 8
ATT_KV_HEADS = 2
ATT_HEAD_DIM = 64
WINDOW = 128
ROPE_THETA = 500000.0
ROT_DIM = ATT_HEAD_DIM // 4
Q_WIDTH = ATT_HEADS * ATT_HEAD_DIM
KV_WIDTH = ATT_KV_HEADS * ATT_HEAD_DIM
AB_IN_WIDTH = 2 * LRU_WIDTH + Q_WIDTH + 2 * KV_WIDTH
AB_OUT_WIDTH = LRU_WIDTH + Q_WIDTH
SPLIT_AB = (LRU_WIDTH, 2 * LRU_WIDTH, 2 * LRU_WIDTH + Q_WIDTH, 2 * LRU_WIDTH + Q_WIDTH + KV_WIDTH)

RET_HEADS = 4
RET_QK_DIM = D_MODEL // RET_HEADS
RET_V_DIM = 2 * D_MODEL // RET_HEADS
RET_THETA = 10000.0
RET_IN_WIDTH = 6 * D_MODEL
SPLIT_RET = (D_MODEL, 2 * D_MODEL, 4 * D_MODEL)

D_FF = -(-8 * D_MODEL // (3 * 256)) * 256

kernel_name = 'hybrid_rglru_swa_retention_block'

F32 = jnp.float32


def _rmsnorm(x, gain):
    xf = x.astype(F32)
    y = xf * lax.rsqrt(jnp.mean(xf * xf, axis=-1, keepdims=True) + RMS_EPS)
    return (y * gain.astype(F32)).astype(x.dtype)


def _rms_noaffine(x):
    xf = x.astype(F32)
    return xf * lax.rsqrt(jnp.mean(xf * xf, axis=-1, keepdims=True) + RMS_EPS)


def _rope_tables(n_pos, rot_dim, theta):
    half = rot_dim // 2
    inv_freq = jnp.power(jnp.asarray(theta, F32), -jnp.arange(half, dtype=F32) / half)
    ang = jnp.arange(n_pos, dtype=F32)[:, None] * inv_freq[None, :]
    return jnp.cos(ang), jnp.sin(ang)


def _rotary(x, cos, sin):
    half = cos.shape[-1]
    c = cos[None, :, None, :]
    s = sin[None, :, None, :]
    x1 = x[..., :half].astype(F32)
    x2 = x[..., half:2 * half].astype(F32)
    rot = jnp.concatenate([x1 * c - x2 * s, x2 * c + x1 * s], axis=-1).astype(x.dtype)
    return jnp.concatenate([rot, x[..., 2 * half:]], axis=-1)


def _rg_lru(xr, conv_w, conv_b, w_a, b_a, w_i, b_i, lam):
    B, T, W = xr.shape
    xp = jnp.pad(xr, ((0, 0), (CONV_W - 1, 0), (0, 0)))
    xc = sum((xp[:, j:j + T, :] * conv_w[j] for j in range(CONV_W)), conv_b)
    xh = xc.reshape(B, T, LRU_HEADS, LRU_BLOCK_W)
    r = jax.nn.sigmoid((jnp.einsum('bthi,hij->bthj', xh, w_a) + b_a).astype(F32)).reshape(B, T, W)
    i = jax.nn.sigmoid((jnp.einsum('bthi,hij->bthj', xh, w_i) + b_i).astype(F32)).reshape(B, T, W)
    log_a = -LRU_C * r * jax.nn.softplus(-lam.astype(F32))
    a = jnp.exp(log_a)
    mult = jnp.sqrt(-jnp.expm1(2.0 * log_a))
    mult = jnp.where(jnp.arange(T)[None, :, None] == 0, 1.0, mult)
    b = mult * i * xc.astype(F32)

    def combine(left, right):
        a_l, b_l = left
        a_r, b_r = right
        return a_l * a_r, a_r * b_l + b_r

    _, h = lax.associative_scan(combine, (a, b), axis=1)
    return h.astype(xr.dtype)


def _swa_with_sinks(q, k, v, sinks):
    B, T, Hq, Dh = q.shape
    G = ATT_HEADS // ATT_KV_HEADS
    Tp = T + META_PAD
    NB = Tp // BLOCK
    qb = jnp.pad(q, ((0, 0), (META_PAD, 0), (0, 0), (0, 0))).reshape(B, NB, BLOCK, ATT_KV_HEADS, G, Dh)
    kpad = ((0, 0), (META_PAD + BLOCK, 0), (0, 0), (0, 0))
    kb = jnp.pad(k, kpad).reshape(B, NB + 1, BLOCK, ATT_KV_HEADS, Dh)
    vb = jnp.pad(v, kpad).reshape(B, NB + 1, BLOCK, ATT_KV_HEADS, Dh)
    k_win = jnp.concatenate([kb[:, :-1], kb[:, 1:]], axis=2)
    v_win = jnp.concatenate([vb[:, :-1], vb[:, 1:]], axis=2)
    k_meta = k[:, :N_META]
    v_meta = v[:, :N_META]
    scale = Dh ** -0.5
    s_meta = jnp.einsum('bnqhgd,bkhd->bnhgqk', qb, k_meta)
    s_win = jnp.einsum('bnqhgd,bnkhd->bnhgqk', qb, k_win)
    s = jnp.concatenate([s_meta, s_win], axis=-1).astype(F32) * scale
    nb = jnp.arange(NB)[:, None]
    qpos = nb * BLOCK + jnp.arange(BLOCK)[None, :] - META_PAD
    kpos = (nb - 1) * BLOCK + jnp.arange(2 * BLOCK)[None, :] - META_PAD
    qp = qpos[:, :, None]
    kp = kpos[:, None, :]
    win_mask = (kp >= N_META) & (kp <= qp) & (qp - kp < WINDOW)
    meta_mask = jnp.arange(N_META)[None, None, :] <= qp
    mask = jnp.concatenate([meta_mask, win_mask], axis=-1)
    s = jnp.where(mask[None, :, None, None], s, NEG_INF)
    sink = jnp.broadcast_to(sinks.astype(F32).reshape(1, 1, ATT_KV_HEADS, G, 1, 1), s.shape[:-1] + (1,))
    p = jax.nn.softmax(jnp.concatenate([s, sink], axis=-1), axis=-1)[..., :-1].astype(v.dtype)
    o = (jnp.einsum('bnhgqk,bkhd->bnqhgd', p[..., :N_META], v_meta)
         + jnp.einsum('bnhgqk,bnkhd->bnqhgd', p[..., N_META:], v_win))
    return o.reshape(B, Tp, Hq * Dh)[:, META_PAD:]


def _mixer_rglru_swa(h, cos, sin, w_in, conv_w, conv_b, w_a, b_a, w_i, b_i, lam,
                     q_gain, k_gain, sinks, w_out):
    B, T, _ = h.shape
    x_rnn, gate, q, k, v = jnp.split(h @ w_in, SPLIT_AB, axis=-1)
    y_rnn = jax.nn.gelu(gate) * _rg_lru(x_rnn, conv_w, conv_b, w_a, b_a, w_i, b_i, lam)
    q = _rotary(_rmsnorm(q.reshape(B, T, ATT_HEADS, ATT_HEAD_DIM), q_gain), cos, sin)
    k = _rotary(_rmsnorm(k.reshape(B, T, ATT_KV_HEADS, ATT_HEAD_DIM), k_gain), cos, sin)
    v = v.reshape(B, T, ATT_KV_HEADS, ATT_HEAD_DIM)
    y_att = _swa_with_sinks(q, k, v, sinks)
    return jnp.concatenate([y_rnn, y_att], axis=-1) @ w_out


def _mixer_retention(h, cos, sin, w_in, w_out):
    B, T, _ = h.shape
    q, k, v, g = jnp.split(h @ w_in, SPLIT_RET, axis=-1)
    q = _rotary(q.reshape(B, T, RET_HEADS, RET_QK_DIM), cos, sin)
    k = _rotary(k.reshape(B, T, RET_HEADS, RET_QK_DIM), cos, sin) * (RET_QK_DIM ** -0.5)
    v = v.reshape(B, T, RET_HEADS, RET_V_DIM)
    Tp = T + META_PAD
    NC = Tp // BLOCK

    def to_chunks(t):
        d = t.shape[-1]
        t = jnp.pad(t, ((0, 0), (META_PAD, 0), (0, 0), (0, 0)))
        return t.reshape(B, NC, BLOCK, RET_HEADS, d).transpose(1, 0, 3, 2, 4)

    qc, kc, vc = to_chunks(q), to_chunks(k), to_chunks(v)
    log_g = jnp.log1p(-jnp.exp2(-5.0 - jnp.arange(RET_HEADS, dtype=F32)))
    idx = jnp.arange(BLOCK, dtype=F32)
    diff = idx[:, None] - idx[None, :]
    decay_intra = jnp.where(diff >= 0, jnp.exp(jnp.maximum(diff, 0.0)[None] * log_g[:, None, None]), 0.0)
    decay_q = jnp.exp((idx + 1.0)[None] * log_g[:, None])
    decay_k = jnp.exp((BLOCK - 1.0 - idx)[None] * log_g[:, None])
    decay_chunk = jnp.exp(BLOCK * log_g)

    def step(state, chunk):
        qn, kn, vn = chunk
        vf = vn.astype(F32)
        qk = jnp.einsum('bhid,bhjd->bhij', qn, kn).astype(F32) * decay_intra
        o = (jnp.einsum('bhij,bhjv->bhiv', qk, vf)
             + jnp.einsum('bhid,bhdv->bhiv', qn.astype(F32) * decay_q[..., None], state))
        state = (decay_chunk[:, None, None] * state
                 + jnp.einsum('bhjd,bhjv->bhdv', kn.astype(F32) * decay_k[..., None], vf))
        return state, o

    state0 = jnp.zeros((B, RET_HEADS, RET_QK_DIM, RET_V_DIM), F32)
    _, o = lax.scan(step, state0, (qc, kc, vc))
    o = o.transpose(1, 0, 3, 2, 4).reshape(B, Tp, RET_HEADS, RET_V_DIM)[:, META_PAD:]
    o = _rms_noaffine(o).reshape(B, T, 2 * D_MODEL).astype(h.dtype)
    return (jax.nn.silu(g) * o) @ w_out


def _swiglu(h, w_gu, w_down):
    gate, up = jnp.split(h @ w_gu, 2, axis=-1)
    return (jax.nn.silu(gate) * up) @ w_down


def setup_inputs(seed: int = 0) -> dict:
    key = jax.random.key(seed)
    keys = jax.random.split(key, 24)
    n_even = (DEPTH + 1) // 2
    n_odd = DEPTH // 2

    def normal(k, shape, scale):
        return jax.random.normal(k, shape, F32) * scale

    def gain(k, shape):
        return 1.0 + 0.02 * jax.random.normal(k, shape, F32)

    u = jax.random.uniform(keys[10], (n_even, LRU_WIDTH), F32, minval=0.9, maxval=0.999)
    a = u ** (1.0 / LRU_C)
    lam = jnp.log(a) - jnp.log1p(-a)
    return {
        'x': normal(keys[0], (BATCH, SEQ, D_MODEL), 1.0),
        'meta_tokens': normal(keys[1], (N_META, D_MODEL), 1.0),
        'mix_norm_ab': gain(keys[2], (n_even, D_MODEL)),
        'ab_w_in': normal(keys[3], (n_even, D_MODEL, AB_IN_WIDTH), D_MODEL ** -0.5),
        'lru_conv_w': normal(keys[4], (n_even, CONV_W, LRU_WIDTH), CONV_W ** -0.5),
        'lru_conv_b': normal(keys[5], (n_even, LRU_WIDTH), 0.01),
        'lru_w_a': normal(keys[6], (n_even, LRU_HEADS, LRU_BLOCK_W, LRU_BLOCK_W), LRU_BLOCK_W ** -0.5),
        'lru_b_a': normal(keys[7], (n_even, LRU_HEADS, LRU_BLOCK_W), 0.01),
        'lru_w_i': normal(keys[8], (n_even, LRU_HEADS, LRU_BLOCK_W, LRU_BLOCK_W), LRU_BLOCK_W ** -0.5),
        'lru_b_i': normal(keys[9], (n_even, LRU_HEADS, LRU_BLOCK_W), 0.01),
        'lru_lambda': lam,
        'q_norm': gain(keys[11], (n_even, ATT_HEAD_DIM)),
        'k_norm': gain(keys[12], (n_even, ATT_HEAD_DIM)),
        'attn_sinks': normal(keys[13], (n_even, ATT_HEADS), 0.5),
        'ab_w_out': normal(keys[14], (n_even, AB_OUT_WIDTH, D_MODEL), AB_OUT_WIDTH ** -0.5),
        'mix_norm_ret': gain(keys[15], (n_odd, D_MODEL)),
        'ret_w_in': normal(keys[16], (n_odd, D_MODEL, RET_IN_WIDTH), D_MODEL ** -0.5),
        'ret_w_out': normal(keys[17], (n_odd, 2 * D_MODEL, D_MODEL), (2 * D_MODEL) ** -0.5),
        'ffn_norm': gain(keys[18], (DEPTH, D_MODEL)),
        'ffn_w_gu': normal(keys[19], (DEPTH, D_MODEL, 2 * D_FF), D_MODEL ** -0.5),
        'ffn_w_down': normal(keys[20], (DEPTH, D_FF, D_MODEL), D_FF ** -0.5),
    }


def reference(x, meta_tokens, mix_norm_ab, ab_w_in, lru_conv_w, lru_conv_b, lru_w_a, lru_b_a,
              lru_w_i, lru_b_i, lru_lambda, q_norm, k_norm, attn_sinks, ab_w_out,
              mix_norm_ret, ret_w_in, ret_w_out, ffn_norm, ffn_w_gu, ffn_w_down):
    B = x.shape[0]
    meta = jnp.broadcast_to(meta_tokens[None].astype(x.dtype), (B, N_META, D_MODEL))
    h = jnp.concatenate([meta, x], axis=1)
    T = h.shape[1]
    att_cos, att_sin = _rope_tables(T, ROT_DIM, ROPE_THETA)
    ret_cos, ret_sin = _rope_tables(T, RET_QK_DIM, RET_THETA)
    for layer in range(DEPTH):
        j = layer // 2
        if layer % 2 == 0:
            h = h + _mixer_rglru_swa(_rmsnorm(h, mix_norm_ab[j]), att_cos, att_sin, ab_w_in[j],
                                     lru_conv_w[j], lru_conv_b[j], lru_w_a[j], lru_b_a[j],
                                     lru_w_i[j], lru_b_i[j], lru_lambda[j], q_norm[j], k_norm[j],
                                     attn_sinks[j], ab_w_out[j])
        else:
            h = h + _mixer_retention(_rmsnorm(h, mix_norm_ret[j]), ret_cos, ret_sin,
                                     ret_w_in[j], ret_w_out[j])
        h = h + _swiglu(_rmsnorm(h, ffn_norm[layer]), ffn_w_gu[layer], ffn_w_down[layer])
    return h[:, N_META:]
```

```python
import numpy as np
import ml_dtypes
from contextlib import ExitStack
import concourse.bass as bass
import concourse.mybir as mybir
from concourse.bass_utils import run_bass_kernel_spmd

ACT = mybir.ActivationFunctionType
ALU = mybir.AluOpType
AX = mybir.AxisListType
F32 = mybir.dt.float32
BF16 = mybir.dt.bfloat16
U8 = mybir.dt.uint8
ENGS = ['pe', 'act', 'dve', 'pool', 'sp']

NOWN = 2048
NMETA = 16
NHALO = 128
NCOL = NOWN + NMETA
MOFF = NOWN
D = 1024
DFF = 2816
EPS = 1e-6
BW = 256
NBLK = NOWN // BW
CPB = BW // 128
LOGG = [float(np.log1p(-2.0 ** (-5.0 - h))) for h in range(4)]


class Buf:
    __slots__ = ('name', 'w', 'r', 'excl')

    def __init__(self, name):
        self.name = name
        self.w = None
        self.r = {}
        self.excl = False


class Tile:
    def __init__(self, ap, name):
        self.ap = ap
        self.b = Buf(name)

    def __getitem__(self, idx):
        return self.ap[idx]


class Ring:
    def __init__(self, tiles):
        self.t = tiles
        self.i = 0

    def next(self):
        t = self.t[self.i % len(self.t)]
        self.i += 1
        return t


def _bufs(lst):
    return [x.b if isinstance(x, Tile) else x for x in lst]


class Prog:
    def __init__(self, nc, es):
        self.nc = nc
        self.es = es
        self.ops = {e: [] for e in ENGS}
        self.waited = {e: set() for e in ENGS}
        self.dma_cnt = {}
        self.floor = {}

    def _deps(self, eng, reads, writes):
        waits = {}

        def need(dep):
            if dep is None:
                return
            k, v = dep
            if v <= self.floor.get(k, -1):
                return
            if waits.get(k, -1) < v:
                waits[k] = v
        for b in reads:
            need(b.w)
            if b.excl:
                for k, v in b.r.items():
                    if k != eng:
                        need((k, v))
        for b in writes:
            need(b.w)
            for k, v in b.r.items():
                need((k, v))
        if eng == 'pe':
            waits.pop('pe', None)
        for k, v in waits.items():
            if k in self.waited:
                self.waited[k].add(v)
        return waits

    def op(self, eng, fn, reads=(), writes=()):
        reads = _bufs(reads)
        writes = _bufs(writes)
        seq = len(self.ops[eng])
        waits = self._deps(eng, reads, writes)
        self.ops[eng].append(dict(fn=fn, waits=waits, dma=None))
        for b in reads:
            if b.r.get(eng, -1) < seq:
                b.r[eng] = seq
        for b in writes:
            b.w = (eng, seq)
            b.r = {}

    def dma(self, eng, fn, reads=(), writes=(), sem='misc'):
        reads = _bufs(reads)
        writes = _bufs(writes)
        waits = self._deps(eng, reads, writes)
        key = ('dma', sem)
        waits.pop(key, None)
        self.dma_cnt[key] = self.dma_cnt.get(key, 0) + 16
        val = self.dma_cnt[key]
        self.ops[eng].append(dict(fn=fn, waits=waits, dma=key))
        for b in reads:
            b.r[key] = val
        for b in writes:
            b.w = (key, val)
            b.r = {}

    def barrier(self):
        last = {}
        for e in ENGS:
            if self.ops[e]:
                for s in range(len(self.ops[e]) - 1, -1, -1):
                    o = self.ops[e][s]
                    if o['fn'] is not None and o['dma'] is None:
                        last[e] = s
                        break
        for k, v in self.dma_cnt.items():
            last[k] = v
        for e in ENGS:
            waits = {}
            for k, v in last.items():
                if k == e:
                    continue
                if v <= self.floor.get(k, -1):
                    continue
                waits[k] = v
                if k in self.waited:
                    self.waited[k].add(v)
            self.ops[e].append(dict(fn=None, waits=waits, dma=None))
        for k, v in last.items():
            self.floor[k] = v

    def build(self):
        nc, es = self.nc, self.es
        sems = {e: es.enter_context(nc.semaphore('s_' + e)) for e in ENGS}
        dsems = {k: es.enter_context(nc.semaphore('d_' + str(k[1]))) for k in self.dma_cnt}
        rank = {}
        for e in self.waited:
            for i, s in enumerate(sorted(self.waited[e])):
                rank[(e, s)] = i + 1
        block = es.enter_context(nc.Block())
        ops, waited = self.ops, self.waited

        def run(ename, e):
            for seq, o in enumerate(ops[ename]):
                for k, v in o['waits'].items():
                    if isinstance(k, tuple):
                        e.wait_ge(dsems[k], v)
                    else:
                        e.wait_ge(sems[k], rank[(k, v)])
                if o['fn'] is None:
                    continue
                ins = o['fn'](e)
                if o['dma'] is not None:
                    ins.then_inc(dsems[o['dma']], 16)
                elif seq in waited[ename]:
                    ins.then_inc(sems[ename], 1)

        @block.tensor
        def _(e):
            run('pe', e)

        @block.scalar
        def _(e):
            run('act', e)

        @block.vector
        def _(e):
            run('dve', e)

        @block.gpsimd
        def _(e):
            run('pool', e)

        @block.sync
        def _(e):
            run('sp', e)


class Arena:
    def __init__(self, nc, es, nbytes):
        self.t = es.enter_context(nc.sbuf_tensor("arena", [128, nbytes], U8))
        self.n = nbytes
        self.off = 0
        self.cnt = 0

    def mark(self):
        return self.off

    def release(self, m):
        self.off = m

    def alloc(self, shape, dt, name=None):
        esz = 4 if dt == F32 else 2
        n = int(np.prod(shape))
        nb = (n * esz + 63) // 64 * 64
        assert self.off + nb <= self.n, f"arena overflow {name} {self.off}+{nb}>{self.n}"
        ap = self.t[:, self.off:self.off + n * esz].bitcast(dt)
        self.off += nb
        if len(shape) == 2:
            ap = ap.rearrange("p (a b) -> p a b", a=shape[0])
        elif len(shape) == 3:
            ap = ap.rearrange("p (a b c) -> p a b c", a=shape[0], b=shape[1])
        self.cnt += 1
        return Tile(ap, name or f"t{self.cnt}")


PAR = {}
_o = 0
for _n, _w in [('g_ab', 8), ('g_ffn0', 8), ('g_ret', 8), ('g_ffn1', 8), ('convw', 16), ('convb', 4),
               ('ba', 4), ('bi', 4), ('lam', 4), ('qg', 1), ('kg', 1), ('sink', 8)]:
    PAR[_n] = (_o, _w)
    _o += _w
NPAR = _o

TAB = {}
_o = 0
for _n, _w in [('ident', 128), ('fm', 8), ('dk', 4), ('dkf', 64), ('dkm', 4),
               ('cf', 20), ('attc_m', 16), ('atts_m', 16), ('attc_h', 128), ('atts_h', 128),
               ('retc_m', 16), ('rets_m', 16), ('dt4', 512), ('dq4', 512)]:
    TAB[_n] = (_o, _w)
    _o += _w
NTAB = _o
NTABP = TAB['dt4'][0]
TAB2 = {}
_o = 0
for _n, _w in [('identb', 128), ('rperm', 128), ('bones', 128), ('mcur', 512), ('mprev', 512), ('mprev0', 512), ('mmeta', 64)]:
    TAB2[_n] = (_o, _w)
    _o += _w
NTAB2 = _o


def _host_tables(r):
    t = np.zeros((128, NTAB), np.float32)
    t2 = np.zeros((128, NTAB2), np.float32)

    def put(name, arr):
        arr = np.asarray(arr, np.float32)
        if name in TAB2:
            o, w = TAB2[name]
            assert arr.shape[1] == w, (name, arr.shape)
            t2[:arr.shape[0], o:o + w] = arr
        else:
            o, w = TAB[name]
            assert arr.shape[1] == w, (name, arr.shape)
            t[:arr.shape[0], o:o + w] = arr
    put('ident', np.eye(128))
    put('identb', np.eye(128))
    rp = np.zeros((128, 128))
    for m in range(128):
        d = m % 64
        if d < 8:
            rp[m + 8, m] = 1
        elif d < 16:
            rp[m - 8, m] = 1
    put('rperm', rp)
    bo = np.zeros((128, 128))
    bo[:64, :64] = 1
    bo[64:, 64:] = 1
    put('bones', bo)
    k = np.arange(128)[:, None]
    q = np.arange(128)[None, :]
    cur = (k <= q).astype(np.float32)
    prev = (k > q).astype(np.float32)
    put('mcur', np.tile(cur, (1, 4)))
    put('mprev', np.tile(prev, (1, 4)))
    put('mprev0', np.tile(prev if r > 0 else np.zeros_like(prev), (1, 4)))
    k16 = np.arange(16)[:, None]
    q16 = np.arange(16)[None, :]
    put('mmeta', np.tile((k16 <= q16).astype(np.float32), (1, 4)))
    fm = np.zeros((128, 8))
    for c in range(4):
        fm[:, c] = 1.0 if c < r else 0.0
        fm[:, 4 + c] = 1.0 - fm[:, c]
    put('fm', fm)
    lg = np.array(LOGG, np.float64)
    j = np.arange(128)[:, None]
    i = np.arange(128)[None, :]
    dt4 = np.concatenate([np.where(i >= j, np.exp(np.maximum(i - j, 0) * lg[h]), 0.0) / 16.0 for h in range(4)], 1)
    put('dt4', dt4)
    dq4 = np.concatenate([np.broadcast_to(np.exp((np.arange(128) + 1.0) * lg[h])[None, :], (128, 128)) for h in range(4)], 1)
    put('dq4', dq4)
    dk = np.stack([np.exp((127.0 - np.arange(128)) * lg[h]) / 16.0 for h in range(4)], 1)
    put('dk', dk)
    dkf = np.concatenate([dk * np.exp(128.0 * (15 - c) * lg)[None, :] for c in range(16)], 1)
    put('dkf', dkf)
    dkm = np.stack([np.exp((15.0 - np.arange(16)) * lg[h]) / 16.0 for h in range(4)], 1)
    put('dkm', dkm)
    cf = np.zeros((128, 20))
    for c in range(4):
        for h in range(4):
            cf[:, c * 4 + h] = np.exp(128.0 * 16 * (r - 1 - c) * lg[h]) if c < r else 0.0
    for h in range(4):
        cf[:, 16 + h] = np.exp(128.0 * 16 * r * lg[h])
    put('cf', cf)
    return t, t2


def _att_tables(pos):
    pos = np.asarray(pos, np.float32)
    inv = np.power(np.float32(500000.0), -np.arange(8, dtype=np.float32) / np.float32(8)).astype(np.float32)
    ang = (pos[:, None] * inv[None, :]).astype(np.float32)
    c = np.cos(ang).astype(np.float32).T
    s = np.sin(ang).astype(np.float32).T
    C = np.ones((128, len(pos)), np.float32)
    S = np.zeros((128, len(pos)), np.float32)
    for half in range(2):
        b = 64 * half
        C[b:b + 8] = c
        C[b + 8:b + 16] = c
        S[b:b + 8] = -s
        S[b + 8:b + 16] = s
    return C, S


def _ret_tables(pos):
    pos = np.asarray(pos, np.float32)
    inv = np.power(np.float32(10000.0), -np.arange(128, dtype=np.float32) / np.float32(128)).astype(np.float32)
    ang = (pos[:, None] * inv[None, :]).astype(np.float32)
    return np.cos(ang).astype(np.float32).T.copy(), np.sin(ang).astype(np.float32).T.copy()


class _Stop(Exception):
    pass


def build_program(dbg=(), stop=None):
    nc = bass.Bass("TRN2", target_bir_lowering=False)

    def din(name, shape, dt=F32):
        return nc.dram_tensor(name, list(shape), dt, kind="ExternalInput").ap()

    xin = din("xin", [NOWN + NMETA + NHALO, D])
    par_d = din("params", [128, NPAR])
    tab_d = din("tables", [128, NTAB])
    tab2_d = din("tables2", [128, NTAB2])
    attcs_d = din("attcs", [2, 128, NOWN])
    retcs_d = din("retcs", [2, 128, NOWN])
    w_in0 = din("ab_w_in", [D, 1792])
    w_out0 = din("ab_w_out", [D, D])
    wa_d = din("lru_w_a", [8, 64, 64])
    wi_d = din("lru_w_i", [8, 64, 64])
    w_in1 = din("ret_w_in", [D, 6144])
    w_out1 = din("ret_w_out", [2048, D])
    w_gu = din("ffn_w_gu", [2, D, 2 * DFF])
    w_dn = din("ffn_w_down", [2, DFF, D])
    out_d = nc.dram_tensor("out", [NOWN, D], F32, kind="ExternalOutput").ap()
    dbg_d = {n: nc.dram_tensor("dbg_" + n, list(s), F32, kind="ExternalOutput").ap() for n, s in dbg}

    cc1_in = nc.dram_tensor("cc1_in", [128, 8], F32)
    cc1_out = nc.dram_tensor("cc1_out", [512, 8], F32)
    cc2_in = nc.dram_tensor("cc2_in", [4, 128, 1024], BF16)
    cc2_out = nc.dram_tensor("cc2_out", [4, 512, 1024], BF16)
    scr_ab = nc.dram_tensor("scr_ab", [128, 2, 4, NCOL], F32)
    scr_sm = nc.dram_tensor("scr_sm", [128, 8, 512], F32)
    scr_kt = nc.dram_tensor("scr_kt", [128, 8, NOWN], BF16)
    scr_kd = nc.dram_tensor("scr_kd", [128, 16, 1024], BF16)
    scr_v = nc.dram_tensor("scr_v", [128, 16, 2048], BF16)
    scr_on = nc.dram_tensor("scr_on", [128, 16, 2048], BF16)

    es = ExitStack()
    with es:
        P = Prog(nc, es)
        A = Arena(nc, es, 207 * 1024)
        PS = [Tile(es.enter_context(nc.psum_tensor(f"ps{i}", [128, 512], F32))[:, :], f"ps{i}") for i in range(8)]
        for _p in PS:
            _p.b.excl = True
        PSR = Ring(PS)

        def mm(out, lhsT, rhs, start, stop, reads, writes):
            P.op('pe', lambda e: e.matmul(out, lhsT, rhs, start=start, stop=stop), reads, writes)

        def tr(out, in_, ident, reads, writes):
            P.op('pe', lambda e: e.transpose(out, in_, ident), reads, writes)

        def act(out, in_, func, reads, writes, **kw):
            P.op('act', lambda e: e.activation(out=out, in_=in_, func=func, **kw), reads, writes)

        def tt(eng, out, in0, in1, op, reads, writes):
            P.op(eng, lambda e: e.tensor_tensor(out=out, in0=in0, in1=in1, op=op), reads, writes)

        def ts(eng, out, in0, s1, s2, op0, op1, reads, writes):
            P.op(eng, lambda e: e.tensor_scalar(out=out, in0=in0, scalar1=s1, scalar2=s2, op0=op0, op1=op1), reads, writes)

        def stt(out, in0, scalar, in1, op0, op1, reads, writes):
            P.op('dve', lambda e: e.scalar_tensor_tensor(out=out, in0=in0, scalar=scalar, in1=in1, op0=op0, op1=op1), reads, writes)

        def cp(eng, out, in_, reads, writes):
            if eng == 'act':
                act(out, in_, ACT.Copy, reads, writes)
            else:
                P.op(eng, lambda e: e.tensor_copy(out=out, in_=in_), reads, writes)

        def memset(eng, out, val, writes):
            P.op(eng, lambda e: e.memset(out, val), (), writes)

        def dma(eng, out, in_, reads, writes, sem):
            P.dma(eng, lambda e: e.dma_start(out=out, in_=in_), reads, writes, sem)

        def recip(out, in_, reads, writes):
            P.op('dve', lambda e: e.reciprocal(out=out, in_=in_), reads, writes)

        def reduce_add(out, in_, reads, writes):
            P.op('dve', lambda e: e.tensor_reduce(out=out, in_=in_, axis=AX.X, op=ALU.add), reads, writes)

        def scan(out, a, b, init, reads, writes):
            P.op('dve', lambda e: e.tensor_tensor_scan(out=out, data0=a, data1=b, initial=init, op0=ALU.mult, op1=ALU.add),
                 reads, writes)

        hT = A.alloc([8, NCOL], F32, "hT")
        HB_ = [Tile(hT.ap, f"hT_b{i}") for i in range(NBLK + 1)]

        def hblk(c0):
            return HB_[NBLK] if c0 >= MOFF else HB_[c0 // BW]

        def hblks(c0, n):
            if c0 >= MOFF:
                return [HB_[NBLK]]
            return [HB_[i] for i in range(c0 // BW, (c0 + n + BW - 1) // BW)]
        par = A.alloc([NPAR], F32, "par")
        tab = A.alloc([NTABP], F32, "tab")
        identb = A.alloc([128], BF16, "identb")
        onesb = A.alloc([128], BF16, "onesb")
        epsc = A.alloc([1], F32, "epsc")

        def pcol(name, i=0, n=1):
            o, w = PAR[name]
            return par[:, o + i:o + i + n]

        def tcol(name, i=0, n=None):
            o, w = TAB[name]
            if n is None:
                n = w
            return tab[:, o + i:o + i + n]

        dma('sp', par.ap, par_d, [], [par], 'const')
        dma('sp', tab.ap, tab_d[:, 0:NTABP], [], [tab], 'const')
        dma('pool', identb.ap, tab2_d[:, TAB2['identb'][0]:TAB2['identb'][0] + 128], [], [identb], 'const2')
        memset('dve', onesb.ap, 1.0, [onesb])
        memset('dve', epsc.ap, EPS, [epsc])
        P.barrier()
        ident_f = tcol('ident')
        consts = [par, tab]

        def dump(name, ap, bufs):
            if name in dbg_d:
                dma('sp', dbg_d[name], ap, bufs, [Buf("dbg")], 'dbg')
                P.barrier()
            if stop == name:
                raise _Stop()

        def rmsnorm(src_ap_fn, nk, n, gain_name, out_t, out_ap_fn, srcbufs, F, H):
            ps = PSR.next()
            for k in range(nk):
                sq = H.next()
                act(sq[:, 0:n], src_ap_fn(k), ACT.Square, srcbufs, [sq])
                mm(ps[:, 0:n], onesb.ap, sq[:, 0:n], k == 0, k == nk - 1, [sq, onesb], [ps])
            sd = F.next()
            act(sd[:, 0:n], ps[:, 0:n], ACT.Ln, [ps, epsc], [sd], scale=1.0 / (nk * 128), bias=epsc[:, 0:1])
            rs = F.next()
            act(rs[:, 0:n], sd[:, 0:n], ACT.Exp, [sd], [rs], scale=-0.5)
            for k in range(nk):
                stt(out_ap_fn(k), src_ap_fn(k), pcol(gain_name, k), rs[:, 0:n], ALU.mult, ALU.mult,
                    srcbufs + [rs, par], [out_t])

        def load_w(dst_t, src_ap, kchunks, ncols, sem, eng='pool', dst_ap=None):
            if dst_ap is None:
                dst_ap = dst_t.ap
            step = max(1, 2048 // ncols)
            for k0 in range(0, kchunks, step):
                k1 = min(kchunks, k0 + step)
                dma(eng, dst_ap[:, k0:k1, 0:ncols], src_ap[k0 * 128:k1 * 128, :].rearrange("(k p) c -> p k c", p=128),
                    [], [dst_t], sem)

        try:
            m0 = A.mark()
            hTh = A.alloc([8, NHALO], F32, "hTh")
            rpermb = A.alloc([128], BF16, "rpermb")
            bonesb = A.alloc([128], BF16, "bonesb")
            mcur = A.alloc([512], BF16, "mcur")
            mprev = A.alloc([512], BF16, "mprev")
            mprev0 = A.alloc([512], BF16, "mprev0")
            mmeta = A.alloc([64], BF16, "mmeta")
            for nm, tl in [('rperm', rpermb), ('bones', bonesb), ('mcur', mcur), ('mprev', mprev),
                           ('mprev0', mprev0), ('mmeta', mmeta)]:
                o2, w2 = TAB2[nm]
                dma('pool', tl.ap, tab2_d[:, o2:o2 + w2], [], [tl], 'const2')
            w0 = A.alloc([8, 1792], BF16, "w_in0")
            wo0 = A.alloc([8, D], BF16, "w_out0")
            load_w(w0, w_in0[:, 0:1024], 8, 1024, 'w0', dst_ap=w0[:, :, 0:1024])
            load_w(w0, w_in0[:, 1536:1792], 8, 256, 'w0', dst_ap=w0[:, :, 1536:1792])
            for k in range(8):
                for s_ in range(2):
                    dma('pool', w0[:, k, 1024:1536].rearrange("p (j s d) -> p j s d", j=4, s=2, d=64)[:, :, s_, :],
                        w_in0[k * 128:(k + 1) * 128, 1024 + 256 * s_:1024 + 256 * s_ + 256].rearrange("p (j d) -> p j d", d=64),
                        [], [w0], 'w0')
            load_w(wo0, w_out0[0:512, :], 4, D, 'wo0')
            for s in range(2):
                dma('pool', wo0[64 * s:64 * s + 64, 4:8, :],
                    w_out0[512 + 256 * s:512 + 256 * s + 256, :].rearrange("(j d) c -> d j c", d=64), [], [wo0], 'wo0')
            wab = A.alloc([4, 128], BF16, "wab")
            wib = A.alloc([4, 128], BF16, "wib")
            memset('dve', wab.ap, 0.0, [wab])
            memset('dve', wib.ap, 0.0, [wib])
            for m in range(4):
                for a in range(2):
                    dma('pool', wab[64 * a:64 * a + 64, m, 64 * a:64 * a + 64], wa_d[2 * m + a], [], [wab], 'wab')
                    dma('pool', wib[64 * a:64 * a + 64, m, 64 * a:64 * a + 64], wi_d[2 * m + a], [], [wib], 'wab')
            lsc = A.alloc([8], F32, "lsc")
            ltmp = A.alloc([4], F32, "ltmp")
            act(ltmp.ap, pcol('lam', 0, 4), ACT.Exp, [par], [ltmp], scale=-1.0)
            act(ltmp.ap, ltmp.ap, ACT.Ln, [ltmp], [ltmp], bias=1.0)
            ts('dve', lsc[:, 0:4], ltmp.ap, -8.0, None, ALU.mult, ALU.bypass, [ltmp], [lsc])
            ts('dve', lsc[:, 4:8], ltmp.ap, -16.0, None, ALU.mult, ALU.bypass, [ltmp], [lsc])
            sexp = A.alloc([8], F32, "sexp")
            act(sexp.ap, pcol('sink', 0, 8), ACT.Exp, [par], [sexp])
            sink4 = A.alloc([2, 512], F32, "sink4")
            onesf = A.alloc([128], F32, "onesf")
            memset('dve', onesf.ap, 1.0, [onesf])
            for g in range(2):
                for j in range(4):
                    ts('dve', sink4[:, g, 128 * j:128 * j + 128], onesf.ap, sexp[:, 4 * g + j:4 * g + j + 1], None,
                       ALU.mult, ALU.bypass, [onesf, sexp], [sink4])
            mxs = A.mark()
            xs = Ring([A.alloc([D], F32, f"xs{i}") for i in range(2)])
            rows = [(c * 128, 128, ('own', c * 128)) for c in range(16)] + [(NOWN, NMETA, ('meta', 0)), (NOWN + NMETA, NHALO, ('halo', 0))]
            for ri, (r0, nr, (kind, c0)) in enumerate(rows):
                xt = xs.next()
                dma('sp', xt[0:nr, :], xin[r0:r0 + nr, :], [], [xt], f'xs{ri % 2}')
                for g in range(2):
                    ps = PSR.next()
                    for kk in range(4):
                        k = g * 4 + kk
                        tr(ps[:, kk * 128:kk * 128 + nr], xt[0:nr, k * 128:(k + 1) * 128], ident_f[0:nr, 0:nr], [xt, tab], [ps])
                    src = ps.ap.rearrange("p (a b) -> p a b", a=4)[:, :, 0:nr]
                    if kind == 'own':
                        cp('act' if g == 0 else 'dve', hT[:, g * 4:g * 4 + 4, c0:c0 + nr], src, [ps], [hblk(c0)])
                    elif kind == 'meta':
                        cp('act' if g == 0 else 'dve', hT[:, g * 4:g * 4 + 4, MOFF:MOFF + nr], src, [ps], [hblk(MOFF)])
                    else:
                        cp('act' if g == 0 else 'dve', hTh[:, g * 4:g * 4 + 4, 0:nr], src, [ps], [hTh])

            P.barrier()
            A.release(mxs)
            dump('x0', hT.ap, HB_)
            P.barrier()

            hcar = A.alloc([4], F32, "hcar")
            sumr = A.alloc([4], F32, "sumr")
            hmeta = A.alloc([4], F32, "hmeta")
            red = A.alloc([4], F32, "red")
            xhist = A.alloc([4, 3], F32, "xhist")
            pay = A.alloc([8], F32, "pay")
            xg = A.alloc([4, 8], F32, "xg")
            hin = A.alloc([4], F32, "hin")
            ftmp = A.alloc([8], F32, "ftmp")
            sk_ab = Buf("sk_ab")
            mp1 = A.mark()
            BW1 = 512
            hnr = [A.alloc([8, BW1], BF16, f"hn0{i}") for i in range(2)]
            Fn = Ring([A.alloc([BW1], F32, f"Fn{i}") for i in range(2)])
            Hn = Ring([A.alloc([BW1], BF16, f"Hn{i}") for i in range(2)])
            xre = A.alloc([4, 3 + BW1], F32, "xre")
            F = Ring([A.alloc([BW1], F32, f"F{i}") for i in range(20)])
            H = Ring([A.alloc([BW1], BF16, f"H{i}") for i in range(4)])

            def l0_src(kind, c0):
                if kind == 'halo':
                    return (lambda k: hTh[:, k, :]), [hTh]
                return (lambda k: hT[:, k, c0:c0 + (NMETA if kind == 'meta' else BW)]), [hblk(c0)]

            def proj_fm(wt, k_n, wcols_fn, rhs_fn, n, rbufs):
                ps = PSR.next()
                for k in range(k_n):
                    mm(ps[:, 0:n], wcols_fn(k), rhs_fn(k), k == 0, k == k_n - 1, [wt] + rbufs, [ps])
                return ps

            def lru_pass1(kind, c0, n, prefetch):
                for m in range(4):
                    ps = proj_fm(w0, 8, lambda k: w0[:, k, 128 * m:128 * m + 128], lambda k: hn[:, k, 0:n], n, [hn])
                    cp('act', xre[:, m, 3:3 + n], ps[:, 0:n], [ps], [xre])
                if kind == 'halo':
                    prefetch()
                    return
                xcs, xcbs, prs, pis, rrs, iis, aas, mus = [], [], [], [], [], [], [], []
                for m in range(4):
                    xc = F.next()
                    ts('dve', xc[:, 0:n], xre[:, m, 0:n], pcol('convw', 4 * m + 0), pcol('convb', m), ALU.mult, ALU.add,
                       [xre, par], [xc])
                    for j in range(1, 4):
                        stt(xc[:, 0:n], xre[:, m, j:j + n], pcol('convw', 4 * m + j), xc[:, 0:n], ALU.mult, ALU.add,
                            [xre, par, xc], [xc])
                    xcs.append(xc)
                for m in range(4):
                    xcb = H.next()
                    cp('act', xcb[:, 0:n], xcs[m][:, 0:n], [xcs[m]], [xcb])
                    xcbs.append(xcb)
                for m in range(4):
                    if m == 2:
                        for mm_ in range(2):
                            rr = F.next()
                            act(rr[:, 0:n], prs[mm_][:, 0:n], ACT.Sigmoid, [prs[mm_], par], [rr], bias=pcol('ba', mm_))
                            ii = F.next()
                            act(ii[:, 0:n], pis[mm_][:, 0:n], ACT.Sigmoid, [pis[mm_], par], [ii], bias=pcol('bi', mm_))
                            rrs.append(rr)
                            iis.append(ii)
                    pr = PSR.next()
                    mm(pr[:, 0:n], wab[:, m, :], xcbs[m][:, 0:n], True, True, [wab, xcbs[m]], [pr])
                    pi = PSR.next()
                    mm(pi[:, 0:n], wib[:, m, :], xcbs[m][:, 0:n], True, True, [wib, xcbs[m]], [pi])
                    prs.append(pr)
                    pis.append(pi)
                for mm_ in range(2, 4):
                    rr = F.next()
                    act(rr[:, 0:n], prs[mm_][:, 0:n], ACT.Sigmoid, [prs[mm_], par], [rr], bias=pcol('ba', mm_))
                    ii = F.next()
                    act(ii[:, 0:n], pis[mm_][:, 0:n], ACT.Sigmoid, [pis[mm_], par], [ii], bias=pcol('bi', mm_))
                    rrs.append(rr)
                    iis.append(ii)
                prefetch()
                if kind == 'own':
                    for m in range(4):
                        reduce_add(red[:, m:m + 1], rrs[m][:, 0:n], [rrs[m]], [red])
                    tt('dve', sumr.ap, sumr.ap, red.ap, ALU.add, [sumr, red], [sumr])
                for m in range(4):
                    aa = F.next()
                    act(aa[:, 0:n], rrs[m][:, 0:n], ACT.Exp, [rrs[m], lsc], [aa], scale=lsc[:, m:m + 1])
                    mu = F.next()
                    act(mu[:, 0:n], rrs[m][:, 0:n], ACT.Exp, [rrs[m], lsc], [mu], scale=lsc[:, 4 + m:5 + m])
                    aas.append(aa)
                    mus.append(mu)
                for m in range(4):
                    act(mus[m][:, 0:n], mus[m][:, 0:n], ACT.Sqrt, [mus[m]], [mus[m]], scale=-1.0, bias=1.0)
                for m in range(4):
                    mu = mus[m]
                    if kind == 'meta':
                        memset('dve', mu[:, 0:1], 1.0, [mu])
                    tt('dve', mu[:, 0:n], mu[:, 0:n], iis[m][:, 0:n], ALU.mult, [mu, iis[m]], [mu])
                    tt('dve', mu[:, 0:n], mu[:, 0:n], xcs[m][:, 0:n], ALU.mult, [mu, xcs[m]], [mu])
                    dma('sp', scr_ab.ap()[:, 0, m, c0:c0 + n], aas[m][:, 0:n], [aas[m]], [sk_ab], f'sa{m}')
                    dma('sp', scr_ab.ap()[:, 1, m, c0:c0 + n], mu[:, 0:n], [mu], [sk_ab], f'sb{m}')
                for m in range(4):
                    hh = F.next()
                    scan(hh[:, 0:n], aas[m][:, 0:n], mus[m][:, 0:n], hcar[:, m:m + 1], [aas[m], mus[m], hcar], [hh])
                    cp('dve', hcar[:, m:m + 1], hh[:, n - 1:n], [hh], [hcar])
                cp('dve', xhist.ap, xre[:, :, n:n + 3], [xre], [xhist])

            blocks_p1 = [('meta', MOFF, NMETA), ('halo', 0, NHALO)] + [('own', b * BW1, BW1) for b in range(NOWN // BW1)]
            memset('dve', sumr.ap, 0.0, [sumr])
            def norm_p1(i):
                if i >= len(blocks_p1):
                    return
                kind, c0, n = blocks_p1[i]
                ht = hnr[i % 2]
                if kind == 'halo':
                    srcf, sb = (lambda k: hTh[:, k, :]), [hTh]
                elif n == BW1:
                    srcf, sb = (lambda k: hT[:, k, c0:c0 + n]), [hblk(c0), hblk(c0 + BW)]
                else:
                    srcf, sb = (lambda k: hT[:, k, c0:c0 + n]), [hblk(c0)]
                rmsnorm(srcf, 8, n, 'g_ab', ht, lambda k: ht[:, k, 0:n], sb, Fn, Hn)
            norm_p1(0)
            for bi, (kind, c0, n) in enumerate(blocks_p1):
                hn = hnr[bi % 2]
                if kind == 'meta':
                    memset('dve', xre[:, :, 0:3], 0.0, [xre])
                    memset('dve', hcar.ap, 0.0, [hcar])
                elif kind == 'own':
                    cp('dve', xre[:, :, 0:3], xhist.ap, [xhist], [xre])
                lru_pass1(kind, c0, n, lambda: norm_p1(bi + 1))
                if kind == 'meta':
                    cp('dve', hmeta.ap, hcar.ap, [hcar], [hmeta])
                if kind == 'halo':
                    cp('dve', xhist.ap, xre[:, :, n:n + 3], [xre], [xhist])
                    memset('dve', hcar.ap, 0.0, [hcar])
            for m in range(4):
                act(pay[:, m:m + 1], sumr[:, m:m + 1], ACT.Exp, [sumr, lsc], [pay], scale=lsc[:, m:m + 1])
            cp('dve', pay[:, 4:8], hcar.ap, [hcar], [pay])
            cc1i = Buf("cc1i")
            cc1o = Buf("cc1o")
            dma('pool', cc1_in.ap(), pay.ap, [pay], [cc1i], 'cc1')
            P.op('pool', lambda e: e.collective_compute("AllGather", ALU.bypass, replica_groups=[[0, 1, 2, 3], [4, 5, 6, 7]],
                                                        ins=[cc1_in.ap().opt()], outs=[cc1_out.ap().opt()]), [cc1i], [cc1o])
            dma('pool', xg.ap, cc1_out.ap().rearrange("(r p) f -> p r f", p=128), [cc1o], [xg], 'cc1b')
            cp('dve', hin.ap, hmeta.ap, [hmeta], [hin])
            for c in range(4):
                ts('dve', ftmp[:, 0:4], xg[:, c, 0:4], tcol('fm', c, 1), tcol('fm', 4 + c, 1), ALU.mult, ALU.add, [xg, tab], [ftmp])
                ts('dve', ftmp[:, 4:8], xg[:, c, 4:8], tcol('fm', c, 1), None, ALU.mult, ALU.bypass, [xg, tab], [ftmp])
                tt('dve', hin.ap, hin.ap, ftmp[:, 0:4], ALU.mult, [hin, ftmp], [hin])
                tt('dve', hin.ap, hin.ap, ftmp[:, 4:8], ALU.add, [hin, ftmp], [hin])

            P.barrier()
            A.release(mp1)
            hnr = [A.alloc([8, BW], BF16, f"hn0b{i}") for i in range(2)]
            Fn = Ring([A.alloc([BW], F32, f"Fnb{i}") for i in range(2)])
            Hn = Ring([A.alloc([BW], BF16, f"Hnb{i}") for i in range(2)])
            yb = A.alloc([8, BW], BF16, "yb")
            QT = A.alloc([4, BW], BF16, "QT")
            KT = A.alloc([NOWN + NMETA + NHALO], BF16, "KT")
            Vt = A.alloc([18, 128], BF16, "Vt")
            atc = A.alloc([BW], F32, "atc")
            ats = A.alloc([BW], F32, "ats")
            F = Ring([A.alloc([BW], F32, f"F2{i}") for i in range(12)])
            H = Ring([A.alloc([BW], BF16, f"H2{i}") for i in range(8)])
            PT = Ring([A.alloc([512], BF16, f"PT{i}") for i in range(12)])
            FA = Ring([A.alloc([512], F32, f"FA{i}") for i in range(4)])
            abr = Ring([A.alloc([2, 4, BW], F32, f"ab{i}") for i in range(2)])
            dump('p1', hT.ap, HB_ + [hin])
            def qk_norm_rope_multi(items, n, ctab, stab, tabbufs):
                sqs, p2s, rss, qns, p3s = [], [], [], [], []
                for (ps, gname, out_ap, out_t) in items:
                    sq = H.next()
                    act(sq[:, 0:n], ps[:, 0:n], ACT.Square, [ps], [sq])
                    sqs.append(sq)
                for i, (ps, gname, out_ap, out_t) in enumerate(items):
                    p2 = PSR.next()
                    mm(p2[:, 0:n], bonesb.ap, sqs[i][:, 0:n], True, True, [bonesb, sqs[i]], [p2])
                    p2s.append(p2)
                for i in range(len(items)):
                    sd = F.next()
                    act(sd[:, 0:n], p2s[i][:, 0:n], ACT.Ln, [p2s[i], epsc], [sd], scale=1.0 / 64, bias=epsc[:, 0:1])
                    act(sd[:, 0:n], sd[:, 0:n], ACT.Exp, [sd], [sd], scale=-0.5)
                    rss.append(sd)
                for i, (ps, gname, out_ap, out_t) in enumerate(items):
                    qn = H.next()
                    stt(qn[:, 0:n], ps[:, 0:n], pcol(gname), rss[i][:, 0:n], ALU.mult, ALU.mult, [ps, rss[i], par], [qn])
                    qns.append(qn)
                for i in range(len(items)):
                    p3 = PSR.next()
                    mm(p3[:, 0:n], rpermb.ap, qns[i][:, 0:n], True, True, [rpermb, qns[i]], [p3])
                    p3s.append(p3)
                for i, (ps, gname, out_ap, out_t) in enumerate(items):
                    t1 = F.next()
                    tt('dve', t1[:, 0:n], p3s[i][:, 0:n], stab, ALU.mult, [p3s[i]] + tabbufs, [t1])
                    t2 = F.next()
                    tt('pool' if i % 2 == 0 else 'dve', t2[:, 0:n], qns[i][:, 0:n], ctab, ALU.mult, [qns[i]] + tabbufs, [t2])
                    tt('dve', out_ap, t1[:, 0:n], t2[:, 0:n], ALU.add, [t1, t2], [out_t])

            def qk_norm_rope(ps, n, gain_name, ctab, stab, tabbufs, out_ap, out_t):
                qk_norm_rope_multi([(ps, gain_name, out_ap, out_t)], n, ctab, stab, tabbufs)

            def attn_front(qn_cols, q0, kblocks):
                nq4 = 4 * qn_cols
                sps = []
                for g in range(2):
                    for (kc0, nk, vi, mk) in kblocks:
                        ps = PSR.next()
                        mm(ps.ap[0:nk, 0:nq4].rearrange("p (j q) -> p j q", j=4), KT[64 * g:64 * g + 64, kc0:kc0 + nk],
                           QT[64 * g:64 * g + 64, :, q0:q0 + qn_cols], True, True, [KT, QT], [ps])
                        sps.append((g, ps, nk, vi, mk))
                pts = {0: [], 1: []}
                for (g, ps, nk, vi, mk) in sps:
                    pt = PT.next()
                    act(pt[0:nk, 0:nq4], ps[0:nk, 0:nq4], ACT.Exp, [ps], [pt], scale=0.125)
                    pts[g].append((pt, nk, vi, mk))
                for g in range(2):
                    for (pt, nk, vi, mk) in pts[g]:
                        if mk is not None:
                            tt('pool' if g == 0 else 'dve', pt[0:nk, 0:nq4], pt[0:nk, 0:nq4], mk[0:nk, 0:nq4], ALU.mult, [pt, mk], [pt])
                return pts

            def attn_back(qn_cols, pts, ycols0):
                nq4 = 4 * qn_cols
                outs = []
                for g in range(2):
                    pv = PSR.next()
                    pdn = PSR.next()
                    for i, (pt, nk, vi, mk) in enumerate(pts[g]):
                        mm(pv[:, 0:nq4], Vt[0:nk, vi, :], pt[0:nk, 0:nq4], i == 0, i == len(pts[g]) - 1, [Vt, pt], [pv])
                    for i, (pt, nk, vi, mk) in enumerate(pts[g]):
                        mm(pdn[:, 0:nq4], onesb[0:nk, :], pt[0:nk, 0:nq4], i == 0, False, [onesb, pt], [pdn])
                    mm(pdn.ap[:, 0:nq4].rearrange("p (j q) -> p j q", j=4), onesf[0:1, :],
                       sink4[0:1, g, :].rearrange("p (j q) -> p j q", j=4)[:, :, 0:qn_cols], False, True, [onesf, sink4], [pdn])
                    outs.append((pv, pdn))
                for g in range(2):
                    pv, pdn = outs[g]
                    dn = FA.next()
                    act(dn[:, 0:nq4], pdn[:, 0:nq4], ACT.Ln, [pdn], [dn])
                    rd = FA.next()
                    act(rd[:, 0:nq4], dn[:, 0:nq4], ACT.Exp, [dn], [rd], scale=-1.0)
                    tt('dve', yb[64 * g:64 * g + 64, 4:8, ycols0:ycols0 + qn_cols],
                       pv.ap[64 * g:64 * g + 64, 0:nq4].rearrange("p (j q) -> p j q", j=4),
                       rd.ap[64 * g:64 * g + 64, 0:nq4].rearrange("p (j q) -> p j q", j=4), ALU.mult, [pv, rd], [yb])

            def attention(qn_cols, q0, kblocks, ycols0):
                attn_back(qn_cols, attn_front(qn_cols, q0, kblocks), ycols0)

            wq = w0[:, :, 1024:1536].rearrange("p k (j m) -> p k j m", j=4, m=128)
            blocks0 = [('meta', MOFF, NMETA), ('halo', 0, NHALO)] + [('own', b * BW, BW) for b in range(NBLK)]
            def norm_p2(i):
                if i >= len(blocks0):
                    return
                kind, c0, n = blocks0[i]
                ht = hnr[i % 2]
                srcf, sb = l0_src(kind, c0)
                rmsnorm(srcf, 8, n, 'g_ab', ht, lambda k: ht[:, k, 0:n], sb, Fn, Hn)
            norm_p2(0)
            for bi, (kind, c0, n) in enumerate(blocks0):
                hn = hnr[bi % 2]
                if kind == 'own':
                    dma('sp', atc[:, 0:n], attcs_d[0, :, c0:c0 + n], [], [atc], 'atc')
                    dma('sp', ats[:, 0:n], attcs_d[1, :, c0:c0 + n], [], [ats], 'atc')
                    ctab, stab, tbufs = atc[:, 0:n], ats[:, 0:n], [atc, ats]
                elif kind == 'meta':
                    ctab, stab, tbufs = tcol('attc_m'), tcol('atts_m'), [tab]
                else:
                    ctab, stab, tbufs = tcol('attc_h'), tcol('atts_h'), [tab]
                kcol0 = {'own': c0, 'meta': NOWN, 'halo': NOWN + NMETA}[kind]
                ps = proj_fm(w0, 8, lambda k: w0[:, k, 1536:1664], lambda k: hn[:, k, 0:n], n, [hn])
                qk_norm_rope(ps, n, 'kg', ctab, stab, tbufs, KT[:, kcol0:kcol0 + n], KT)
                nch = max(1, n // 128)
                for ci in range(nch):
                    nt = min(128, n)
                    vi = {'own': c0 // 128 + ci, 'meta': 16, 'halo': 17}[kind]
                    ps = PSR.next()
                    for k in range(8):
                        mm(ps[0:nt, 0:128], hn[:, k, ci * 128:ci * 128 + nt], w0[:, k, 1664:1792], k == 0, k == 7, [hn, w0], [ps])
                    cp('act', Vt[0:nt, vi, :], ps[0:nt, 0:128], [ps], [Vt])
                if kind == 'halo':
                    cp('dve', hcar.ap, hin.ap, [hin], [hcar])
                    norm_p2(bi + 1)
                    continue
                if kind == 'meta':
                    memset('dve', hcar.ap, 0.0, [hcar])
                absem = f'lab{abr.i % 2}'
                ab = abr.next()
                dma('sp', ab[:, :, :, 0:n], scr_ab.ap()[:, :, :, c0:c0 + n], [sk_ab], [ab], absem)
                for m in range(4):
                    ps = proj_fm(w0, 8, lambda k: w0[:, k, 512 + 128 * m:512 + 128 * m + 128], lambda k: hn[:, k, 0:n], n, [hn])
                    act(yb[:, m, 0:n], ps[:, 0:n], ACT.Gelu_apprx_tanh, [ps], [yb])
                for m in range(4):
                    hh = F.next()
                    scan(hh[:, 0:n], ab[:, 0, m, 0:n], ab[:, 1, m, 0:n], hcar[:, m:m + 1], [ab, hcar], [hh])
                    cp('dve', hcar[:, m:m + 1], hh[:, n - 1:n], [hh], [hcar])
                    tt('pool', yb[:, m, 0:n], yb[:, m, 0:n], hh[:, 0:n], ALU.mult, [yb, hh], [yb])
                qitems = []
                for j in range(4):
                    ps = proj_fm(w0, 8, lambda k: wq[:, k, j], lambda k: hn[:, k, 0:n], n, [hn])
                    qitems.append((ps, 'qg', QT[:, j, 0:n], QT))
                qk_norm_rope_multi(qitems, n, ctab, stab, tbufs)
                norm_p2(bi + 1)
                if kind == 'meta':
                    attention(NMETA, 0, [(NOWN, NMETA, 16, mmeta)], 0)
                else:
                    fronts = []
                    for ci in range(n // 128):
                        ch = c0 // 128 + ci
                        if ch == 0:
                            prevb = (NOWN + NMETA, 128, 17, mprev0)
                        else:
                            prevb = ((ch - 1) * 128, 128, ch - 1, mprev)
                        fronts.append(attn_front(128, ci * 128, [(NOWN, NMETA, 16, None), prevb, (ch * 128, 128, ch, mcur)]))
                    for ci in range(n // 128):
                        attn_back(128, fronts[ci], ci * 128)
                for mo in range(8):
                    ps = proj_fm(wo0, 8, lambda k: wo0[:, k, 128 * mo:128 * mo + 128], lambda k: yb[:, k, 0:n], n, [yb])
                    tt('dve', hT[:, mo, c0:c0 + n], hT[:, mo, c0:c0 + n], ps[:, 0:n], ALU.add, [hblk(c0), ps], [hblk(c0)])
            P.barrier()
            A.release(m0)

            dump('h1', hT.ap, HB_)

            FCB = [(0, 512), (512, 512), (1024, 512), (1536, 512), (MOFF, NMETA)]

            def ffn(layer, gain_name, with_meta):
                m1 = A.mark()
                cbs = FCB if with_meta else FCB[:4]
                ncol = NCOL if with_meta else NOWN
                hnf = A.alloc([8, NCOL], BF16, "hnf")
                Ff = Ring([A.alloc([512], F32, f"Ff{i}") for i in range(2)])
                Hf = Ring([A.alloc([512], BF16, f"Hf{i}") for i in range(2)])
                hnfB = {c0: Tile(hnf.ap, f"hnf_{c0}") for (c0, n) in cbs}
                normed = set()

                def need_norm(c0, n):
                    if c0 in normed:
                        return
                    normed.add(c0)
                    rmsnorm(lambda k: hT[:, k, c0:c0 + n], 8, n, gain_name, hnfB[c0], lambda k: hnf[:, k, c0:c0 + n],
                            [hblk(c0)] if n < 512 else [hblk(c0), hblk(c0 + BW)], Ff, Hf)
                for (c0_, n_) in cbs:
                    need_norm(c0_, n_)
                actb = Ring([A.alloc([4, NCOL], BF16, f"actb{i}") for i in range(2)])
                wg = Ring([A.alloc([8, 512], BF16, f"wg{i}") for i in range(2)])
                wu = Ring([A.alloc([8, 512], BF16, f"wu{i}") for i in range(2)])
                wd = Ring([A.alloc([4, D], BF16, f"wd{i}") for i in range(2)])
                sgr = Ring([A.alloc([512], BF16, f"sg{i}") for i in range(2)])
                for q in range(6):
                    nt = 4 if q < 5 else 2
                    wgt, wut, wdt, ab = wg.next(), wu.next(), wd.next(), actb.next()
                    load_w(wgt, w_gu[layer, :, 512 * q:512 * q + 128 * nt], 8, 128 * nt, f'wg{q % 2}')
                    load_w(wut, w_gu[layer, :, DFF + 512 * q:DFF + 512 * q + 128 * nt], 8, 128 * nt, f'wu{q % 2}')
                    for t0 in range(0, nt, 2):
                        dma('pool', wdt[:, t0:t0 + 2, :], w_dn[layer, 512 * q + 128 * t0:512 * q + 128 * (t0 + 2), :].rearrange("(k p) c -> p k c", p=128),
                            [], [wdt], f'wd{q % 2}')
                    for t in range(nt):
                        for (c0, n) in cbs:
                            need_norm(c0, n)
                            pg = PSR.next()
                            for k in range(8):
                                mm(pg[:, 0:n], wgt[:, k, 128 * t:128 * t + 128], hnf[:, k, c0:c0 + n], k == 0, k == 7, [wgt, hnfB[c0]], [pg])
                            pu = PSR.next()
                            for k in range(8):
                                mm(pu[:, 0:n], wut[:, k, 128 * t:128 * t + 128], hnf[:, k, c0:c0 + n], k == 0, k == 7, [wut, hnfB[c0]], [pu])
                            sg = sgr.next()
                            act(sg[:, 0:n], pg[:, 0:n], ACT.Silu, [pg], [sg])
                            tt('dve', ab[:, t, c0:c0 + n], sg[:, 0:n], pu[:, 0:n], ALU.mult, [sg, pu], [ab])
                    for mo in range(8):
                        for (c0, n) in cbs:
                            ps = PSR.next()
                            for t in range(nt):
                                mm(ps[:, 0:n], wdt[:, t, 128 * mo:128 * mo + 128], ab[:, t, c0:c0 + n], t == 0, t == nt - 1, [wdt, ab], [ps])
                            hb = [hblk(c0)] if n < 512 else [hblk(c0), hblk(c0 + BW)]
                            tt('dve', hT[:, mo, c0:c0 + n], hT[:, mo, c0:c0 + n], ps[:, 0:n], ALU.add, hb + [ps], hb)
                P.barrier()
                A.release(m1)

            ffn(0, 'g_ffn0', True)
            dump('h2', hT.ap, HB_)

            m2 = A.mark()
            Sacc = A.alloc([8, 512], F32, "Sacc")
            mA = A.mark()
            sk_sm = Buf("sk_sm")
            mA2 = A.mark()
            wk = A.alloc([8, 1024], BF16, "wk")
            wv = A.alloc([8, 2048], BF16, "wv")
            load_w(wk, w_in1[:, 1024:2048], 8, 1024, 'wk')
            load_w(wv, w_in1[:, 2048:4096], 8, 2048, 'wv')
            BWA = 512
            hnr = [A.alloc([8, BWA], BF16, f"hn1{i}") for i in range(2)]
            Fn = Ring([A.alloc([BWA], F32, f"FnA{i}") for i in range(2)])
            Hn = Ring([A.alloc([BWA], BF16, f"HnA{i}") for i in range(2)])
            F = Ring([A.alloc([BWA], F32, f"G{i}") for i in range(6)])
            smst = Ring([A.alloc([512], F32, f"smst{i}") for i in range(2)])
            H = Ring([A.alloc([BWA], BF16, f"I{i}") for i in range(4)])
            KTb = A.alloc([8, BWA], BF16, "KTb")
            rc = A.alloc([BWA], F32, "rc")
            rs_ = A.alloc([BWA], F32, "rs")
            Kd = Ring([A.alloc([1024], BF16, f"Kd{i}") for i in range(2)])
            Kdf = Ring([A.alloc([1024], BF16, f"Kdf{i}") for i in range(2)])
            Vc = Ring([A.alloc([2048], BF16, f"Vc{i}") for i in range(2)])
            memset('pool', Sacc.ap, 0.0, [Sacc])
            sk_kt, sk_kd, sk_v, sk_on = Buf("skt"), Buf("skd"), Buf("sv"), Buf("son")

            def rope_ret(psa, psb, n, ctab, stab, tbufs, outa, outb, out_t):
                x1 = F.next()
                cp('act', x1[:, 0:n], psa[:, 0:n], [psa], [x1])
                x2 = F.next()
                cp('act', x2[:, 0:n], psb[:, 0:n], [psb], [x2])
                t1, t2, t3, t4 = F.next(), F.next(), F.next(), F.next()
                tt('dve', t1[:, 0:n], x1[:, 0:n], ctab, ALU.mult, [x1] + tbufs, [t1])
                tt('dve', t2[:, 0:n], x2[:, 0:n], stab, ALU.mult, [x2] + tbufs, [t2])
                tt('dve', t3[:, 0:n], x2[:, 0:n], ctab, ALU.mult, [x2] + tbufs, [t3])
                tt('dve', t4[:, 0:n], x1[:, 0:n], stab, ALU.mult, [x1] + tbufs, [t4])
                tt('dve', outa, t1[:, 0:n], t2[:, 0:n], ALU.subtract, [t1, t2], [out_t])
                tt('dve', outb, t3[:, 0:n], t4[:, 0:n], ALU.add, [t3, t4], [out_t])

            blocks1 = [('meta', MOFF, NMETA)] + [('own', b * BWA, BWA) for b in range(NOWN // BWA)]
            def norm_A(i):
                if i >= len(blocks1):
                    return
                kind, c0, n = blocks1[i]
                ht = hnr[i % 2]
                rmsnorm(lambda k: hT[:, k, c0:c0 + n], 8, n, 'g_ret', ht, lambda k: ht[:, k, 0:n], hblks(c0, n), Fn, Hn)
            norm_A(0)
            for bi, (kind, c0, n) in enumerate(blocks1):
                hn = hnr[bi % 2]
                if kind == 'own':
                    dma('sp', rc[:, 0:n], retcs_d[0, :, c0:c0 + n], [], [rc], 'rc')
                    dma('sp', rs_[:, 0:n], retcs_d[1, :, c0:c0 + n], [], [rs_], 'rc')
                    ctab, stab, tbufs = rc[:, 0:n], rs_[:, 0:n], [rc, rs_]
                else:
                    ctab, stab, tbufs = tcol('retc_m'), tcol('rets_m'), [tab]
                for h in range(4):
                    pa = proj_fm(wk, 8, lambda k: wk[:, k, 256 * h:256 * h + 128], lambda k: hn[:, k, 0:n], n, [hn])
                    pb = proj_fm(wk, 8, lambda k: wk[:, k, 256 * h + 128:256 * h + 256], lambda k: hn[:, k, 0:n], n, [hn])
                    rope_ret(pa, pb, n, ctab, stab, tbufs, KTb[:, 2 * h, 0:n], KTb[:, 2 * h + 1, 0:n], KTb)
                if kind == 'own':
                    dma('sp', scr_kt.ap()[:, :, c0:c0 + n], KTb[:, :, 0:n], [KTb], [sk_kt], 'skt')
                norm_A(bi + 1)
                nt = min(128, n)
                for ci in range(max(1, n // 128)):
                    ch = c0 // 128 + ci
                    ps = PSR.next()
                    psb16 = ps.ap.bitcast(BF16)
                    for f in range(8):
                        tr(psb16[0:nt, 128 * f:128 * f + 128], KTb[:, f, ci * 128:ci * 128 + nt], identb.ap, [KTb, identb], [ps])
                    kd, kdf = Kd.next(), Kdf.next()
                    for h in range(4):
                        if kind == 'own':
                            act(kd[0:nt, 256 * h:256 * h + 256], psb16[0:nt, 256 * h:256 * h + 256], ACT.Copy, [ps, tab], [kd],
                                scale=tcol('dk', h, 1)[0:nt])
                            ts('dve', kdf[0:nt, 256 * h:256 * h + 256], psb16[0:nt, 256 * h:256 * h + 256],
                               tcol('dkf', 4 * ch + h, 1)[0:nt], None, ALU.mult, ALU.bypass, [ps, tab], [kdf])
                        else:
                            ts('dve', kdf[0:nt, 256 * h:256 * h + 256], psb16[0:nt, 256 * h:256 * h + 256],
                               tcol('dkm', h, 1)[0:nt], None, ALU.mult, ALU.bypass, [ps, tab], [kdf])
                    vc = Vc.next()
                    for h in range(4):
                        ps = PSR.next()
                        for k in range(8):
                            mm(ps[0:nt, :], hn[:, k, ci * 128:ci * 128 + nt], wv[:, k, 512 * h:512 * h + 512], k == 0, k == 7, [hn, wv], [ps])
                        cp('act' if h % 2 == 0 else 'dve', vc[0:nt, 512 * h:512 * h + 512], ps[0:nt, :], [ps], [vc])
                    if kind == 'own':
                        dma('sp', scr_kd.ap()[:, ch, :], kd.ap, [kd], [sk_kd], f'skd{ch % 2}')
                        dma('sp', scr_v.ap()[:, ch, :], vc.ap, [vc], [sk_v], f'sv{ch % 2}')
                    for h in range(4):
                        for dtl in range(2):
                            ps = PSR.next()
                            mm(ps[:, :], kdf[0:nt, 256 * h + 128 * dtl:256 * h + 128 * dtl + 128], vc[0:nt, 512 * h:512 * h + 512],
                               True, True, [kdf, vc], [ps])
                            if kind == 'own':
                                tt('dve', Sacc[:, 2 * h + dtl, :], Sacc[:, 2 * h + dtl, :], ps[:, :], ALU.add, [Sacc, ps], [Sacc])
                            else:
                                st_ = smst.next()
                                cp('dve', st_.ap, ps[:, :], [ps], [st_])
                                dma('sp', scr_sm.ap()[:, 2 * h + dtl, :], st_.ap, [st_], [sk_sm], f'ssm{(2 * h + dtl) % 2}')
            dump('pa', hT.ap, HB_ + [Sacc])
            cc2i, cc2o = Buf("cc2i"), Buf("cc2o")
            P.barrier()
            A.release(mA2)
            wqr = A.alloc([8, 1024], BF16, "wqr")
            mB = A.mark()
            Sbf = A.alloc([8, 512], BF16, "Sbf")
            cp('act', Sbf[:, 0:4, :], Sacc[:, 0:4, :], [Sacc], [Sbf])
            cp('dve', Sbf[:, 4:8, :], Sacc[:, 4:8, :], [Sacc], [Sbf])
            dma('sp', cc2_in.ap().rearrange("g p (t b) -> p g t b", t=2), Sbf.ap.rearrange("p (g t) b -> p g t b", t=2),
                [Sbf], [cc2i], 'cc2')
            cc2og = [Buf(f"cc2o{g}") for g in range(4)]
            for g4 in range(4):
                def _cc(e, g4=g4):
                    return e.collective_compute("AllGather", ALU.bypass, replica_groups=[[0, 1, 2, 3], [4, 5, 6, 7]],
                                                ins=[cc2_in.ap()[g4].opt()], outs=[cc2_out.ap()[g4].opt()])
                P.op('pool', _cc, [cc2i], [cc2og[g4]])
            load_w(wqr, w_in1[:, 0:1024], 8, 1024, 'wq1')
            Sin = Sacc
            sgm = A.alloc([8, 512], F32, "sgm")
            sgr = [A.alloc([4, 2, 512], BF16, f"sgr{i}") for i in range(2)]
            dma('sp', sgm.ap, scr_sm.ap(), [sk_sm], [sgm], 'cc2m')
            for h in range(4):
                for dtl in range(2):
                    f = 2 * h + dtl
                    ts('dve', Sin[:, f, :], sgm[:, f, :], tcol('cf', 16 + h, 1), None, ALU.mult, ALU.bypass, [sgm, tab], [Sin])
            for g4 in range(4):
                sgt = sgr[g4 % 2]
                dma('sp', sgt.ap, cc2_out.ap()[g4].rearrange("(r p) (t b) -> p r t b", p=128, t=2), [cc2og[g4]], [sgt], f'cc2b{g4 % 2}')
                for c in range(4):
                    for dtl in range(2):
                        f = 2 * g4 + dtl
                        stt(Sin[:, f, :], sgt[:, c, dtl, :], tcol('cf', 4 * c + g4, 1), Sin[:, f, :], ALU.mult, ALU.add, [sgt, tab, Sin], [Sin])
            P.barrier()
            A.release(mB)
            Sf = Sacc
            dump('ex2', hT.ap, HB_ + [Sacc])

            Sb = A.alloc([8, 512], BF16, "Sb")
            BWB = 512
            tabB = A.alloc([512], F32, "tabB")
            dma('sp', tabB.ap, tab_d[:, NTABP:NTABP + 512], [], [tabB], 'tabB')
            dqx = A.alloc([4, BWB], BF16, "dqx")
            hnr = [A.alloc([8, BWB], BF16, f"hn2{i}") for i in range(2)]
            Fn = Ring([A.alloc([BWB], F32, f"FnB{i}") for i in range(2)])
            Hn = Ring([A.alloc([BWB], BF16, f"HnB{i}") for i in range(2)])
            F = Ring([A.alloc([BWB], F32, f"J{i}") for i in range(6)])
            dq_tmp = F.next()
            dma('sp', dq_tmp.ap, tab_d[:, NTABP + 512:NTABP + 1024], [], [dq_tmp], 'tabB2')
            for h in range(4):
                for ci in range(BWB // 128):
                    cp('dve', dqx[:, h, 128 * ci:128 * ci + 128], dq_tmp[:, 128 * h:128 * h + 128], [dq_tmp], [dqx])
            QTb = A.alloc([8, BWB], BF16, "QTb")
            QDb = A.alloc([8, BWB], BF16, "QDb")
            rc = A.alloc([BWB], F32, "rc2")
            rs_ = A.alloc([BWB], F32, "rs2")
            KTc = Ring([A.alloc([8, 128], BF16, f"KTc{i}") for i in range(2)])
            Kdc = Ring([A.alloc([1024], BF16, f"Kdc{i}") for i in range(2)])
            Vcc = Ring([A.alloc([2048], BF16, f"Vcc{i}") for i in range(2)])
            PTr = Ring([A.alloc([512], BF16, f"PTr{i}") for i in range(2)])
            onr = Ring([A.alloc([2048], BF16, f"on{i}") for i in range(2)])
            junk = Ring([A.alloc([512], BF16, f"junk{i}") for i in range(4)])
            osbr = Ring([A.alloc([512], F32, f"osb{i}") for i in range(4)])
            ssq = A.alloc([8], F32, "ssq")
            SfB = [Buf(f"Sf{f}") for f in range(8)]
            SbB = [Buf(f"Sb{f}") for f in range(8)]
            cp('act', Sb.ap, Sf.ap, [Sf], [Sb] + SbB)
            DH = [float(np.exp(128.0 * LOGG[h])) for h in range(4)]
            def norm_B(i):
                if i >= NOWN // BWB:
                    return
                c0_, n_ = i * BWB, BWB
                ht = hnr[i % 2]
                rmsnorm(lambda k: hT[:, k, c0_:c0_ + n_], 8, n_, 'g_ret', ht, lambda k: ht[:, k, 0:n_], hblks(c0_, n_), Fn, Hn)
            norm_B(0)
            for b in range(NOWN // BWB):
                c0, n = b * BWB, BWB
                hn = hnr[b % 2]
                dma('sp', rc[:, 0:n], retcs_d[0, :, c0:c0 + n], [], [rc], 'rc')
                dma('sp', rs_[:, 0:n], retcs_d[1, :, c0:c0 + n], [], [rs_], 'rc')
                for h in range(4):
                    pa = proj_fm(wqr, 8, lambda k: wqr[:, k, 256 * h:256 * h + 128], lambda k: hn[:, k, 0:n], n, [hn])
                    pb = proj_fm(wqr, 8, lambda k: wqr[:, k, 256 * h + 128:256 * h + 256], lambda k: hn[:, k, 0:n], n, [hn])
                    rope_ret(pa, pb, n, rc[:, 0:n], rs_[:, 0:n], [rc, rs_], QTb[:, 2 * h, 0:n], QTb[:, 2 * h + 1, 0:n], QTb)
                    for dtl in range(2):
                        tt('dve', QDb[:, 2 * h + dtl, :].rearrange("p (c i) -> p c i", i=128),
                           QTb[:, 2 * h + dtl, :].rearrange("p (c i) -> p c i", i=128),
                           dqx[:, h, :].rearrange("p (c i) -> p c i", i=128), ALU.mult, [QTb, dqx], [QDb])
                def chunk_front(ci):
                    ch = c0 // 128 + ci
                    cs = slice(128 * ci, 128 * ci + 128)
                    ktc, kdc, vcc = KTc.next(), Kdc.next(), Vcc.next()
                    dma('sp', ktc.ap, scr_kt.ap()[:, :, ch * 128:ch * 128 + 128], [sk_kt], [ktc], f'lk{ch % 2}')
                    dma('sp', kdc.ap, scr_kd.ap()[:, ch, :], [sk_kd], [kdc], f'lkd{ch % 2}')
                    dma('sp', vcc.ap, scr_v.ap()[:, ch, :], [sk_v], [vcc], f'lv{ch % 2}')
                    pA = PSR.next()
                    for h in range(4):
                        for dtl in range(2):
                            mm(pA[:, 128 * h:128 * h + 128], ktc[:, 2 * h + dtl, :], QTb[:, 2 * h + dtl, cs], dtl == 0, dtl == 1, [ktc, QTb], [pA])
                    pt = PTr.next()
                    tt('dve', pt.ap, pA.ap, tabB[:, 0:512], ALU.mult, [pA, tabB], [pt])
                    return (ch, cs, kdc, vcc, pt)

                nxt = chunk_front(0)
                for ci in range(BWB // 128):
                    ch, cs, kdc, vcc, pt = nxt
                    on = onr.next()
                    osbs = []
                    for h in range(4):
                        po = PSR.next()
                        mm(po[:, :], pt[:, 128 * h:128 * h + 128], vcc[:, 512 * h:512 * h + 512], True, False, [pt, vcc], [po])
                        for dtl in range(2):
                            mm(po[:, :], QDb[:, 2 * h + dtl, cs], Sb[:, 2 * h + dtl, :], False, dtl == 1, [QDb, SbB[2 * h + dtl]], [po])
                        osb = osbr.next()
                        cp('act', osb.ap, po.ap, [po], [osb])
                        osbs.append(osb)
                    for h in range(4):
                        for dtl in range(2):
                            f = 2 * h + dtl
                            pS = PSR.next()
                            mm(pS[:, :], kdc[:, 256 * h + 128 * dtl:256 * h + 128 * dtl + 128], vcc[:, 512 * h:512 * h + 512], True, True, [kdc, vcc], [pS])
                            stt(Sf[:, f, :], Sf[:, f, :], DH[h], pS.ap, ALU.mult, ALU.add, [SfB[f], pS], [SfB[f]])
                            cp('act' if dtl == 0 else 'dve', Sb[:, f, :], Sf[:, f, :], [SfB[f]], [SbB[f]])
                    if ci + 1 < BWB // 128:
                        nxt = chunk_front(ci + 1)
                    if ci == 1:
                        norm_B(b + 1)
                    for h in range(4):
                        jk = junk.next()
                        P.op('dve', (lambda e, jk=jk, ob=osbs[h], h=h: e.scalar_tensor_tensor(
                            out=jk.ap, in0=ob.ap, scalar=1.0, in1=ob.ap, op0=ALU.mult, op1=ALU.mult,
                            accum_out=ssq[:, h:h + 1])), [osbs[h]], [jk, ssq])
                    act(ssq[:, 4:8], ssq[:, 0:4], ACT.Sqrt, [ssq, epsc], [ssq], scale=1.0 / 512, bias=epsc[:, 0:1])
                    recip(ssq[:, 4:8], ssq[:, 4:8], [ssq], [ssq])
                    for h in range(4):
                        act(on[:, 512 * h:512 * h + 512], osbs[h].ap, ACT.Copy, [osbs[h], ssq], [on], scale=ssq[:, 4 + h:5 + h])
                    dma('act', scr_on.ap()[:, ch, :], on.ap, [on], [sk_on], f'son{ch % 2}')
            P.barrier()
            A.release(m2)
            dump('h2b', hT.ap, HB_)

            wgr = A.alloc([8, 2048], BF16, "wgr")
            wor = A.alloc([16, D], BF16, "wor")
            load_w(wgr, w_in1[:, 4096:6144], 8, 2048, 'wg1')
            load_w(wor, w_out1, 16, D, 'wo1')
            BW2 = 512
            hnr = [A.alloc([8, BW2], BF16, f"hn3{i}") for i in range(2)]
            Fn = Ring([A.alloc([BW2], F32, f"FnC{i}") for i in range(2)])
            Hn = Ring([A.alloc([BW2], BF16, f"HnC{i}") for i in range(2)])
            sgT = A.alloc([16, BW2], BF16, "sgT")
            gT = A.alloc([16, BW2], BF16, "gT")
            onb = Ring([A.alloc([2048], BF16, f"onb{i}") for i in range(2)])

            def norm_B2(i):
                if i >= NOWN // BW2:
                    return
                c0_, n_ = i * BW2, BW2
                ht = hnr[i % 2]
                rmsnorm(lambda k: hT[:, k, c0_:c0_ + n_], 8, n_, 'g_ret', ht, lambda k: ht[:, k, 0:n_], hblks(c0_, n_), Fn, Hn)
            norm_B2(0)
            for b in range(NOWN // BW2):
                c0, n = b * BW2, BW2
                hn = hnr[b % 2]
                for ft in range(16):
                    ps = proj_fm(wgr, 8, lambda k: wgr[:, k, 128 * ft:128 * ft + 128], lambda k: hn[:, k, 0:n], n, [hn])
                    act(sgT[:, ft, 0:n], ps[:, 0:n], ACT.Silu, [ps], [sgT])
                norm_B2(b + 1)
                for ci in range(BW2 // 128):
                    ch = c0 // 128 + ci
                    ob = onb.next()
                    dma('sp', ob.ap, scr_on.ap()[:, ch, :], [sk_on], [ob], f'lon{ch % 2}')
                    for g2 in range(2):
                        ps = PSR.next()
                        psb16 = ps.ap.bitcast(BF16)
                        for f8 in range(8):
                            ft = 8 * g2 + f8
                            tr(psb16[:, 128 * f8:128 * f8 + 128], ob[:, 128 * ft:128 * ft + 128], identb.ap, [ob, identb], [ps])
                        tt('dve', gT[:, 8 * g2:8 * g2 + 8, 128 * ci:128 * ci + 128], psb16.rearrange("p (a b) -> p a b", a=8),
                           sgT[:, 8 * g2:8 * g2 + 8, 128 * ci:128 * ci + 128], ALU.mult, [ps, sgT], [gT])
                for mo in range(8):
                    ps = proj_fm(wor, 16, lambda k: wor[:, k, 128 * mo:128 * mo + 128], lambda k: gT[:, k, 0:n], n, [gT])
                    tt('dve', hT[:, mo, c0:c0 + n], hT[:, mo, c0:c0 + n], ps[:, 0:n], ALU.add, hblks(c0, n) + [ps], hblks(c0, n))
            P.barrier()
            A.release(m2)
            dump('h3', hT.ap, HB_)

            ffn(1, 'g_ffn1', False)

            ot = Ring([A.alloc([D], F32, f"ot{i}") for i in range(2)])
            outb = Buf("out")
            for ch in range(16):
                o = ot.next()
                for g in range(2):
                    ps = PSR.next()
                    for kk in range(4):
                        k = 4 * g + kk
                        tr(ps[:, 128 * kk:128 * kk + 128], hT[:, k, ch * 128:ch * 128 + 128], ident_f, [hblk(ch * 128), tab], [ps])
                    cp('act' if g == 0 else 'dve', o[:, 512 * g:512 * g + 512], ps.ap, [ps], [o])
                dma('sp', out_d[ch * 128:ch * 128 + 128, :], o.ap, [o], [outb], f'out{ch % 2}')

        except _Stop:
            pass
        P.barrier()
        P.build()
    return nc


def _core_inputs(c, inp):
    b, r = c // 4, c % 4
    x = inp['x']
    meta = inp['meta_tokens']
    own = x[b, NOWN * r:NOWN * (r + 1)]
    if r > 0:
        halo = x[b, NOWN * r - NHALO:NOWN * r]
        hpos = 16 + NOWN * r - NHALO + np.arange(NHALO)
    else:
        halo = np.concatenate([np.zeros((NHALO - NMETA, D), np.float32), meta], 0)
        hpos = np.maximum(np.arange(NHALO) - (NHALO - NMETA), 0)
    xin = np.ascontiguousarray(np.concatenate([own, meta, halo], 0), dtype=np.float32)
    opos = 16 + NOWN * r + np.arange(NOWN)
    mpos = np.arange(NMETA)

    par = np.zeros((128, NPAR), np.float32)

    def put(name, arr):
        o, w = PAR[name]
        par[:, o:o + w] = arr
    put('g_ab', inp['mix_norm_ab'][0].reshape(8, 128).T)
    put('g_ffn0', inp['ffn_norm'][0].reshape(8, 128).T)
    put('g_ret', inp['mix_norm_ret'][0].reshape(8, 128).T)
    put('g_ffn1', inp['ffn_norm'][1].reshape(8, 128).T)
    cw = inp['lru_conv_w'][0]
    put('convw', cw.reshape(4, 4, 128).transpose(2, 1, 0).reshape(128, 16))
    put('convb', inp['lru_conv_b'][0].reshape(4, 128).T)
    put('ba', inp['lru_b_a'][0].reshape(4, 128).T)
    put('bi', inp['lru_b_i'][0].reshape(4, 128).T)
    put('lam', inp['lru_lambda'][0].reshape(4, 128).T)
    put('qg', np.tile(inp['q_norm'][0], 2)[:, None])
    put('kg', np.tile(inp['k_norm'][0], 2)[:, None])
    put('sink', np.broadcast_to(inp['attn_sinks'][0][None, :], (128, 8)))

    tab, tab2 = _host_tables(r)

    def tput(name, arr):
        o, w = TAB[name]
        tab[:arr.shape[0], o:o + w] = arr
    C, S = _att_tables(mpos)
    tput('attc_m', C)
    tput('atts_m', S)
    C, S = _att_tables(hpos)
    tput('attc_h', C)
    tput('atts_h', S)
    rc, rs = _ret_tables(mpos)
    tput('retc_m', rc)
    tput('rets_m', rs)
    C, S = _att_tables(opos)
    attcs = np.stack([C, S], 0)
    rc, rs = _ret_tables(opos)
    retcs = np.stack([rc, rs], 0)
    return {
        "xin": xin, "params": par, "tables": tab, "tables2": tab2, "attcs": np.ascontiguousarray(attcs), "retcs": np.ascontiguousarray(retcs),
        "ab_w_in": inp['ab_w_in'][0], "ab_w_out": inp['ab_w_out'][0], "lru_w_a": inp['lru_w_a'][0], "lru_w_i": inp['lru_w_i'][0],
        "ret_w_in": inp['ret_w_in'][0], "ret_w_out": inp['ret_w_out'][0], "ffn_w_gu": inp['ffn_w_gu'], "ffn_w_down": inp['ffn_w_down'],
    }


_NC_CACHE = {}


def kernel(dbg=(), stop=None, **inputs):
    inp = {k: np.asarray(v) for k, v in inputs.items()}
    key = (tuple(dbg), stop)
    if key not in _NC_CACHE:
        _NC_CACHE[key] = build_program(dbg, stop)
    nc = _NC_CACHE[key]
    in_maps = [_core_inputs(c, inp) for c in range(8)]
    res = run_bass_kernel_spmd(nc, in_maps, core_ids=list(range(8)))
    out = np.zeros((2, 8192, D), np.float32)
    for c in range(8):
        b, r = c // 4, c % 4
        out[b, NOWN * r:NOWN * (r + 1)] = res.results[c]["out"]
    if dbg:
        return out, res
    return out
```

```python
import numpy as np
import ml_dtypes
from contextlib import ExitStack
import concourse.bass as bass
import concourse.mybir as mybir
from concourse.bass_utils import run_bass_kernel_spmd

ACT = mybir.ActivationFunctionType
ALU = mybir.AluOpType
AX = mybir.AxisListType
F32 = mybir.dt.float32
BF16 = mybir.dt.bfloat16
U8 = mybir.dt.uint8
ENGS = ['pe', 'act', 'dve', 'pool', 'sp']

NOWN = 2048
NMETA = 16
NHALO = 128
NCOL = NOWN + NMETA
MOFF = NOWN
D = 1024
DFF = 2816
EPS = 1e-6
BW = 256
NBLK = NOWN // BW
CPB = BW // 128
LOGG = [float(np.log1p(-2.0 ** (-5.0 - h))) for h in range(4)]


class Buf:
    __slots__ = ('name', 'w', 'r', 'excl')

    def __init__(self, name):
        self.name = name
        self.w = None
        self.r = {}
        self.excl = False


class Tile:
    def __init__(self, ap, name):
        self.ap = ap
        self.b = Buf(name)

    def __getitem__(self, idx):
        return self.ap[idx]


class Ring:
    def __init__(self, tiles):
        self.t = tiles
        self.i = 0

    def next(self):
        t = self.t[self.i % len(self.t)]
        self.i += 1
        return t


def _bufs(lst):
    return [x.b if isinstance(x, Tile) else x for x in lst]


class Prog:
    def __init__(self, nc, es):
        self.nc = nc
        self.es = es
        self.ops = {e: [] for e in ENGS}
        self.waited = {e: set() for e in ENGS}
        self.dma_cnt = {}
        self.floor = {}

    def _deps(self, eng, reads, writes):
        waits = {}

        def need(dep):
            if dep is None:
                return
            k, v = dep
            if v <= self.floor.get(k, -1):
                return
            if waits.get(k, -1) < v:
                waits[k] = v
        for b in reads:
            need(b.w)
            if b.excl:
                for k, v in b.r.items():
                    if k != eng:
                        need((k, v))
        for b in writes:
            need(b.w)
            for k, v in b.r.items():
                need((k, v))
        if eng == 'pe':
            waits.pop('pe', None)
        for k, v in waits.items():
            if k in self.waited:
                self.waited[k].add(v)
        return waits

    def op(self, eng, fn, reads=(), writes=()):
        reads = _bufs(reads)
        writes = _bufs(writes)
        seq = len(self.ops[eng])
        waits = self._deps(eng, reads, writes)
        self.ops[eng].append(dict(fn=fn, waits=waits, dma=None))
        for b in reads:
            if b.r.get(eng, -1) < seq:
                b.r[eng] = seq
        for b in writes:
            b.w = (eng, seq)
            b.r = {}

    def dma(self, eng, fn, reads=(), writes=(), sem='misc'):
        reads = _bufs(reads)
        writes = _bufs(writes)
        waits = self._deps(eng, reads, writes)
        key = ('dma', sem)
        waits.pop(key, None)
        self.dma_cnt[key] = self.dma_cnt.get(key, 0) + 16
        val = self.dma_cnt[key]
        self.ops[eng].append(dict(fn=fn, waits=waits, dma=key))
        for b in reads:
            b.r[key] = val
        for b in writes:
            b.w = (key, val)
            b.r = {}

    def barrier(self):
        last = {}
        for e in ENGS:
            if self.ops[e]:
                for s in range(len(self.ops[e]) - 1, -1, -1):
                    o = self.ops[e][s]
                    if o['fn'] is not None and o['dma'] is None:
                        last[e] = s
                        break
        for k, v in self.dma_cnt.items():
            last[k] = v
        for e in ENGS:
            waits = {}
            for k, v in last.items():
                if k == e:
                    continue
                if v <= self.floor.get(k, -1):
                    continue
                waits[k] = v
                if k in self.waited:
                    self.waited[k].add(v)
            self.ops[e].append(dict(fn=None, waits=waits, dma=None))
        for k, v in last.items():
            self.floor[k] = v

    def build(self):
        nc, es = self.nc, self.es
        sems = {e: es.enter_context(nc.semaphore('s_' + e)) for e in ENGS}
        dsems = {k: es.enter_context(nc.semaphore('d_' + str(k[1]))) for k in self.dma_cnt}
        rank = {}
        for e in self.waited:
            for i, s in enumerate(sorted(self.waited[e])):
                rank[(e, s)] = i + 1
        block = es.enter_context(nc.Block())
        ops, waited = self.ops, self.waited

        def run(ename, e):
            for seq, o in enumerate(ops[ename]):
                for k, v in o['waits'].items():
                    if isinstance(k, tuple):
                        e.wait_ge(dsems[k], v)
                    else:
                        e.wait_ge(sems[k], rank[(k, v)])
                if o['fn'] is None:
                    continue
                ins = o['fn'](e)
                if o['dma'] is not None:
                    ins.then_inc(dsems[o['dma']], 16)
                elif seq in waited[ename]:
                    ins.then_inc(sems[ename], 1)

        @block.tensor
        def _(e):
            run('pe', e)

        @block.scalar
        def _(e):
            run('act', e)

        @block.vector
        def _(e):
            run('dve', e)

        @block.gpsimd
        def _(e):
            run('pool', e)

        @block.sync
        def _(e):
            run('sp', e)


class Arena:
    def __init__(self, nc, es, nbytes):
        self.t = es.enter_context(nc.sbuf_tensor("arena", [128, nbytes], U8))
        self.n = nbytes
        self.off = 0
        self.cnt = 0

    def mark(self):
        return self.off

    def release(self, m):
        self.off = m

    def alloc(self, shape, dt, name=None):
        esz = 4 if dt == F32 else 2
        n = int(np.prod(shape))
        nb = (n * esz + 63) // 64 * 64
        assert self.off + nb <= self.n, f"arena overflow {name} {self.off}+{nb}>{self.n}"
        ap = self.t[:, self.off:self.off + n * esz].bitcast(dt)
        self.off += nb
        if len(shape) == 2:
            ap = ap.rearrange("p (a b) -> p a b", a=shape[0])
        elif len(shape) == 3:
            ap = ap.rearrange("p (a b c) -> p a b c", a=shape[0], b=shape[1])
        self.cnt += 1
        return Tile(ap, name or f"t{self.cnt}")


PAR = {}
_o = 0
for _n, _w in [('g_ab', 8), ('g_ffn0', 8), ('g_ret', 8), ('g_ffn1', 8), ('convw', 16), ('convb', 4),
               ('ba', 4), ('bi', 4), ('lam', 4), ('qg', 1), ('kg', 1), ('sink', 8)]:
    PAR[_n] = (_o, _w)
    _o += _w
NPAR = _o

TAB = {}
_o = 0
for _n, _w in [('ident', 128), ('fm', 8), ('dk', 4), ('dkf', 64), ('dkm', 4),
               ('cf', 20), ('attc_m', 16), ('atts_m', 16), ('attc_h', 128), ('atts_h', 128),
               ('retc_m', 16), ('rets_m', 16), ('dt4', 512), ('dq4', 512)]:
    TAB[_n] = (_o, _w)
    _o += _w
NTAB = _o
NTABP = TAB['dt4'][0]
TAB2 = {}
_o = 0
for _n, _w in [('identb', 128), ('rperm', 128), ('bones', 128), ('mcur', 512), ('mprev', 512), ('mprev0', 512), ('mmeta', 64)]:
    TAB2[_n] = (_o, _w)
    _o += _w
NTAB2 = _o


def _host_tables(r):
    t = np.zeros((128, NTAB), np.float32)
    t2 = np.zeros((128, NTAB2), np.float32)

    def put(name, arr):
        arr = np.asarray(arr, np.float32)
        if name in TAB2:
            o, w = TAB2[name]
            assert arr.shape[1] == w, (name, arr.shape)
            t2[:arr.shape[0], o:o + w] = arr
        else:
            o, w = TAB[name]
            assert arr.shape[1] == w, (name, arr.shape)
            t[:arr.shape[0], o:o + w] = arr
    put('ident', np.eye(128))
    put('identb', np.eye(128))
    rp = np.zeros((128, 128))
    for m in range(128):
        d = m % 64
        if d < 8:
            rp[m + 8, m] = 1
        elif d < 16:
            rp[m - 8, m] = 1
    put('rperm', rp)
    bo = np.zeros((128, 128))
    bo[:64, :64] = 1
    bo[64:, 64:] = 1
    put('bones', bo)
    k = np.arange(128)[:, None]
    q = np.arange(128)[None, :]
    cur = (k <= q).astype(np.float32)
    prev = (k > q).astype(np.float32)
    put('mcur', np.tile(cur, (1, 4)))
    put('mprev', np.tile(prev, (1, 4)))
    put('mprev0', np.tile(prev if r > 0 else np.zeros_like(prev), (1, 4)))
    k16 = np.arange(16)[:, None]
    q16 = np.arange(16)[None, :]
    put('mmeta', np.tile((k16 <= q16).astype(np.float32), (1, 4)))
    fm = np.zeros((128, 8))
    for c in range(4):
        fm[:, c] = 1.0 if c < r else 0.0
        fm[:, 4 + c] = 1.0 - fm[:, c]
    put('fm', fm)
    lg = np.array(LOGG, np.float64)
    j = np.arange(128)[:, None]
    i = np.arange(128)[None, :]
    dt4 = np.concatenate([np.where(i >= j, np.exp(np.maximum(i - j, 0) * lg[h]), 0.0) / 16.0 for h in range(4)], 1)
    put('dt4', dt4)
    dq4 = np.concatenate([np.broadcast_to(np.exp((np.arange(128) + 1.0) * lg[h])[None, :], (128, 128)) for h in range(4)], 1)
    put('dq4', dq4)
    dk = np.stack([np.exp((127.0 - np.arange(128)) * lg[h]) / 16.0 for h in range(4)], 1)
    put('dk', dk)
    dkf = np.concatenate([dk * np.exp(128.0 * (15 - c) * lg)[None, :] for c in range(16)], 1)
    put('dkf', dkf)
    dkm = np.stack([np.exp((15.0 - np.arange(16)) * lg[h]) / 16.0 for h in range(4)], 1)
    put('dkm', dkm)
    cf = np.zeros((128, 20))
    for c in range(4):
        for h in range(4):
            cf[:, c * 4 + h] = np.exp(128.0 * 16 * (r - 1 - c) * lg[h]) if c < r else 0.0
    for h in range(4):
        cf[:, 16 + h] = np.exp(128.0 * 16 * r * lg[h])
    put('cf', cf)
    return t, t2


def _att_tables(pos):
    pos = np.asarray(pos, np.float32)
    inv = np.power(np.float32(500000.0), -np.arange(8, dtype=np.float32) / np.float32(8)).astype(np.float32)
    ang = (pos[:, None] * inv[None, :]).astype(np.float32)
    c = np.cos(ang).astype(np.float32).T
    s = np.sin(ang).astype(np.float32).T
    C = np.ones((128, len(pos)), np.float32)
    S = np.zeros((128, len(pos)), np.float32)
    for half in range(2):
        b = 64 * half
        C[b:b + 8] = c
        C[b + 8:b + 16] = c
        S[b:b + 8] = -s
        S[b + 8:b + 16] = s
    return C, S


def _ret_tables(pos):
    pos = np.asarray(pos, np.float32)
    inv = np.power(np.float32(10000.0), -np.arange(128, dtype=np.float32) / np.float32(128)).astype(np.float32)
    ang = (pos[:, None] * inv[None, :]).astype(np.float32)
    return np.cos(ang).astype(np.float32).T.copy(), np.sin(ang).astype(np.float32).T.copy()


class _Stop(Exception):
    pass


def build_program(dbg=(), stop=None):
    nc = bass.Bass("TRN2", target_bir_lowering=False)

    def din(name, shape, dt=F32):
        return nc.dram_tensor(name, list(shape), dt, kind="ExternalInput").ap()

    xin = din("xin", [NOWN + NMETA + NHALO, D])
    par_d = din("params", [128, NPAR])
    tab_d = din("tables", [128, NTAB])
    tab2_d = din("tables2", [128, NTAB2])
    attcs_d = din("attcs", [2, 128, NOWN])
    retcs_d = din("retcs", [2, 128, NOWN])
    w_in0 = din("ab_w_in", [D, 1792])
    w_out0 = din("ab_w_out", [D, D])
    wa_d = din("lru_w_a", [8, 64, 64])
    wi_d = din("lru_w_i", [8, 64, 64])
    w_in1 = din("ret_w_in", [D, 6144])
    w_out1 = din("ret_w_out", [2048, D])
    w_gu = din("ffn_w_gu", [2, D, 2 * DFF])
    w_dn = din("ffn_w_down", [2, DFF, D])
    out_d = nc.dram_tensor("out", [NOWN, D], F32, kind="ExternalOutput").ap()
    dbg_d = {n: nc.dram_tensor("dbg_" + n, list(s), F32, kind="ExternalOutput").ap() for n, s in dbg}

    cc1_in = nc.dram_tensor("cc1_in", [128, 8], F32)
    cc1_out = nc.dram_tensor("cc1_out", [512, 8], F32)
    cc2_in = nc.dram_tensor("cc2_in", [4, 128, 1024], BF16)
    cc2_out = nc.dram_tensor("cc2_out", [4, 512, 1024], BF16)
    scr_ab = nc.dram_tensor("scr_ab", [128, 2, 4, NCOL], F32)
    scr_sm = nc.dram_tensor("scr_sm", [128, 8, 512], F32)
    scr_kt = nc.dram_tensor("scr_kt", [128, 8, NOWN], BF16)
    scr_kd = nc.dram_tensor("scr_kd", [128, 16, 1024], BF16)
    scr_v = nc.dram_tensor("scr_v", [128, 16, 2048], BF16)
    scr_on = nc.dram_tensor("scr_on", [128, 16, 2048], BF16)

    es = ExitStack()
    with es:
        P = Prog(nc, es)
        A = Arena(nc, es, 207 * 1024)
        PS = [Tile(es.enter_context(nc.psum_tensor(f"ps{i}", [128, 512], F32))[:, :], f"ps{i}") for i in range(8)]
        for _p in PS:
            _p.b.excl = True
        PSR = Ring(PS)

        def mm(out, lhsT, rhs, start, stop, reads, writes):
            P.op('pe', lambda e: e.matmul(out, lhsT, rhs, start=start, stop=stop), reads, writes)

        def tr(out, in_, ident, reads, writes):
            P.op('pe', lambda e: e.transpose(out, in_, ident), reads, writes)

        def act(out, in_, func, reads, writes, **kw):
            P.op('act', lambda e: e.activation(out=out, in_=in_, func=func, **kw), reads, writes)

        def tt(eng, out, in0, in1, op, reads, writes):
            P.op(eng, lambda e: e.tensor_tensor(out=out, in0=in0, in1=in1, op=op), reads, writes)

        def ts(eng, out, in0, s1, s2, op0, op1, reads, writes):
            P.op(eng, lambda e: e.tensor_scalar(out=out, in0=in0, scalar1=s1, scalar2=s2, op0=op0, op1=op1), reads, writes)

        def stt(out, in0, scalar, in1, op0, op1, reads, writes):
            P.op('dve', lambda e: e.scalar_tensor_tensor(out=out, in0=in0, scalar=scalar, in1=in1, op0=op0, op1=op1), reads, writes)

        def cp(eng, out, in_, reads, writes):
            if eng == 'act':
                act(out, in_, ACT.Copy, reads, writes)
            else:
                P.op(eng, lambda e: e.tensor_copy(out=out, in_=in_), reads, writes)

        def memset(eng, out, val, writes):
            P.op(eng, lambda e: e.memset(out, val), (), writes)

        def dma(eng, out, in_, reads, writes, sem):
            P.dma(eng, lambda e: e.dma_start(out=out, in_=in_), reads, writes, sem)

        def recip(out, in_, reads, writes):
            P.op('dve', lambda e: e.reciprocal(out=out, in_=in_), reads, writes)

        def reduce_add(out, in_, reads, writes):
            P.op('dve', lambda e: e.tensor_reduce(out=out, in_=in_, axis=AX.X, op=ALU.add), reads, writes)

        def scan(out, a, b, init, reads, writes):
            P.op('dve', lambda e: e.tensor_tensor_scan(out=out, data0=a, data1=b, initial=init, op0=ALU.mult, op1=ALU.add),
                 reads, writes)

        hT = A.alloc([8, NCOL], F32, "hT")
        HB_ = [Tile(hT.ap, f"hT_b{i}") for i in range(NBLK + 1)]

        def hblk(c0):
            return HB_[NBLK] if c0 >= MOFF else HB_[c0 // BW]

        def hblks(c0, n):
            if c0 >= MOFF:
                return [HB_[NBLK]]
            return [HB_[i] for i in range(c0 // BW, (c0 + n + BW - 1) // BW)]
        par = A.alloc([NPAR], F32, "par")
        tab = A.alloc([NTABP], F32, "tab")
        identb = A.alloc([128], BF16, "identb")
        onesb = A.alloc([128], BF16, "onesb")
        epsc = A.alloc([1], F32, "epsc")

        def pcol(name, i=0, n=1):
            o, w = PAR[name]
            return par[:, o + i:o + i + n]

        def tcol(name, i=0, n=None):
            o, w = TAB[name]
            if n is None:
                n = w
            return tab[:, o + i:o + i + n]

        dma('sp', par.ap, par_d, [], [par], 'const')
        dma('sp', tab.ap, tab_d[:, 0:NTABP], [], [tab], 'const')
        dma('pool', identb.ap, tab2_d[:, TAB2['identb'][0]:TAB2['identb'][0] + 128], [], [identb], 'const2')
        memset('dve', onesb.ap, 1.0, [onesb])
        memset('dve', epsc.ap, EPS, [epsc])
        P.barrier()
        ident_f = tcol('ident')
        consts = [par, tab]

        def dump(name, ap, bufs):
            if name in dbg_d:
                dma('sp', dbg_d[name], ap, bufs, [Buf("dbg")], 'dbg')
                P.barrier()
            if stop == name:
                raise _Stop()

        def rmsnorm(src_ap_fn, nk, n, gain_name, out_t, out_ap_fn, srcbufs, F, H):
            ps = PSR.next()
            for k in range(nk):
                sq = H.next()
                act(sq[:, 0:n], src_ap_fn(k), ACT.Square, srcbufs, [sq])
                mm(ps[:, 0:n], onesb.ap, sq[:, 0:n], k == 0, k == nk - 1, [sq, onesb], [ps])
            sd = F.next()
            act(sd[:, 0:n], ps[:, 0:n], ACT.Ln, [ps, epsc], [sd], scale=1.0 / (nk * 128), bias=epsc[:, 0:1])
            rs = F.next()
            act(rs[:, 0:n], sd[:, 0:n], ACT.Exp, [sd], [rs], scale=-0.5)
            for k in range(nk):
                stt(out_ap_fn(k), src_ap_fn(k), pcol(gain_name, k), rs[:, 0:n], ALU.mult, ALU.mult,
                    srcbufs + [rs, par], [out_t])

        def load_w(dst_t, src_ap, kchunks, ncols, sem, eng='pool', dst_ap=None):
            if dst_ap is None:
                dst_ap = dst_t.ap
            step = max(1, 2048 // ncols)
            for k0 in range(0, kchunks, step):
                k1 = min(kchunks, k0 + step)
                dma(eng, dst_ap[:, k0:k1, 0:ncols], src_ap[k0 * 128:k1 * 128, :].rearrange("(k p) c -> p k c", p=128),
                    [], [dst_t], sem)

        try:
            m0 = A.mark()
            hTh = A.alloc([8, NHALO], F32, "hTh")
            rpermb = A.alloc([128], BF16, "rpermb")
            bonesb = A.alloc([128], BF16, "bonesb")
            mcur = A.alloc([512], BF16, "mcur")
            mprev = A.alloc([512], BF16, "mprev")
            mprev0 = A.alloc([512], BF16, "mprev0")
            mmeta = A.alloc([64], BF16, "mmeta")
            for nm, tl in [('rperm', rpermb), ('bones', bonesb), ('mcur', mcur), ('mprev', mprev),
                           ('mprev0', mprev0), ('mmeta', mmeta)]:
                o2, w2 = TAB2[nm]
                dma('pool', tl.ap, tab2_d[:, o2:o2 + w2], [], [tl], 'const2')
            w0 = A.alloc([8, 1792], BF16, "w_in0")
            wo0 = A.alloc([8, D], BF16, "w_out0")
            load_w(w0, w_in0[:, 0:1024], 8, 1024, 'w0', dst_ap=w0[:, :, 0:1024])
            load_w(w0, w_in0[:, 1536:1792], 8, 256, 'w0', dst_ap=w0[:, :, 1536:1792])
            for k in range(8):
                for s_ in range(2):
                    dma('pool', w0[:, k, 1024:1536].rearrange("p (j s d) -> p j s d", j=4, s=2, d=64)[:, :, s_, :],
                        w_in0[k * 128:(k + 1) * 128, 1024 + 256 * s_:1024 + 256 * s_ + 256].rearrange("p (j d) -> p j d", d=64),
                        [], [w0], 'w0')
            load_w(wo0, w_out0[0:512, :], 4, D, 'wo0')
            for s in range(2):
                dma('pool', wo0[64 * s:64 * s + 64, 4:8, :],
                    w_out0[512 + 256 * s:512 + 256 * s + 256, :].rearrange("(j d) c -> d j c", d=64), [], [wo0], 'wo0')
            wab = A.alloc([4, 128], BF16, "wab")
            wib = A.alloc([4, 128], BF16, "wib")
            memset('dve', wab.ap, 0.0, [wab])
            memset('dve', wib.ap, 0.0, [wib])
            for m in range(4):
                for a in range(2):
                    dma('pool', wab[64 * a:64 * a + 64, m, 64 * a:64 * a + 64], wa_d[2 * m + a], [], [wab], 'wab')
                    dma('pool', wib[64 * a:64 * a + 64, m, 64 * a:64 * a + 64], wi_d[2 * m + a], [], [wib], 'wab')
            lsc = A.alloc([8], F32, "lsc")
            ltmp = A.alloc([4], F32, "ltmp")
            act(ltmp.ap, pcol('lam', 0, 4), ACT.Exp, [par], [ltmp], scale=-1.0)
            act(ltmp.ap, ltmp.ap, ACT.Ln, [ltmp], [ltmp], bias=1.0)
            ts('dve', lsc[:, 0:4], ltmp.ap, -8.0, None, ALU.mult, ALU.bypass, [ltmp], [lsc])
            ts('dve', lsc[:, 4:8], ltmp.ap, -16.0, None, ALU.mult, ALU.bypass, [ltmp], [lsc])
            sexp = A.alloc([8], F32, "sexp")
            act(sexp.ap, pcol('sink', 0, 8), ACT.Exp, [par], [sexp])
            sink4 = A.alloc([2, 512], F32, "sink4")
            onesf = A.alloc([128], F32, "onesf")
            memset('dve', onesf.ap, 1.0, [onesf])
            for g in range(2):
                for j in range(4):
                    ts('dve', sink4[:, g, 128 * j:128 * j + 128], onesf.ap, sexp[:, 4 * g + j:4 * g + j + 1], None,
                       ALU.mult, ALU.bypass, [onesf, sexp], [sink4])
            mxs = A.mark()
            xs = Ring([A.alloc([D], F32, f"xs{i}") for i in range(2)])
            rows = [(c * 128, 128, ('own', c * 128)) for c in range(16)] + [(NOWN, NMETA, ('meta', 0)), (NOWN + NMETA, NHALO, ('halo', 0))]
            for ri, (r0, nr, (kind, c0)) in enumerate(rows):
                xt = xs.next()
                dma('sp', xt[0:nr, :], xin[r0:r0 + nr, :], [], [xt], f'xs{ri % 2}')
                for g in range(2):
                    ps = PSR.next()
                    for kk in range(4):
                        k = g * 4 + kk
                        tr(ps[:, kk * 128:kk * 128 + nr], xt[0:nr, k * 128:(k + 1) * 128], ident_f[0:nr, 0:nr], [xt, tab], [ps])
                    src = ps.ap.rearrange("p (a b) -> p a b", a=4)[:, :, 0:nr]
                    if kind == 'own':
                        cp('act' if g == 0 else 'dve', hT[:, g * 4:g * 4 + 4, c0:c0 + nr], src, [ps], [hblk(c0)])
                    elif kind == 'meta':
                        cp('act' if g == 0 else 'dve', hT[:, g * 4:g * 4 + 4, MOFF:MOFF + nr], src, [ps], [hblk(MOFF)])
                    else:
                        cp('act' if g == 0 else 'dve', hTh[:, g * 4:g * 4 + 4, 0:nr], src, [ps], [hTh])

            P.barrier()
            A.release(mxs)
            dump('x0', hT.ap, HB_)
            P.barrier()

            hcar = A.alloc([4], F32, "hcar")
            sumr = A.alloc([4], F32, "sumr")
            hmeta = A.alloc([4], F32, "hmeta")
            red = A.alloc([4], F32, "red")
            xhist = A.alloc([4, 3], F32, "xhist")
            pay = A.alloc([8], F32, "pay")
            xg = A.alloc([4, 8], F32, "xg")
            hin = A.alloc([4], F32, "hin")
            ftmp = A.alloc([8], F32, "ftmp")
            sk_ab = Buf("sk_ab")
            mp1 = A.mark()
            BW1 = 512
            hnr = [A.alloc([8, BW1], BF16, f"hn0{i}") for i in range(2)]
            Fn = Ring([A.alloc([BW1], F32, f"Fn{i}") for i in range(2)])
            Hn = Ring([A.alloc([BW1], BF16, f"Hn{i}") for i in range(2)])
            xre = A.alloc([4, 3 + BW1], F32, "xre")
            F = Ring([A.alloc([BW1], F32, f"F{i}") for i in range(20)])
            H = Ring([A.alloc([BW1], BF16, f"H{i}") for i in range(4)])

            def l0_src(kind, c0):
                if kind == 'halo':
                    return (lambda k: hTh[:, k, :]), [hTh]
                return (lambda k: hT[:, k, c0:c0 + (NMETA if kind == 'meta' else BW)]), [hblk(c0)]

            def proj_fm(wt, k_n, wcols_fn, rhs_fn, n, rbufs):
                ps = PSR.next()
                for k in range(k_n):
                    mm(ps[:, 0:n], wcols_fn(k), rhs_fn(k), k == 0, k == k_n - 1, [wt] + rbufs, [ps])
                return ps

            def lru_pass1(kind, c0, n, prefetch):
                for m in range(4):
                    ps = proj_fm(w0, 8, lambda k: w0[:, k, 128 * m:128 * m + 128], lambda k: hn[:, k, 0:n], n, [hn])
                    cp('act', xre[:, m, 3:3 + n], ps[:, 0:n], [ps], [xre])
                if kind == 'halo':
                    prefetch()
                    return
                xcs, xcbs, prs, pis, rrs, iis, aas, mus = [], [], [], [], [], [], [], []
                for m in range(4):
                    xc = F.next()
                    ts('dve', xc[:, 0:n], xre[:, m, 0:n], pcol('convw', 4 * m + 0), pcol('convb', m), ALU.mult, ALU.add,
                       [xre, par], [xc])
                    for j in range(1, 4):
                        stt(xc[:, 0:n], xre[:, m, j:j + n], pcol('convw', 4 * m + j), xc[:, 0:n], ALU.mult, ALU.add,
                            [xre, par, xc], [xc])
                    xcs.append(xc)
                for m in range(4):
                    xcb = H.next()
                    cp('act', xcb[:, 0:n], xcs[m][:, 0:n], [xcs[m]], [xcb])
                    xcbs.append(xcb)
                for m in range(4):
                    if m == 2:
                        for mm_ in range(2):
                            rr = F.next()
                            act(rr[:, 0:n], prs[mm_][:, 0:n], ACT.Sigmoid, [prs[mm_], par], [rr], bias=pcol('ba', mm_))
                            ii = F.next()
                            act(ii[:, 0:n], pis[mm_][:, 0:n], ACT.Sigmoid, [pis[mm_], par], [ii], bias=pcol('bi', mm_))
                            rrs.append(rr)
                            iis.append(ii)
                    pr = PSR.next()
                    mm(pr[:, 0:n], wab[:, m, :], xcbs[m][:, 0:n], True, True, [wab, xcbs[m]], [pr])
                    pi = PSR.next()
                    mm(pi[:, 0:n], wib[:, m, :], xcbs[m][:, 0:n], True, True, [wib, xcbs[m]], [pi])
                    prs.append(pr)
                    pis.append(pi)
                for mm_ in range(2, 4):
                    rr = F.next()
                    act(rr[:, 0:n], prs[mm_][:, 0:n], ACT.Sigmoid, [prs[mm_], par], [rr], bias=pcol('ba', mm_))
                    ii = F.next()
                    act(ii[:, 0:n], pis[mm_][:, 0:n], ACT.Sigmoid, [pis[mm_], par], [ii], bias=pcol('bi', mm_))
                    rrs.append(rr)
                    iis.append(ii)
                prefetch()
                if kind == 'own':
                    for m in range(4):
                        reduce_add(red[:, m:m + 1], rrs[m][:, 0:n], [rrs[m]], [red])
                    tt('dve', sumr.ap, sumr.ap, red.ap, ALU.add, [sumr, red], [sumr])
                for m in range(4):
                    aa = F.next()
                    act(aa[:, 0:n], rrs[m][:, 0:n], ACT.Exp, [rrs[m], lsc], [aa], scale=lsc[:, m:m + 1])
                    mu = F.next()
                    act(mu[:, 0:n], rrs[m][:, 0:n], ACT.Exp, [rrs[m], lsc], [mu], scale=lsc[:, 4 + m:5 + m])
                    aas.append(aa)
                    mus.append(mu)
                for m in range(4):
                    act(mus[m][:, 0:n], mus[m][:, 0:n], ACT.Sqrt, [mus[m]], [mus[m]], scale=-1.0, bias=1.0)
                for m in range(4):
                    mu = mus[m]
                    if kind == 'meta':
                        memset('dve', mu[:, 0:1], 1.0, [mu])
                    tt('dve', mu[:, 0:n], mu[:, 0:n], iis[m][:, 0:n], ALU.mult, [mu, iis[m]], [mu])
                    tt('dve', mu[:, 0:n], mu[:, 0:n], xcs[m][:, 0:n], ALU.mult, [mu, xcs[m]], [mu])
                    dma('sp', scr_ab.ap()[:, 0, m, c0:c0 + n], aas[m][:, 0:n], [aas[m]], [sk_ab], f'sa{m}')
                    dma('sp', scr_ab.ap()[:, 1, m, c0:c0 + n], mu[:, 0:n], [mu], [sk_ab], f'sb{m}')
                for m in range(4):
                    hh = F.next()
                    scan(hh[:, 0:n], aas[m][:, 0:n], mus[m][:, 0:n], hcar[:, m:m + 1], [aas[m], mus[m], hcar], [hh])
                    cp('dve', hcar[:, m:m + 1], hh[:, n - 1:n], [hh], [hcar])
                cp('dve', xhist.ap, xre[:, :, n:n + 3], [xre], [xhist])

            blocks_p1 = [('meta', MOFF, NMETA), ('halo', 0, NHALO)] + [('own', b * BW1, BW1) for b in range(NOWN // BW1)]
            memset('dve', sumr.ap, 0.0, [sumr])
            def norm_p1(i):
                if i >= len(blocks_p1):
                    return
                kind, c0, n = blocks_p1[i]
                ht = hnr[i % 2]
                if kind == 'halo':
                    srcf, sb = (lambda k: hTh[:, k, :]), [hTh]
                elif n == BW1:
                    srcf, sb = (lambda k: hT[:, k, c0:c0 + n]), [hblk(c0), hblk(c0 + BW)]
                else:
                    srcf, sb = (lambda k: hT[:, k, c0:c0 + n]), [hblk(c0)]
                rmsnorm(srcf, 8, n, 'g_ab', ht, lambda k: ht[:, k, 0:n], sb, Fn, Hn)
            norm_p1(0)
            for bi, (kind, c0, n) in enumerate(blocks_p1):
                hn = hnr[bi % 2]
                if kind == 'meta':
                    memset('dve', xre[:, :, 0:3], 0.0, [xre])
                    memset('dve', hcar.ap, 0.0, [hcar])
                elif kind == 'own':
                    cp('dve', xre[:, :, 0:3], xhist.ap, [xhist], [xre])
                lru_pass1(kind, c0, n, lambda: norm_p1(bi + 1))
                if kind == 'meta':
                    cp('dve', hmeta.ap, hcar.ap, [hcar], [hmeta])
                if kind == 'halo':
                    cp('dve', xhist.ap, xre[:, :, n:n + 3], [xre], [xhist])
                    memset('dve', hcar.ap, 0.0, [hcar])
            for m in range(4):
                act(pay[:, m:m + 1], sumr[:, m:m + 1], ACT.Exp, [sumr, lsc], [pay], scale=lsc[:, m:m + 1])
            cp('dve', pay[:, 4:8], hcar.ap, [hcar], [pay])
            cc1i = Buf("cc1i")
            cc1o = Buf("cc1o")
            dma('pool', cc1_in.ap(), pay.ap, [pay], [cc1i], 'cc1')
            P.op('pool', lambda e: e.collective_compute("AllGather", ALU.bypass, replica_groups=[[0, 1, 2, 3], [4, 5, 6, 7]],
                                                        ins=[cc1_in.ap().opt()], outs=[cc1_out.ap().opt()]), [cc1i], [cc1o])
            dma('pool', xg.ap, cc1_out.ap().rearrange("(r p) f -> p r f", p=128), [cc1o], [xg], 'cc1b')
            cp('dve', hin.ap, hmeta.ap, [hmeta], [hin])
            for c in range(4):
                ts('dve', ftmp[:, 0:4], xg[:, c, 0:4], tcol('fm', c, 1), tcol('fm', 4 + c, 1), ALU.mult, ALU.add, [xg, tab], [ftmp])
                ts('dve', ftmp[:, 4:8], xg[:, c, 4:8], tcol('fm', c, 1), None, ALU.mult, ALU.bypass, [xg, tab], [ftmp])
                tt('dve', hin.ap, hin.ap, ftmp[:, 0:4], ALU.mult, [hin, ftmp], [hin])
                tt('dve', hin.ap, hin.ap, ftmp[:, 4:8], ALU.add, [hin, ftmp], [hin])

            P.barrier()
            A.release(mp1)
            hnr = [A.alloc([8, BW], BF16, f"hn0b{i}") for i in range(2)]
            Fn = Ring([A.alloc([BW], F32, f"Fnb{i}") for i in range(2)])
            Hn = Ring([A.alloc([BW], BF16, f"Hnb{i}") for i in range(2)])
            yb = A.alloc([8, BW], BF16, "yb")
            QT = A.alloc([4, BW], BF16, "QT")
            KT = A.alloc([NOWN + NMETA + NHALO], BF16, "KT")
            Vt = A.alloc([18, 128], BF16, "Vt")
            atc = A.alloc([BW], F32, "atc")
            ats = A.alloc([BW], F32, "ats")
            F = Ring([A.alloc([BW], F32, f"F2{i}") for i in range(12)])
            H = Ring([A.alloc([BW], BF16, f"H2{i}") for i in range(8)])
            PT = Ring([A.alloc([512], BF16, f"PT{i}") for i in range(12)])
            FA = Ring([A.alloc([512], F32, f"FA{i}") for i in range(4)])
            abr = Ring([A.alloc([2, 4, BW], F32, f"ab{i}") for i in range(2)])
            dump('p1', hT.ap, HB_ + [hin])
            def qk_norm_rope_multi(items, n, ctab, stab, tabbufs):
                sqs, p2s, rss, qns, p3s = [], [], [], [], []
                for (ps, gname, out_ap, out_t) in items:
                    sq = H.next()
                    act(sq[:, 0:n], ps[:, 0:n], ACT.Square, [ps], [sq])
                    sqs.append(sq)
                for i, (ps, gname, out_ap, out_t) in enumerate(items):
                    p2 = PSR.next()
                    mm(p2[:, 0:n], bonesb.ap, sqs[i][:, 0:n], True, True, [bonesb, sqs[i]], [p2])
                    p2s.append(p2)
                for i in range(len(items)):
                    sd = F.next()
                    act(sd[:, 0:n], p2s[i][:, 0:n], ACT.Ln, [p2s[i], epsc], [sd], scale=1.0 / 64, bias=epsc[:, 0:1])
                    act(sd[:, 0:n], sd[:, 0:n], ACT.Exp, [sd], [sd], scale=-0.5)
                    rss.append(sd)
                for i, (ps, gname, out_ap, out_t) in enumerate(items):
                    qn = H.next()
                    stt(qn[:, 0:n], ps[:, 0:n], pcol(gname), rss[i][:, 0:n], ALU.mult, ALU.mult, [ps, rss[i], par], [qn])
                    qns.append(qn)
                for i in range(len(items)):
                    p3 = PSR.next()
                    mm(p3[:, 0:n], rpermb.ap, qns[i][:, 0:n], True, True, [rpermb, qns[i]], [p3])
                    p3s.append(p3)
                for i, (ps, gname, out_ap, out_t) in enumerate(items):
                    t1 = F.next()
                    tt('dve', t1[:, 0:n], p3s[i][:, 0:n], stab, ALU.mult, [p3s[i]] + tabbufs, [t1])
                    t2 = F.next()
                    tt('pool' if i % 2 == 0 else 'dve', t2[:, 0:n], qns[i][:, 0:n], ctab, ALU.mult, [qns[i]] + tabbufs, [t2])
                    tt('dve', out_ap, t1[:, 0:n], t2[:, 0:n], ALU.add, [t1, t2], [out_t])

            def qk_norm_rope(ps, n, gain_name, ctab, stab, tabbufs, out_ap, out_t):
                qk_norm_rope_multi([(ps, gain_name, out_ap, out_t)], n, ctab, stab, tabbufs)

            def attn_front(qn_cols, q0, kblocks):
                nq4 = 4 * qn_cols
                sps = []
                for g in range(2):
                    for (kc0, nk, vi, mk) in kblocks:
                        ps = PSR.next()
                        mm(ps.ap[0:nk, 0:nq4].rearrange("p (j q) -> p j q", j=4), KT[64 * g:64 * g + 64, kc0:kc0 + nk],
                           QT[64 * g:64 * g + 64, :, q0:q0 + qn_cols], True, True, [KT, QT], [ps])
                        sps.append((g, ps, nk, vi, mk))
                pts = {0: [], 1: []}
                for (g, ps, nk, vi, mk) in sps:
                    pt = PT.next()
                    act(pt[0:nk, 0:nq4], ps[0:nk, 0:nq4], ACT.Exp, [ps], [pt], scale=0.125)
                    pts[g].append((pt, nk, vi, mk))
                for g in range(2):
                    for (pt, nk, vi, mk) in pts[g]:
                        if mk is not None:
                            tt('pool' if g == 0 else 'dve', pt[0:nk, 0:nq4], pt[0:nk, 0:nq4], mk[0:nk, 0:nq4], ALU.mult, [pt, mk], [pt])
                return pts

            def attn_back(qn_cols, pts, ycols0):
                nq4 = 4 * qn_cols
                outs = []
                for g in range(2):
                    pv = PSR.next()
                    pdn = PSR.next()
                    for i, (pt, nk, vi, mk) in enumerate(pts[g]):
                        mm(pv[:, 0:nq4], Vt[0:nk, vi, :], pt[0:nk, 0:nq4], i == 0, i == len(pts[g]) - 1, [Vt, pt], [pv])
                    for i, (pt, nk, vi, mk) in enumerate(pts[g]):
                        mm(pdn[:, 0:nq4], onesb[0:nk, :], pt[0:nk, 0:nq4], i == 0, False, [onesb, pt], [pdn])
                    mm(pdn.ap[:, 0:nq4].rearrange("p (j q) -> p j q", j=4), onesf[0:1, :],
                       sink4[0:1, g, :].rearrange("p (j q) -> p j q", j=4)[:, :, 0:qn_cols], False, True, [onesf, sink4], [pdn])
                    outs.append((pv, pdn))
                for g in range(2):
                    pv, pdn = outs[g]
                    dn = FA.next()
                    act(dn[:, 0:nq4], pdn[:, 0:nq4], ACT.Ln, [pdn], [dn])
                    rd = FA.next()
                    act(rd[:, 0:nq4], dn[:, 0:nq4], ACT.Exp, [dn], [rd], scale=-1.0)
                    tt('dve', yb[64 * g:64 * g + 64, 4:8, ycols0:ycols0 + qn_cols],
                       pv.ap[64 * g:64 * g + 64, 0:nq4].rearrange("p (j q) -> p j q", j=4),
                       rd.ap[64 * g:64 * g + 64, 0:nq4].rearrange("p (j q) -> p j q", j=4), ALU.mult, [pv, rd], [yb])

            def attention(qn_cols, q0, kblocks, ycols0):
                attn_back(qn_cols, attn_front(qn_cols, q0, kblocks), ycols0)

            wq = w0[:, :, 1024:1536].rearrange("p k (j m) -> p k j m", j=4, m=128)
            blocks0 = [('meta', MOFF, NMETA), ('halo', 0, NHALO)] + [('own', b * BW, BW) for b in range(NBLK)]
            def norm_p2(i):
                if i >= len(blocks0):
                    return
                kind, c0, n = blocks0[i]
                ht = hnr[i % 2]
                srcf, sb = l0_src(kind, c0)
                rmsnorm(srcf, 8, n, 'g_ab', ht, lambda k: ht[:, k, 0:n], sb, Fn, Hn)
            norm_p2(0)
            for bi, (kind, c0, n) in enumerate(blocks0):
                hn = hnr[bi % 2]
                if kind == 'own':
                    dma('sp', atc[:, 0:n], attcs_d[0, :, c0:c0 + n], [], [atc], 'atc')
                    dma('sp', ats[:, 0:n], attcs_d[1, :, c0:c0 + n], [], [ats], 'atc')
                    ctab, stab, tbufs = atc[:, 0:n], ats[:, 0:n], [atc, ats]
                elif kind == 'meta':
                    ctab, stab, tbufs = tcol('attc_m'), tcol('atts_m'), [tab]
                else:
                    ctab, stab, tbufs = tcol('attc_h'), tcol('atts_h'), [tab]
                kcol0 = {'own': c0, 'meta': NOWN, 'halo': NOWN + NMETA}[kind]
                ps = proj_fm(w0, 8, lambda k: w0[:, k, 1536:1664], lambda k: hn[:, k, 0:n], n, [hn])
                qk_norm_rope(ps, n, 'kg', ctab, stab, tbufs, KT[:, kcol0:kcol0 + n], KT)
                nch = max(1, n // 128)
                for ci in range(nch):
                    nt = min(128, n)
                    vi = {'own': c0 // 128 + ci, 'meta': 16, 'halo': 17}[kind]
                    ps = PSR.next()
                    for k in range(8):
                        mm(ps[0:nt, 0:128], hn[:, k, ci * 128:ci * 128 + nt], w0[:, k, 1664:1792], k == 0, k == 7, [hn, w0], [ps])
                    cp('act', Vt[0:nt, vi, :], ps[0:nt, 0:128], [ps], [Vt])
                if kind == 'halo':
                    cp('dve', hcar.ap, hin.ap, [hin], [hcar])
                    norm_p2(bi + 1)
                    continue
                if kind == 'meta':
                    memset('dve', hcar.ap, 0.0, [hcar])
                absem = f'lab{abr.i % 2}'
                ab = abr.next()
                dma('sp', ab[:, :, :, 0:n], scr_ab.ap()[:, :, :, c0:c0 + n], [sk_ab], [ab], absem)
                for m in range(4):
                    ps = proj_fm(w0, 8, lambda k: w0[:, k, 512 + 128 * m:512 + 128 * m + 128], lambda k: hn[:, k, 0:n], n, [hn])
                    act(yb[:, m, 0:n], ps[:, 0:n], ACT.Gelu_apprx_tanh, [ps], [yb])
                for m in range(4):
                    hh = F.next()
                    scan(hh[:, 0:n], ab[:, 0, m, 0:n], ab[:, 1, m, 0:n], hcar[:, m:m + 1], [ab, hcar], [hh])
                    cp('dve', hcar[:, m:m + 1], hh[:, n - 1:n], [hh], [hcar])
                    tt('pool', yb[:, m, 0:n], yb[:, m, 0:n], hh[:, 0:n], ALU.mult, [yb, hh], [yb])
                qitems = []
                for j in range(4):
                    ps = proj_fm(w0, 8, lambda k: wq[:, k, j], lambda k: hn[:, k, 0:n], n, [hn])
                    qitems.append((ps, 'qg', QT[:, j, 0:n], QT))
                qk_norm_rope_multi(qitems, n, ctab, stab, tbufs)
                if kind == 'meta':
                    norm_p2(bi + 1)
                    attention(NMETA, 0, [(NOWN, NMETA, 16, mmeta)], 0)
                else:
                    fronts = []
                    for ci in range(n // 128):
                        ch = c0 // 128 + ci
                        if ch == 0:
                            prevb = (NOWN + NMETA, 128, 17, mprev0)
                        else:
                            prevb = ((ch - 1) * 128, 128, ch - 1, mprev)
                        fronts.append(attn_front(128, ci * 128, [(NOWN, NMETA, 16, None), prevb, (ch * 128, 128, ch, mcur)]))
                    norm_p2(bi + 1)
                    for ci in range(n // 128):
                        attn_back(128, fronts[ci], ci * 128)
                for mo in range(8):
                    ps = proj_fm(wo0, 8, lambda k: wo0[:, k, 128 * mo:128 * mo + 128], lambda k: yb[:, k, 0:n], n, [yb])
                    tt('dve', hT[:, mo, c0:c0 + n], hT[:, mo, c0:c0 + n], ps[:, 0:n], ALU.add, [hblk(c0), ps], [hblk(c0)])
            P.barrier()
            A.release(m0)

            dump('h1', hT.ap, HB_)

            FCB = [(0, 512), (512, 512), (1024, 512), (1536, 512), (MOFF, NMETA)]

            def ffn(layer, gain_name, with_meta):
                m1 = A.mark()
                cbs = FCB if with_meta else FCB[:4]
                ncol = NCOL if with_meta else NOWN
                hnf = A.alloc([8, NCOL], BF16, "hnf")
                Ff = Ring([A.alloc([512], F32, f"Ff{i}") for i in range(2)])
                Hf = Ring([A.alloc([512], BF16, f"Hf{i}") for i in range(2)])
                hnfB = {c0: Tile(hnf.ap, f"hnf_{c0}") for (c0, n) in cbs}
                normed = set()

                def need_norm(c0, n):
                    if c0 in normed:
                        return
                    normed.add(c0)
                    rmsnorm(lambda k: hT[:, k, c0:c0 + n], 8, n, gain_name, hnfB[c0], lambda k: hnf[:, k, c0:c0 + n],
                            [hblk(c0)] if n < 512 else [hblk(c0), hblk(c0 + BW)], Ff, Hf)
                for (c0_, n_) in cbs:
                    need_norm(c0_, n_)
                actb = Ring([A.alloc([4, NCOL], BF16, f"actb{i}") for i in range(2)])
                wg = Ring([A.alloc([8, 512], BF16, f"wg{i}") for i in range(2)])
                wu = Ring([A.alloc([8, 512], BF16, f"wu{i}") for i in range(2)])
                wd = Ring([A.alloc([4, D], BF16, f"wd{i}") for i in range(2)])
                sgr = Ring([A.alloc([512], BF16, f"sg{i}") for i in range(2)])
                for q in range(6):
                    nt = 4 if q < 5 else 2
                    wgt, wut, wdt, ab = wg.next(), wu.next(), wd.next(), actb.next()
                    load_w(wgt, w_gu[layer, :, 512 * q:512 * q + 128 * nt], 8, 128 * nt, f'wg{q % 2}')
                    load_w(wut, w_gu[layer, :, DFF + 512 * q:DFF + 512 * q + 128 * nt], 8, 128 * nt, f'wu{q % 2}')
                    for t0 in range(0, nt, 2):
                        dma('pool', wdt[:, t0:t0 + 2, :], w_dn[layer, 512 * q + 128 * t0:512 * q + 128 * (t0 + 2), :].rearrange("(k p) c -> p k c", p=128),
                            [], [wdt], f'wd{q % 2}')
                    for t in range(nt):
                        for (c0, n) in cbs:
                            need_norm(c0, n)
                            pg = PSR.next()
                            for k in range(8):
                                mm(pg[:, 0:n], wgt[:, k, 128 * t:128 * t + 128], hnf[:, k, c0:c0 + n], k == 0, k == 7, [wgt, hnfB[c0]], [pg])
                            pu = PSR.next()
                            for k in range(8):
                                mm(pu[:, 0:n], wut[:, k, 128 * t:128 * t + 128], hnf[:, k, c0:c0 + n], k == 0, k == 7, [wut, hnfB[c0]], [pu])
                            sg = sgr.next()
                            act(sg[:, 0:n], pg[:, 0:n], ACT.Silu, [pg], [sg])
                            tt('dve', ab[:, t, c0:c0 + n], sg[:, 0:n], pu[:, 0:n], ALU.mult, [sg, pu], [ab])
                    for mo in range(8):
                        for (c0, n) in cbs:
                            ps = PSR.next()
                            for t in range(nt):
                                mm(ps[:, 0:n], wdt[:, t, 128 * mo:128 * mo + 128], ab[:, t, c0:c0 + n], t == 0, t == nt - 1, [wdt, ab], [ps])
                            hb = [hblk(c0)] if n < 512 else [hblk(c0), hblk(c0 + BW)]
                            tt('dve', hT[:, mo, c0:c0 + n], hT[:, mo, c0:c0 + n], ps[:, 0:n], ALU.add, hb + [ps], hb)
                P.barrier()
                A.release(m1)

            ffn(0, 'g_ffn0', True)
            dump('h2', hT.ap, HB_)

            m2 = A.mark()
            Sacc = A.alloc([8, 512], F32, "Sacc")
            mA = A.mark()
            sk_sm = Buf("sk_sm")
            mA2 = A.mark()
            wk = A.alloc([8, 1024], BF16, "wk")
            wv = A.alloc([8, 2048], BF16, "wv")
            load_w(wk, w_in1[:, 1024:2048], 8, 1024, 'wk')
            load_w(wv, w_in1[:, 2048:4096], 8, 2048, 'wv')
            BWA = 512
            hnr = [A.alloc([8, BWA], BF16, f"hn1{i}") for i in range(2)]
            Fn = Ring([A.alloc([BWA], F32, f"FnA{i}") for i in range(2)])
            Hn = Ring([A.alloc([BWA], BF16, f"HnA{i}") for i in range(2)])
            F = Ring([A.alloc([BWA], F32, f"G{i}") for i in range(6)])
            smst = Ring([A.alloc([512], F32, f"smst{i}") for i in range(2)])
            H = Ring([A.alloc([BWA], BF16, f"I{i}") for i in range(4)])
            KTb = A.alloc([8, BWA], BF16, "KTb")
            rc = A.alloc([BWA], F32, "rc")
            rs_ = A.alloc([BWA], F32, "rs")
            Kd = Ring([A.alloc([1024], BF16, f"Kd{i}") for i in range(2)])
            Kdf = Ring([A.alloc([1024], BF16, f"Kdf{i}") for i in range(2)])
            Vc = Ring([A.alloc([2048], BF16, f"Vc{i}") for i in range(2)])
            memset('pool', Sacc.ap, 0.0, [Sacc])
            sk_kt, sk_kd, sk_v, sk_on = Buf("skt"), Buf("skd"), Buf("sv"), Buf("son")

            def rope_ret(psa, psb, n, ctab, stab, tbufs, outa, outb, out_t):
                x1 = F.next()
                cp('act', x1[:, 0:n], psa[:, 0:n], [psa], [x1])
                x2 = F.next()
                cp('act', x2[:, 0:n], psb[:, 0:n], [psb], [x2])
                t1, t2, t3, t4 = F.next(), F.next(), F.next(), F.next()
                tt('dve', t1[:, 0:n], x1[:, 0:n], ctab, ALU.mult, [x1] + tbufs, [t1])
                tt('dve', t2[:, 0:n], x2[:, 0:n], stab, ALU.mult, [x2] + tbufs, [t2])
                tt('dve', t3[:, 0:n], x2[:, 0:n], ctab, ALU.mult, [x2] + tbufs, [t3])
                tt('dve', t4[:, 0:n], x1[:, 0:n], stab, ALU.mult, [x1] + tbufs, [t4])
                tt('dve', outa, t1[:, 0:n], t2[:, 0:n], ALU.subtract, [t1, t2], [out_t])
                tt('dve', outb, t3[:, 0:n], t4[:, 0:n], ALU.add, [t3, t4], [out_t])

            blocks1 = [('meta', MOFF, NMETA)] + [('own', b * BWA, BWA) for b in range(NOWN // BWA)]
            def norm_A(i):
                if i >= len(blocks1):
                    return
                kind, c0, n = blocks1[i]
                ht = hnr[i % 2]
                rmsnorm(lambda k: hT[:, k, c0:c0 + n], 8, n, 'g_ret', ht, lambda k: ht[:, k, 0:n], hblks(c0, n), Fn, Hn)
            norm_A(0)
            for bi, (kind, c0, n) in enumerate(blocks1):
                hn = hnr[bi % 2]
                if kind == 'own':
                    dma('sp', rc[:, 0:n], retcs_d[0, :, c0:c0 + n], [], [rc], 'rc')
                    dma('sp', rs_[:, 0:n], retcs_d[1, :, c0:c0 + n], [], [rs_], 'rc')
                    ctab, stab, tbufs = rc[:, 0:n], rs_[:, 0:n], [rc, rs_]
                else:
                    ctab, stab, tbufs = tcol('retc_m'), tcol('rets_m'), [tab]
                for h in range(4):
                    pa = proj_fm(wk, 8, lambda k: wk[:, k, 256 * h:256 * h + 128], lambda k: hn[:, k, 0:n], n, [hn])
                    pb = proj_fm(wk, 8, lambda k: wk[:, k, 256 * h + 128:256 * h + 256], lambda k: hn[:, k, 0:n], n, [hn])
                    rope_ret(pa, pb, n, ctab, stab, tbufs, KTb[:, 2 * h, 0:n], KTb[:, 2 * h + 1, 0:n], KTb)
                if kind == 'own':
                    dma('sp', scr_kt.ap()[:, :, c0:c0 + n], KTb[:, :, 0:n], [KTb], [sk_kt], 'skt')
                norm_A(bi + 1)
                nt = min(128, n)
                for ci in range(max(1, n // 128)):
                    ch = c0 // 128 + ci
                    ps = PSR.next()
                    psb16 = ps.ap.bitcast(BF16)
                    for f in range(8):
                        tr(psb16[0:nt, 128 * f:128 * f + 128], KTb[:, f, ci * 128:ci * 128 + nt], identb.ap, [KTb, identb], [ps])
                    kd, kdf = Kd.next(), Kdf.next()
                    for h in range(4):
                        if kind == 'own':
                            act(kd[0:nt, 256 * h:256 * h + 256], psb16[0:nt, 256 * h:256 * h + 256], ACT.Copy, [ps, tab], [kd],
                                scale=tcol('dk', h, 1)[0:nt])
                            ts('dve', kdf[0:nt, 256 * h:256 * h + 256], psb16[0:nt, 256 * h:256 * h + 256],
                               tcol('dkf', 4 * ch + h, 1)[0:nt], None, ALU.mult, ALU.bypass, [ps, tab], [kdf])
                        else:
                            ts('dve', kdf[0:nt, 256 * h:256 * h + 256], psb16[0:nt, 256 * h:256 * h + 256],
                               tcol('dkm', h, 1)[0:nt], None, ALU.mult, ALU.bypass, [ps, tab], [kdf])
                    vc = Vc.next()
                    for h in range(4):
                        ps = PSR.next()
                        for k in range(8):
                            mm(ps[0:nt, :], hn[:, k, ci * 128:ci * 128 + nt], wv[:, k, 512 * h:512 * h + 512], k == 0, k == 7, [hn, wv], [ps])
                        cp('act' if h % 2 == 0 else 'dve', vc[0:nt, 512 * h:512 * h + 512], ps[0:nt, :], [ps], [vc])
                    if kind == 'own':
                        dma('sp', scr_kd.ap()[:, ch, :], kd.ap, [kd], [sk_kd], f'skd{ch % 2}')
                        dma('sp', scr_v.ap()[:, ch, :], vc.ap, [vc], [sk_v], f'sv{ch % 2}')
                    for h in range(4):
                        for dtl in range(2):
                            ps = PSR.next()
                            mm(ps[:, :], kdf[0:nt, 256 * h + 128 * dtl:256 * h + 128 * dtl + 128], vc[0:nt, 512 * h:512 * h + 512],
                               True, True, [kdf, vc], [ps])
                            if kind == 'own':
                                tt('dve', Sacc[:, 2 * h + dtl, :], Sacc[:, 2 * h + dtl, :], ps[:, :], ALU.add, [Sacc, ps], [Sacc])
                            else:
                                st_ = smst.next()
                                cp('dve', st_.ap, ps[:, :], [ps], [st_])
                                dma('sp', scr_sm.ap()[:, 2 * h + dtl, :], st_.ap, [st_], [sk_sm], f'ssm{(2 * h + dtl) % 2}')
            dump('pa', hT.ap, HB_ + [Sacc])
            cc2i, cc2o = Buf("cc2i"), Buf("cc2o")
            P.barrier()
            A.release(mA2)
            wqr = A.alloc([8, 1024], BF16, "wqr")
            mB = A.mark()
            Sbf = A.alloc([8, 512], BF16, "Sbf")
            cp('act', Sbf[:, 0:4, :], Sacc[:, 0:4, :], [Sacc], [Sbf])
            cp('dve', Sbf[:, 4:8, :], Sacc[:, 4:8, :], [Sacc], [Sbf])
            dma('sp', cc2_in.ap().rearrange("g p (t b) -> p g t b", t=2), Sbf.ap.rearrange("p (g t) b -> p g t b", t=2),
                [Sbf], [cc2i], 'cc2')
            cc2og = [Buf(f"cc2o{g}") for g in range(4)]
            for g4 in range(4):
                def _cc(e, g4=g4):
                    return e.collective_compute("AllGather", ALU.bypass, replica_groups=[[0, 1, 2, 3], [4, 5, 6, 7]],
                                                ins=[cc2_in.ap()[g4].opt()], outs=[cc2_out.ap()[g4].opt()])
                P.op('pool', _cc, [cc2i], [cc2og[g4]])
            load_w(wqr, w_in1[:, 0:1024], 8, 1024, 'wq1')
            Sin = Sacc
            sgm = A.alloc([8, 512], F32, "sgm")
            sgr = [A.alloc([4, 2, 512], BF16, f"sgr{i}") for i in range(2)]
            dma('sp', sgm.ap, scr_sm.ap(), [sk_sm], [sgm], 'cc2m')
            for h in range(4):
                for dtl in range(2):
                    f = 2 * h + dtl
                    ts('dve', Sin[:, f, :], sgm[:, f, :], tcol('cf', 16 + h, 1), None, ALU.mult, ALU.bypass, [sgm, tab], [Sin])
            for g4 in range(4):
                sgt = sgr[g4 % 2]
                dma('sp', sgt.ap, cc2_out.ap()[g4].rearrange("(r p) (t b) -> p r t b", p=128, t=2), [cc2og[g4]], [sgt], f'cc2b{g4 % 2}')
                for c in range(4):
                    for dtl in range(2):
                        f = 2 * g4 + dtl
                        stt(Sin[:, f, :], sgt[:, c, dtl, :], tcol('cf', 4 * c + g4, 1), Sin[:, f, :], ALU.mult, ALU.add, [sgt, tab, Sin], [Sin])
            P.barrier()
            A.release(mB)
            Sf = Sacc
            dump('ex2', hT.ap, HB_ + [Sacc])

            Sb = A.alloc([8, 512], BF16, "Sb")
            BWB = 512
            tabB = A.alloc([512], F32, "tabB")
            dma('sp', tabB.ap, tab_d[:, NTABP:NTABP + 512], [], [tabB], 'tabB')
            dqx = A.alloc([4, BWB], BF16, "dqx")
            hnr = [A.alloc([8, BWB], BF16, f"hn2{i}") for i in range(2)]
            Fn = Ring([A.alloc([BWB], F32, f"FnB{i}") for i in range(2)])
            Hn = Ring([A.alloc([BWB], BF16, f"HnB{i}") for i in range(2)])
            F = Ring([A.alloc([BWB], F32, f"J{i}") for i in range(6)])
            dq_tmp = F.next()
            dma('sp', dq_tmp.ap, tab_d[:, NTABP + 512:NTABP + 1024], [], [dq_tmp], 'tabB2')
            for h in range(4):
                for ci in range(BWB // 128):
                    cp('dve', dqx[:, h, 128 * ci:128 * ci + 128], dq_tmp[:, 128 * h:128 * h + 128], [dq_tmp], [dqx])
            QTb = A.alloc([8, BWB], BF16, "QTb")
            QDb = A.alloc([8, BWB], BF16, "QDb")
            rc = A.alloc([BWB], F32, "rc2")
            rs_ = A.alloc([BWB], F32, "rs2")
            KTc = Ring([A.alloc([8, 128], BF16, f"KTc{i}") for i in range(2)])
            Kdc = Ring([A.alloc([1024], BF16, f"Kdc{i}") for i in range(2)])
            Vcc = Ring([A.alloc([2048], BF16, f"Vcc{i}") for i in range(2)])
            PTr = Ring([A.alloc([512], BF16, f"PTr{i}") for i in range(2)])
            onr = Ring([A.alloc([2048], BF16, f"on{i}") for i in range(2)])
            junk = Ring([A.alloc([512], BF16, f"junk{i}") for i in range(4)])
            osbr = Ring([A.alloc([512], F32, f"osb{i}") for i in range(4)])
            ssq = A.alloc([8], F32, "ssq")
            SfB = [Buf(f"Sf{f}") for f in range(8)]
            SbB = [Buf(f"Sb{f}") for f in range(8)]
            cp('act', Sb.ap, Sf.ap, [Sf], [Sb] + SbB)
            DH = [float(np.exp(128.0 * LOGG[h])) for h in range(4)]
            def norm_B(i):
                if i >= NOWN // BWB:
                    return
                c0_, n_ = i * BWB, BWB
                ht = hnr[i % 2]
                rmsnorm(lambda k: hT[:, k, c0_:c0_ + n_], 8, n_, 'g_ret', ht, lambda k: ht[:, k, 0:n_], hblks(c0_, n_), Fn, Hn)
            norm_B(0)
            for b in range(NOWN // BWB):
                c0, n = b * BWB, BWB
                hn = hnr[b % 2]
                dma('sp', rc[:, 0:n], retcs_d[0, :, c0:c0 + n], [], [rc], 'rc')
                dma('sp', rs_[:, 0:n], retcs_d[1, :, c0:c0 + n], [], [rs_], 'rc')
                for h in range(4):
                    pa = proj_fm(wqr, 8, lambda k: wqr[:, k, 256 * h:256 * h + 128], lambda k: hn[:, k, 0:n], n, [hn])
                    pb = proj_fm(wqr, 8, lambda k: wqr[:, k, 256 * h + 128:256 * h + 256], lambda k: hn[:, k, 0:n], n, [hn])
                    rope_ret(pa, pb, n, rc[:, 0:n], rs_[:, 0:n], [rc, rs_], QTb[:, 2 * h, 0:n], QTb[:, 2 * h + 1, 0:n], QTb)
                    for dtl in range(2):
                        tt('dve', QDb[:, 2 * h + dtl, :].rearrange("p (c i) -> p c i", i=128),
                           QTb[:, 2 * h + dtl, :].rearrange("p (c i) -> p c i", i=128),
                           dqx[:, h, :].rearrange("p (c i) -> p c i", i=128), ALU.mult, [QTb, dqx], [QDb])
                norm_B(b + 1)
                def chunk_front(ci):
                    ch = c0 // 128 + ci
                    cs = slice(128 * ci, 128 * ci + 128)
                    ktc, kdc, vcc = KTc.next(), Kdc.next(), Vcc.next()
                    dma('sp', ktc.ap, scr_kt.ap()[:, :, ch * 128:ch * 128 + 128], [sk_kt], [ktc], f'lk{ch % 2}')
                    dma('sp', kdc.ap, scr_kd.ap()[:, ch, :], [sk_kd], [kdc], f'lkd{ch % 2}')
                    dma('sp', vcc.ap, scr_v.ap()[:, ch, :], [sk_v], [vcc], f'lv{ch % 2}')
                    pA = PSR.next()
                    for h in range(4):
                        for dtl in range(2):
                            mm(pA[:, 128 * h:128 * h + 128], ktc[:, 2 * h + dtl, :], QTb[:, 2 * h + dtl, cs], dtl == 0, dtl == 1, [ktc, QTb], [pA])
                    pt = PTr.next()
                    tt('dve', pt.ap, pA.ap, tabB[:, 0:512], ALU.mult, [pA, tabB], [pt])
                    return (ch, cs, kdc, vcc, pt)

                nxt = chunk_front(0)
                for ci in range(BWB // 128):
                    ch, cs, kdc, vcc, pt = nxt
                    on = onr.next()
                    osbs = []
                    for h in range(4):
                        po = PSR.next()
                        mm(po[:, :], pt[:, 128 * h:128 * h + 128], vcc[:, 512 * h:512 * h + 512], True, False, [pt, vcc], [po])
                        for dtl in range(2):
                            mm(po[:, :], QDb[:, 2 * h + dtl, cs], Sb[:, 2 * h + dtl, :], False, dtl == 1, [QDb, SbB[2 * h + dtl]], [po])
                        osb = osbr.next()
                        cp('act', osb.ap, po.ap, [po], [osb])
                        osbs.append(osb)
                    for h in range(4):
                        for dtl in range(2):
                            f = 2 * h + dtl
                            pS = PSR.next()
                            mm(pS[:, :], kdc[:, 256 * h + 128 * dtl:256 * h + 128 * dtl + 128], vcc[:, 512 * h:512 * h + 512], True, True, [kdc, vcc], [pS])
                            stt(Sf[:, f, :], Sf[:, f, :], DH[h], pS.ap, ALU.mult, ALU.add, [SfB[f], pS], [SfB[f]])
                            cp('act' if dtl == 0 else 'dve', Sb[:, f, :], Sf[:, f, :], [SfB[f]], [SbB[f]])
                    if ci + 1 < BWB // 128:
                        nxt = chunk_front(ci + 1)
                    for h in range(4):
                        jk = junk.next()
                        P.op('dve', (lambda e, jk=jk, ob=osbs[h], h=h: e.scalar_tensor_tensor(
                            out=jk.ap, in0=ob.ap, scalar=1.0, in1=ob.ap, op0=ALU.mult, op1=ALU.mult,
                            accum_out=ssq[:, h:h + 1])), [osbs[h]], [jk, ssq])
                    act(ssq[:, 4:8], ssq[:, 0:4], ACT.Sqrt, [ssq, epsc], [ssq], scale=1.0 / 512, bias=epsc[:, 0:1])
                    recip(ssq[:, 4:8], ssq[:, 4:8], [ssq], [ssq])
                    for h in range(4):
                        act(on[:, 512 * h:512 * h + 512], osbs[h].ap, ACT.Copy, [osbs[h], ssq], [on], scale=ssq[:, 4 + h:5 + h])
                    dma('act', scr_on.ap()[:, ch, :], on.ap, [on], [sk_on], f'son{ch % 2}')
            P.barrier()
            A.release(m2)
            dump('h2b', hT.ap, HB_)

            wgr = A.alloc([8, 2048], BF16, "wgr")
            wor = A.alloc([16, D], BF16, "wor")
            load_w(wgr, w_in1[:, 4096:6144], 8, 2048, 'wg1')
            load_w(wor, w_out1, 16, D, 'wo1')
            BW2 = 512
            hnr = [A.alloc([8, BW2], BF16, f"hn3{i}") for i in range(2)]
            Fn = Ring([A.alloc([BW2], F32, f"FnC{i}") for i in range(2)])
            Hn = Ring([A.alloc([BW2], BF16, f"HnC{i}") for i in range(2)])
            sgT = A.alloc([16, BW2], BF16, "sgT")
            gT = A.alloc([16, BW2], BF16, "gT")
            onb = Ring([A.alloc([2048], BF16, f"onb{i}") for i in range(2)])

            def norm_B2(i):
                if i >= NOWN // BW2:
                    return
                c0_, n_ = i * BW2, BW2
                ht = hnr[i % 2]
                rmsnorm(lambda k: hT[:, k, c0_:c0_ + n_], 8, n_, 'g_ret', ht, lambda k: ht[:, k, 0:n_], hblks(c0_, n_), Fn, Hn)
            norm_B2(0)
            for b in range(NOWN // BW2):
                c0, n = b * BW2, BW2
                hn = hnr[b % 2]
                for ft in range(16):
                    ps = proj_fm(wgr, 8, lambda k: wgr[:, k, 128 * ft:128 * ft + 128], lambda k: hn[:, k, 0:n], n, [hn])
                    act(sgT[:, ft, 0:n], ps[:, 0:n], ACT.Silu, [ps], [sgT])
                norm_B2(b + 1)
                for ci in range(BW2 // 128):
                    ch = c0 // 128 + ci
                    ob = onb.next()
                    dma('sp', ob.ap, scr_on.ap()[:, ch, :], [sk_on], [ob], f'lon{ch % 2}')
                    for g2 in range(2):
                        ps = PSR.next()
                        psb16 = ps.ap.bitcast(BF16)
                        for f8 in range(8):
                            ft = 8 * g2 + f8
                            tr(psb16[:, 128 * f8:128 * f8 + 128], ob[:, 128 * ft:128 * ft + 128], identb.ap, [ob, identb], [ps])
                        tt('dve', gT[:, 8 * g2:8 * g2 + 8, 128 * ci:128 * ci + 128], psb16.rearrange("p (a b) -> p a b", a=8),
                           sgT[:, 8 * g2:8 * g2 + 8, 128 * ci:128 * ci + 128], ALU.mult, [ps, sgT], [gT])
                for mo in range(8):
                    ps = proj_fm(wor, 16, lambda k: wor[:, k, 128 * mo:128 * mo + 128], lambda k: gT[:, k, 0:n], n, [gT])
                    tt('dve', hT[:, mo, c0:c0 + n], hT[:, mo, c0:c0 + n], ps[:, 0:n], ALU.add, hblks(c0, n) + [ps], hblks(c0, n))
            P.barrier()
            A.release(m2)
            dump('h3', hT.ap, HB_)

            ffn(1, 'g_ffn1', False)

            ot = Ring([A.alloc([D], F32, f"ot{i}") for i in range(2)])
            outb = Buf("out")
            for ch in range(16):
                o = ot.next()
                for g in range(2):
                    ps = PSR.next()
                    for kk in range(4):
                        k = 4 * g + kk
                        tr(ps[:, 128 * kk:128 * kk + 128], hT[:, k, ch * 128:ch * 128 + 128], ident_f, [hblk(ch * 128), tab], [ps])
                    cp('act' if g == 0 else 'dve', o[:, 512 * g:512 * g + 512], ps.ap, [ps], [o])
                dma('sp', out_d[ch * 128:ch * 128 + 128, :], o.ap, [o], [outb], f'out{ch % 2}')

        except _Stop:
            pass
        P.barrier()
        P.build()
    return nc


def _core_inputs(c, inp):
    b, r = c // 4, c % 4
    x = inp['x']
    meta = inp['meta_tokens']
    own = x[b, NOWN * r:NOWN * (r + 1)]
    if r > 0:
        halo = x[b, NOWN * r - NHALO:NOWN * r]
        hpos = 16 + NOWN * r - NHALO + np.arange(NHALO)
    else:
        halo = np.concatenate([np.zeros((NHALO - NMETA, D), np.float32), meta], 0)
        hpos = np.maximum(np.arange(NHALO) - (NHALO - NMETA), 0)
    xin = np.ascontiguousarray(np.concatenate([own, meta, halo], 0), dtype=np.float32)
    opos = 16 + NOWN * r + np.arange(NOWN)
    mpos = np.arange(NMETA)

    par = np.zeros((128, NPAR), np.float32)

    def put(name, arr):
        o, w = PAR[name]
        par[:, o:o + w] = arr
    put('g_ab', inp['mix_norm_ab'][0].reshape(8, 128).T)
    put('g_ffn0', inp['ffn_norm'][0].reshape(8, 128).T)
    put('g_ret', inp['mix_norm_ret'][0].reshape(8, 128).T)
    put('g_ffn1', inp['ffn_norm'][1].reshape(8, 128).T)
    cw = inp['lru_conv_w'][0]
    put('convw', cw.reshape(4, 4, 128).transpose(2, 1, 0).reshape(128, 16))
    put('convb', inp['lru_conv_b'][0].reshape(4, 128).T)
    put('ba', inp['lru_b_a'][0].reshape(4, 128).T)
    put('bi', inp['lru_b_i'][0].reshape(4, 128).T)
    put('lam', inp['lru_lambda'][0].reshape(4, 128).T)
    put('qg', np.tile(inp['q_norm'][0], 2)[:, None])
    put('kg', np.tile(inp['k_norm'][0], 2)[:, None])
    put('sink', np.broadcast_to(inp['attn_sinks'][0][None, :], (128, 8)))

    tab, tab2 = _host_tables(r)

    def tput(name, arr):
        o, w = TAB[name]
        tab[:arr.shape[0], o:o + w] = arr
    C, S = _att_tables(mpos)
    tput('attc_m', C)
    tput('atts_m', S)
    C, S = _att_tables(hpos)
    tput('attc_h', C)
    tput('atts_h', S)
    rc, rs = _ret_tables(mpos)
    tput('retc_m', rc)
    tput('rets_m', rs)
    C, S = _att_tables(opos)
    attcs = np.stack([C, S], 0)
    rc, rs = _ret_tables(opos)
    retcs = np.stack([rc, rs], 0)
    return {
        "xin": xin, "params": par, "tables": tab, "tables2": tab2, "attcs": np.ascontiguousarray(attcs), "retcs": np.ascontiguousarray(retcs),
        "ab_w_in": inp['ab_w_in'][0], "ab_w_out": inp['ab_w_out'][0], "lru_w_a": inp['lru_w_a'][0], "lru_w_i": inp['lru_w_i'][0],
        "ret_w_in": inp['ret_w_in'][0], "ret_w_out": inp['ret_w_out'][0], "ffn_w_gu": inp['ffn_w_gu'], "ffn_w_down": inp['ffn_w_down'],
    }


_NC_CACHE = {}


def kernel(dbg=(), stop=None, **inputs):
    inp = {k: np.asarray(v) for k, v in inputs.items()}
    key = (tuple(dbg), stop)
    if key not in _NC_CACHE:
        _NC_CACHE[key] = build_program(dbg, stop)
    nc = _NC_CACHE[key]
    in_maps = [_core_inputs(c, inp) for c in range(8)]
    res = run_bass_kernel_spmd(nc, in_maps, core_ids=list(range(8)))
    out = np.zeros((2, 8192, D), np.float32)
    for c in range(8):
        b, r = c // 4, c % 4
        out[b, NOWN * r:NOWN * (r + 1)] = res.results[c]["out"]
    if dbg:
        return out, res
    return out
```

```python
import numpy as np
import ml_dtypes
from contextlib import ExitStack
import concourse.bass as bass
import concourse.mybir as mybir
from concourse.bass_utils import run_bass_kernel_spmd

ACT = mybir.ActivationFunctionType
ALU = mybir.AluOpType
AX = mybir.AxisListType
F32 = mybir.dt.float32
BF16 = mybir.dt.bfloat16
U8 = mybir.dt.uint8
ENGS = ['pe', 'act', 'dve', 'pool', 'sp']

NOWN = 2048
NMETA = 16
NHALO = 128
NCOL = NOWN + NMETA
MOFF = NOWN
D = 1024
DFF = 2816
EPS = 1e-6
BW = 256
NBLK = NOWN // BW
CPB = BW // 128
LOGG = [float(np.log1p(-2.0 ** (-5.0 - h))) for h in range(4)]


class Buf:
    __slots__ = ('name', 'w', 'r', 'excl')

    def __init__(self, name):
        self.name = name
        self.w = None
        self.r = {}
        self.excl = False


class Tile:
    def __init__(self, ap, name):
        self.ap = ap
        self.b = Buf(name)

    def __getitem__(self, idx):
        return self.ap[idx]


class Ring:
    def __init__(self, tiles):
        self.t = tiles
        self.i = 0

    def next(self):
        t = self.t[self.i % len(self.t)]
        self.i += 1
        return t


def _bufs(lst):
    return [x.b if isinstance(x, Tile) else x for x in lst]


class Prog:
    def __init__(self, nc, es):
        self.nc = nc
        self.es = es
        self.ops = {e: [] for e in ENGS}
        self.waited = {e: set() for e in ENGS}
        self.dma_cnt = {}
        self.floor = {}

    def _deps(self, eng, reads, writes):
        waits = {}

        def need(dep):
            if dep is None:
                return
            k, v = dep
            if v <= self.floor.get(k, -1):
                return
            if waits.get(k, -1) < v:
                waits[k] = v
        for b in reads:
            need(b.w)
            if b.excl:
                for k, v in b.r.items():
                    if k != eng:
                        need((k, v))
        for b in writes:
            need(b.w)
            for k, v in b.r.items():
                need((k, v))
        if eng == 'pe':
            waits.pop('pe', None)
        for k, v in waits.items():
            if k in self.waited:
                self.waited[k].add(v)
        return waits

    def op(self, eng, fn, reads=(), writes=()):
        reads = _bufs(reads)
        writes = _bufs(writes)
        seq = len(self.ops[eng])
        waits = self._deps(eng, reads, writes)
        self.ops[eng].append(dict(fn=fn, waits=waits, dma=None))
        for b in reads:
            if b.r.get(eng, -1) < seq:
                b.r[eng] = seq
        for b in writes:
            b.w = (eng, seq)
            b.r = {}

    def dma(self, eng, fn, reads=(), writes=(), sem='misc'):
        reads = _bufs(reads)
        writes = _bufs(writes)
        waits = self._deps(eng, reads, writes)
        key = ('dma', sem)
        waits.pop(key, None)
        self.dma_cnt[key] = self.dma_cnt.get(key, 0) + 16
        val = self.dma_cnt[key]
        self.ops[eng].append(dict(fn=fn, waits=waits, dma=key))
        for b in reads:
            b.r[key] = val
        for b in writes:
            b.w = (key, val)
            b.r = {}

    def barrier(self):
        last = {}
        for e in ENGS:
            if self.ops[e]:
                for s in range(len(self.ops[e]) - 1, -1, -1):
                    o = self.ops[e][s]
                    if o['fn'] is not None and o['dma'] is None:
                        last[e] = s
                        break
        for k, v in self.dma_cnt.items():
            last[k] = v
        for e in ENGS:
            waits = {}
            for k, v in last.items():
                if k == e:
                    continue
                if v <= self.floor.get(k, -1):
                    continue
                waits[k] = v
                if k in self.waited:
                    self.waited[k].add(v)
            self.ops[e].append(dict(fn=None, waits=waits, dma=None))
        for k, v in last.items():
            self.floor[k] = v

    def build(self):
        nc, es = self.nc, self.es
        sems = {e: es.enter_context(nc.semaphore('s_' + e)) for e in ENGS}
        dsems = {k: es.enter_context(nc.semaphore('d_' + str(k[1]))) for k in self.dma_cnt}
        rank = {}
        for e in self.waited:
            for i, s in enumerate(sorted(self.waited[e])):
                rank[(e, s)] = i + 1
        block = es.enter_context(nc.Block())
        ops, waited = self.ops, self.waited

        def run(ename, e):
            for seq, o in enumerate(ops[ename]):
                for k, v in o['waits'].items():
                    if isinstance(k, tuple):
                        e.wait_ge(dsems[k], v)
                    else:
                        e.wait_ge(sems[k], rank[(k, v)])
                if o['fn'] is None:
                    continue
                ins = o['fn'](e)
                if o['dma'] is not None:
                    ins.then_inc(dsems[o['dma']], 16)
                elif seq in waited[ename]:
                    ins.then_inc(sems[ename], 1)

        @block.tensor
        def _(e):
            run('pe', e)

        @block.scalar
        def _(e):
            run('act', e)

        @block.vector
        def _(e):
            run('dve', e)

        @block.gpsimd
        def _(e):
            run('pool', e)

        @block.sync
        def _(e):
            run('sp', e)


class Arena:
    def __init__(self, nc, es, nbytes):
        self.t = es.enter_context(nc.sbuf_tensor("arena", [128, nbytes], U8))
        self.n = nbytes
        self.off = 0
        self.cnt = 0

    def mark(self):
        return self.off

    def release(self, m):
        self.off = m

    def alloc(self, shape, dt, name=None):
        esz = 4 if dt == F32 else 2
        n = int(np.prod(shape))
        nb = (n * esz + 63) // 64 * 64
        assert self.off + nb <= self.n, f"arena overflow {name} {self.off}+{nb}>{self.n}"
        ap = self.t[:, self.off:self.off + n * esz].bitcast(dt)
        self.off += nb
        if len(shape) == 2:
            ap = ap.rearrange("p (a b) -> p a b", a=shape[0])
        elif len(shape) == 3:
            ap = ap.rearrange("p (a b c) -> p a b c", a=shape[0], b=shape[1])
        self.cnt += 1
        return Tile(ap, name or f"t{self.cnt}")


PAR = {}
_o = 0
for _n, _w in [('g_ab', 8), ('g_ffn0', 8), ('g_ret', 8), ('g_ffn1', 8), ('convw', 16), ('convb', 4),
               ('ba', 4), ('bi', 4), ('lam', 4), ('qg', 1), ('kg', 1), ('sink', 8)]:
    PAR[_n] = (_o, _w)
    _o += _w
NPAR = _o

TAB = {}
_o = 0
for _n, _w in [('ident', 128), ('fm', 8), ('dk', 4), ('dkf', 64), ('dkm', 4),
               ('cf', 20), ('attc_m', 16), ('atts_m', 16), ('attc_h', 128), ('atts_h', 128),
               ('retc_m', 16), ('rets_m', 16), ('dt4', 512), ('dq4', 512)]:
    TAB[_n] = (_o, _w)
    _o += _w
NTAB = _o
NTABP = TAB['dt4'][0]
TAB2 = {}
_o = 0
for _n, _w in [('identb', 128), ('rperm', 128), ('bones', 128), ('mcur', 512), ('mprev', 512), ('mprev0', 512), ('mmeta', 64)]:
    TAB2[_n] = (_o, _w)
    _o += _w
NTAB2 = _o


def _host_tables(r):
    t = np.zeros((128, NTAB), np.float32)
    t2 = np.zeros((128, NTAB2), np.float32)

    def put(name, arr):
        arr = np.asarray(arr, np.float32)
        if name in TAB2:
            o, w = TAB2[name]
            assert arr.shape[1] == w, (name, arr.shape)
            t2[:arr.shape[0], o:o + w] = arr
        else:
            o, w = TAB[name]
            assert arr.shape[1] == w, (name, arr.shape)
            t[:arr.shape[0], o:o + w] = arr
    put('ident', np.eye(128))
    put('identb', np.eye(128))
    rp = np.zeros((128, 128))
    for m in range(128):
        d = m % 64
        if d < 8:
            rp[m + 8, m] = 1
        elif d < 16:
            rp[m - 8, m] = 1
    put('rperm', rp)
    bo = np.zeros((128, 128))
    bo[:64, :64] = 1
    bo[64:, 64:] = 1
    put('bones', bo)
    k = np.arange(128)[:, None]
    q = np.arange(128)[None, :]
    cur = (k <= q).astype(np.float32)
    prev = (k > q).astype(np.float32)
    put('mcur', np.tile(cur, (1, 4)))
    put('mprev', np.tile(prev, (1, 4)))
    put('mprev0', np.tile(prev if r > 0 else np.zeros_like(prev), (1, 4)))
    k16 = np.arange(16)[:, None]
    q16 = np.arange(16)[None, :]
    put('mmeta', np.tile((k16 <= q16).astype(np.float32), (1, 4)))
    fm = np.zeros((128, 8))
    for c in range(4):
        fm[:, c] = 1.0 if c < r else 0.0
        fm[:, 4 + c] = 1.0 - fm[:, c]
    put('fm', fm)
    lg = np.array(LOGG, np.float64)
    j = np.arange(128)[:, None]
    i = np.arange(128)[None, :]
    dt4 = np.concatenate([np.where(i >= j, np.exp(np.maximum(i - j, 0) * lg[h]), 0.0) / 16.0 for h in range(4)], 1)
    put('dt4', dt4)
    dq4 = np.concatenate([np.broadcast_to(np.exp((np.arange(128) + 1.0) * lg[h])[None, :], (128, 128)) for h in range(4)], 1)
    put('dq4', dq4)
    dk = np.stack([np.exp((127.0 - np.arange(128)) * lg[h]) / 16.0 for h in range(4)], 1)
    put('dk', dk)
    dkf = np.concatenate([dk * np.exp(128.0 * (15 - c) * lg)[None, :] for c in range(16)], 1)
    put('dkf', dkf)
    dkm = np.stack([np.exp((15.0 - np.arange(16)) * lg[h]) / 16.0 for h in range(4)], 1)
    put('dkm', dkm)
    cf = np.zeros((128, 20))
    for c in range(4):
        for h in range(4):
            cf[:, c * 4 + h] = np.exp(128.0 * 16 * (r - 1 - c) * lg[h]) if c < r else 0.0
    for h in range(4):
        cf[:, 16 + h] = np.exp(128.0 * 16 * r * lg[h])
    put('cf', cf)
    return t, t2


def _att_tables(pos):
    pos = np.asarray(pos, np.float32)
    inv = np.power(np.float32(500000.0), -np.arange(8, dtype=np.float32) / np.float32(8)).astype(np.float32)
    ang = (pos[:, None] * inv[None, :]).astype(np.float32)
    c = np.cos(ang).astype(np.float32).T
    s = np.sin(ang).astype(np.float32).T
    C = np.ones((128, len(pos)), np.float32)
    S = np.zeros((128, len(pos)), np.float32)
    for half in range(2):
        b = 64 * half
        C[b:b + 8] = c
        C[b + 8:b + 16] = c
        S[b:b + 8] = -s
        S[b + 8:b + 16] = s
    return C, S


def _ret_tables(pos):
    pos = np.asarray(pos, np.float32)
    inv = np.power(np.float32(10000.0), -np.arange(128, dtype=np.float32) / np.float32(128)).astype(np.float32)
    ang = (pos[:, None] * inv[None, :]).astype(np.float32)
    return np.cos(ang).astype(np.float32).T.copy(), np.sin(ang).astype(np.float32).T.copy()


class _Stop(Exception):
    pass


def build_program(dbg=(), stop=None):
    nc = bass.Bass("TRN2", target_bir_lowering=False)

    def din(name, shape, dt=F32):
        return nc.dram_tensor(name, list(shape), dt, kind="ExternalInput").ap()

    xin = din("xin", [NOWN + NMETA + NHALO, D])
    par_d = din("params", [128, NPAR])
    tab_d = din("tables", [128, NTAB])
    tab2_d = din("tables2", [128, NTAB2])
    attcs_d = din("attcs", [2, 128, NOWN])
    retcs_d = din("retcs", [2, 128, NOWN])
    w_in0 = din("ab_w_in", [D, 1792])
    w_out0 = din("ab_w_out", [D, D])
    wa_d = din("lru_w_a", [8, 64, 64])
    wi_d = din("lru_w_i", [8, 64, 64])
    w_in1 = din("ret_w_in", [D, 6144])
    w_out1 = din("ret_w_out", [2048, D])
    w_gu = din("ffn_w_gu", [2, D, 2 * DFF])
    w_dn = din("ffn_w_down", [2, DFF, D])
    out_d = nc.dram_tensor("out", [NOWN, D], F32, kind="ExternalOutput").ap()
    dbg_d = {n: nc.dram_tensor("dbg_" + n, list(s), F32, kind="ExternalOutput").ap() for n, s in dbg}

    cc1_in = nc.dram_tensor("cc1_in", [128, 8], F32)
    cc1_out = nc.dram_tensor("cc1_out", [512, 8], F32)
    cc2_in = nc.dram_tensor("cc2_in", [4, 128, 1024], BF16)
    cc2_out = nc.dram_tensor("cc2_out", [4, 512, 1024], BF16)
    scr_ab = nc.dram_tensor("scr_ab", [128, 2, 4, NCOL], F32)
    scr_sm = nc.dram_tensor("scr_sm", [128, 8, 512], F32)
    scr_kt = nc.dram_tensor("scr_kt", [128, 8, NOWN], BF16)
    scr_kd = nc.dram_tensor("scr_kd", [128, 16, 1024], BF16)
    scr_v = nc.dram_tensor("scr_v", [128, 16, 2048], BF16)
    scr_on = nc.dram_tensor("scr_on", [128, 16, 2048], BF16)

    es = ExitStack()
    with es:
        P = Prog(nc, es)
        A = Arena(nc, es, 207 * 1024)
        PS = [Tile(es.enter_context(nc.psum_tensor(f"ps{i}", [128, 512], F32))[:, :], f"ps{i}") for i in range(8)]
        for _p in PS:
            _p.b.excl = True
        PSR = Ring(PS)

        def mm(out, lhsT, rhs, start, stop, reads, writes):
            P.op('pe', lambda e: e.matmul(out, lhsT, rhs, start=start, stop=stop), reads, writes)

        def tr(out, in_, ident, reads, writes):
            P.op('pe', lambda e: e.transpose(out, in_, ident), reads, writes)

        def act(out, in_, func, reads, writes, **kw):
            P.op('act', lambda e: e.activation(out=out, in_=in_, func=func, **kw), reads, writes)

        def tt(eng, out, in0, in1, op, reads, writes):
            P.op(eng, lambda e: e.tensor_tensor(out=out, in0=in0, in1=in1, op=op), reads, writes)

        def ts(eng, out, in0, s1, s2, op0, op1, reads, writes):
            P.op(eng, lambda e: e.tensor_scalar(out=out, in0=in0, scalar1=s1, scalar2=s2, op0=op0, op1=op1), reads, writes)

        def stt(out, in0, scalar, in1, op0, op1, reads, writes):
            P.op('dve', lambda e: e.scalar_tensor_tensor(out=out, in0=in0, scalar=scalar, in1=in1, op0=op0, op1=op1), reads, writes)

        def cp(eng, out, in_, reads, writes):
            if eng == 'act':
                act(out, in_, ACT.Copy, reads, writes)
            else:
                P.op(eng, lambda e: e.tensor_copy(out=out, in_=in_), reads, writes)

        def memset(eng, out, val, writes):
            P.op(eng, lambda e: e.memset(out, val), (), writes)

        def dma(eng, out, in_, reads, writes, sem):
            P.dma(eng, lambda e: e.dma_start(out=out, in_=in_), reads, writes, sem)

        def recip(out, in_, reads, writes):
            P.op('dve', lambda e: e.reciprocal(out=out, in_=in_), reads, writes)

        def reduce_add(out, in_, reads, writes):
            P.op('dve', lambda e: e.tensor_reduce(out=out, in_=in_, axis=AX.X, op=ALU.add), reads, writes)

        def scan(out, a, b, init, reads, writes):
            P.op('dve', lambda e: e.tensor_tensor_scan(out=out, data0=a, data1=b, initial=init, op0=ALU.mult, op1=ALU.add),
                 reads, writes)

        hT = A.alloc([8, NCOL], F32, "hT")
        HB_ = [Tile(hT.ap, f"hT_b{i}") for i in range(NBLK + 1)]

        def hblk(c0):
            return HB_[NBLK] if c0 >= MOFF else HB_[c0 // BW]

        def hblks(c0, n):
            if c0 >= MOFF:
                return [HB_[NBLK]]
            return [HB_[i] for i in range(c0 // BW, (c0 + n + BW - 1) // BW)]
        par = A.alloc([NPAR], F32, "par")
        tab = A.alloc([NTABP], F32, "tab")
        identb = A.alloc([128], BF16, "identb")
        onesb = A.alloc([128], BF16, "onesb")
        epsc = A.alloc([1], F32, "epsc")

        def pcol(name, i=0, n=1):
            o, w = PAR[name]
            return par[:, o + i:o + i + n]

        def tcol(name, i=0, n=None):
            o, w = TAB[name]
            if n is None:
                n = w
            return tab[:, o + i:o + i + n]

        dma('sp', par.ap, par_d, [], [par], 'const')
        dma('sp', tab.ap, tab_d[:, 0:NTABP], [], [tab], 'const')
        dma('pool', identb.ap, tab2_d[:, TAB2['identb'][0]:TAB2['identb'][0] + 128], [], [identb], 'const2')
        memset('dve', onesb.ap, 1.0, [onesb])
        memset('dve', epsc.ap, EPS, [epsc])
        P.barrier()
        ident_f = tcol('ident')
        consts = [par, tab]

        def dump(name, ap, bufs):
            if name in dbg_d:
                dma('sp', dbg_d[name], ap, bufs, [Buf("dbg")], 'dbg')
                P.barrier()
            if stop == name:
                raise _Stop()

        def rmsnorm(src_ap_fn, nk, n, gain_name, out_t, out_ap_fn, srcbufs, F, H):
            ps = PSR.next()
            for k in range(nk):
                sq = H.next()
                act(sq[:, 0:n], src_ap_fn(k), ACT.Square, srcbufs, [sq])
                mm(ps[:, 0:n], onesb.ap, sq[:, 0:n], k == 0, k == nk - 1, [sq, onesb], [ps])
            sd = F.next()
            act(sd[:, 0:n], ps[:, 0:n], ACT.Ln, [ps, epsc], [sd], scale=1.0 / (nk * 128), bias=epsc[:, 0:1])
            rs = F.next()
            act(rs[:, 0:n], sd[:, 0:n], ACT.Exp, [sd], [rs], scale=-0.5)
            for k in range(nk):
                stt(out_ap_fn(k), src_ap_fn(k), pcol(gain_name, k), rs[:, 0:n], ALU.mult, ALU.mult,
                    srcbufs + [rs, par], [out_t])

        def load_w(dst_t, src_ap, kchunks, ncols, sem, eng='pool', dst_ap=None):
            if dst_ap is None:
                dst_ap = dst_t.ap
            step = max(1, 2048 // ncols)
            for k0 in range(0, kchunks, step):
                k1 = min(kchunks, k0 + step)
                dma(eng, dst_ap[:, k0:k1, 0:ncols], src_ap[k0 * 128:k1 * 128, :].rearrange("(k p) c -> p k c", p=128),
                    [], [dst_t], sem)

        try:
            m0 = A.mark()
            hTh = A.alloc([8, NHALO], F32, "hTh")
            rpermb = A.alloc([128], BF16, "rpermb")
            bonesb = A.alloc([128], BF16, "bonesb")
            mcur = A.alloc([512], BF16, "mcur")
            mprev = A.alloc([512], BF16, "mprev")
            mprev0 = A.alloc([512], BF16, "mprev0")
            mmeta = A.alloc([64], BF16, "mmeta")
            for nm, tl in [('rperm', rpermb), ('bones', bonesb), ('mcur', mcur), ('mprev', mprev),
                           ('mprev0', mprev0), ('mmeta', mmeta)]:
                o2, w2 = TAB2[nm]
                dma('pool', tl.ap, tab2_d[:, o2:o2 + w2], [], [tl], 'const2')
            w0 = A.alloc([8, 1792], BF16, "w_in0")
            wo0 = A.alloc([8, D], BF16, "w_out0")
            load_w(w0, w_in0[:, 0:1024], 8, 1024, 'w0', dst_ap=w0[:, :, 0:1024])
            load_w(w0, w_in0[:, 1536:1792], 8, 256, 'w0', dst_ap=w0[:, :, 1536:1792])
            for k in range(8):
                for s_ in range(2):
                    dma('pool', w0[:, k, 1024:1536].rearrange("p (j s d) -> p j s d", j=4, s=2, d=64)[:, :, s_, :],
                        w_in0[k * 128:(k + 1) * 128, 1024 + 256 * s_:1024 + 256 * s_ + 256].rearrange("p (j d) -> p j d", d=64),
                        [], [w0], 'w0')
            load_w(wo0, w_out0[0:512, :], 4, D, 'wo0')
            for s in range(2):
                dma('pool', wo0[64 * s:64 * s + 64, 4:8, :],
                    w_out0[512 + 256 * s:512 + 256 * s + 256, :].rearrange("(j d) c -> d j c", d=64), [], [wo0], 'wo0')
            wab = A.alloc([4, 128], BF16, "wab")
            wib = A.alloc([4, 128], BF16, "wib")
            memset('dve', wab.ap, 0.0, [wab])
            memset('dve', wib.ap, 0.0, [wib])
            for m in range(4):
                for a in range(2):
                    dma('pool', wab[64 * a:64 * a + 64, m, 64 * a:64 * a + 64], wa_d[2 * m + a], [], [wab], 'wab')
                    dma('pool', wib[64 * a:64 * a + 64, m, 64 * a:64 * a + 64], wi_d[2 * m + a], [], [wib], 'wab')
            lsc = A.alloc([8], F32, "lsc")
            ltmp = A.alloc([4], F32, "ltmp")
            act(ltmp.ap, pcol('lam', 0, 4), ACT.Exp, [par], [ltmp], scale=-1.0)
            act(ltmp.ap, ltmp.ap, ACT.Ln, [ltmp], [ltmp], bias=1.0)
            ts('dve', lsc[:, 0:4], ltmp.ap, -8.0, None, ALU.mult, ALU.bypass, [ltmp], [lsc])
            ts('dve', lsc[:, 4:8], ltmp.ap, -16.0, None, ALU.mult, ALU.bypass, [ltmp], [lsc])
            sexp = A.alloc([8], F32, "sexp")
            act(sexp.ap, pcol('sink', 0, 8), ACT.Exp, [par], [sexp])
            sink4 = A.alloc([2, 512], F32, "sink4")
            onesf = A.alloc([128], F32, "onesf")
            memset('dve', onesf.ap, 1.0, [onesf])
            for g in range(2):
                for j in range(4):
                    ts('dve', sink4[:, g, 128 * j:128 * j + 128], onesf.ap, sexp[:, 4 * g + j:4 * g + j + 1], None,
                       ALU.mult, ALU.bypass, [onesf, sexp], [sink4])
            mxs = A.mark()
            xs = Ring([A.alloc([D], F32, f"xs{i}") for i in range(2)])
            rows = [(c * 128, 128, ('own', c * 128)) for c in range(16)] + [(NOWN, NMETA, ('meta', 0)), (NOWN + NMETA, NHALO, ('halo', 0))]
            for ri, (r0, nr, (kind, c0)) in enumerate(rows):
                xt = xs.next()
                dma('sp', xt[0:nr, :], xin[r0:r0 + nr, :], [], [xt], f'xs{ri % 2}')
                for g in range(2):
                    ps = PSR.next()
                    for kk in range(4):
                        k = g * 4 + kk
                        tr(ps[:, kk * 128:kk * 128 + nr], xt[0:nr, k * 128:(k + 1) * 128], ident_f[0:nr, 0:nr], [xt, tab], [ps])
                    src = ps.ap.rearrange("p (a b) -> p a b", a=4)[:, :, 0:nr]
                    if kind == 'own':
                        cp('act' if g == 0 else 'dve', hT[:, g * 4:g * 4 + 4, c0:c0 + nr], src, [ps], [hblk(c0)])
                    elif kind == 'meta':
                        cp('act' if g == 0 else 'dve', hT[:, g * 4:g * 4 + 4, MOFF:MOFF + nr], src, [ps], [hblk(MOFF)])
                    else:
                        cp('act' if g == 0 else 'dve', hTh[:, g * 4:g * 4 + 4, 0:nr], src, [ps], [hTh])

            P.barrier()
            A.release(mxs)
            dump('x0', hT.ap, HB_)
            P.barrier()

            hcar = A.alloc([4], F32, "hcar")
            sumr = A.alloc([4], F32, "sumr")
            hmeta = A.alloc([4], F32, "hmeta")
            red = A.alloc([4], F32, "red")
            xhist = A.alloc([4, 3], F32, "xhist")
            pay = A.alloc([8], F32, "pay")
            xg = A.alloc([4, 8], F32, "xg")
            hin = A.alloc([4], F32, "hin")
            ftmp = A.alloc([8], F32, "ftmp")
            sk_ab = Buf("sk_ab")
            mp1 = A.mark()
            BW1 = 512
            hnr = [A.alloc([8, BW1], BF16, f"hn0{i}") for i in range(2)]
            Fn = Ring([A.alloc([BW1], F32, f"Fn{i}") for i in range(2)])
            Hn = Ring([A.alloc([BW1], BF16, f"Hn{i}") for i in range(2)])
            xre = A.alloc([4, 3 + BW1], F32, "xre")
            F = Ring([A.alloc([BW1], F32, f"F{i}") for i in range(20)])
            H = Ring([A.alloc([BW1], BF16, f"H{i}") for i in range(4)])

            def l0_src(kind, c0):
                if kind == 'halo':
                    return (lambda k: hTh[:, k, :]), [hTh]
                return (lambda k: hT[:, k, c0:c0 + (NMETA if kind == 'meta' else BW)]), [hblk(c0)]

            def proj_fm(wt, k_n, wcols_fn, rhs_fn, n, rbufs):
                ps = PSR.next()
                for k in range(k_n):
                    mm(ps[:, 0:n], wcols_fn(k), rhs_fn(k), k == 0, k == k_n - 1, [wt] + rbufs, [ps])
                return ps

            def lru_pass1(kind, c0, n, prefetch):
                for m in range(4):
                    ps = proj_fm(w0, 8, lambda k: w0[:, k, 128 * m:128 * m + 128], lambda k: hn[:, k, 0:n], n, [hn])
                    cp('act', xre[:, m, 3:3 + n], ps[:, 0:n], [ps], [xre])
                if kind == 'halo':
                    prefetch()
                    return
                xcs, xcbs, prs, pis, rrs, iis, aas, mus = [], [], [], [], [], [], [], []
                for m in range(4):
                    xc = F.next()
                    act(xc[:, 0:n], xre[:, m, 0:n], ACT.Identity, [xre, par], [xc],
                        scale=pcol('convw', 4 * m + 0), bias=pcol('convb', m))
                    for j in range(1, 4):
                        stt(xc[:, 0:n], xre[:, m, j:j + n], pcol('convw', 4 * m + j), xc[:, 0:n], ALU.mult, ALU.add,
                            [xre, par, xc], [xc])
                    xcs.append(xc)
                for m in range(4):
                    xcb = H.next()
                    cp('act', xcb[:, 0:n], xcs[m][:, 0:n], [xcs[m]], [xcb])
                    xcbs.append(xcb)
                for m in range(4):
                    if m == 2:
                        for mm_ in range(2):
                            rr = F.next()
                            act(rr[:, 0:n], prs[mm_][:, 0:n], ACT.Sigmoid, [prs[mm_], par], [rr], bias=pcol('ba', mm_))
                            ii = F.next()
                            act(ii[:, 0:n], pis[mm_][:, 0:n], ACT.Sigmoid, [pis[mm_], par], [ii], bias=pcol('bi', mm_))
                            rrs.append(rr)
                            iis.append(ii)
                    pr = PSR.next()
                    mm(pr[:, 0:n], wab[:, m, :], xcbs[m][:, 0:n], True, True, [wab, xcbs[m]], [pr])
                    pi = PSR.next()
                    mm(pi[:, 0:n], wib[:, m, :], xcbs[m][:, 0:n], True, True, [wib, xcbs[m]], [pi])
                    prs.append(pr)
                    pis.append(pi)
                for mm_ in range(2, 4):
                    rr = F.next()
                    act(rr[:, 0:n], prs[mm_][:, 0:n], ACT.Sigmoid, [prs[mm_], par], [rr], bias=pcol('ba', mm_))
                    ii = F.next()
                    act(ii[:, 0:n], pis[mm_][:, 0:n], ACT.Sigmoid, [pis[mm_], par], [ii], bias=pcol('bi', mm_))
                    rrs.append(rr)
                    iis.append(ii)
                prefetch()
                if kind == 'own':
                    for m in range(4):
                        reduce_add(red[:, m:m + 1], rrs[m][:, 0:n], [rrs[m]], [red])
                    tt('dve', sumr.ap, sumr.ap, red.ap, ALU.add, [sumr, red], [sumr])
                for m in range(4):
                    aa = F.next()
                    act(aa[:, 0:n], rrs[m][:, 0:n], ACT.Exp, [rrs[m], lsc], [aa], scale=lsc[:, m:m + 1])
                    mu = F.next()
                    act(mu[:, 0:n], rrs[m][:, 0:n], ACT.Exp, [rrs[m], lsc], [mu], scale=lsc[:, 4 + m:5 + m])
                    aas.append(aa)
                    mus.append(mu)
                for m in range(4):
                    act(mus[m][:, 0:n], mus[m][:, 0:n], ACT.Sqrt, [mus[m]], [mus[m]], scale=-1.0, bias=1.0)
                for m in range(4):
                    mu = mus[m]
                    if kind == 'meta':
                        memset('dve', mu[:, 0:1], 1.0, [mu])
                    tt('dve', mu[:, 0:n], mu[:, 0:n], iis[m][:, 0:n], ALU.mult, [mu, iis[m]], [mu])
                    tt('dve', mu[:, 0:n], mu[:, 0:n], xcs[m][:, 0:n], ALU.mult, [mu, xcs[m]], [mu])
                    dma('sp', scr_ab.ap()[:, 0, m, c0:c0 + n], aas[m][:, 0:n], [aas[m]], [sk_ab], f'sa{m}')
                    dma('sp', scr_ab.ap()[:, 1, m, c0:c0 + n], mu[:, 0:n], [mu], [sk_ab], f'sb{m}')
                for m in range(4):
                    hh = F.next()
                    scan(hh[:, 0:n], aas[m][:, 0:n], mus[m][:, 0:n], hcar[:, m:m + 1], [aas[m], mus[m], hcar], [hh])
                    cp('dve', hcar[:, m:m + 1], hh[:, n - 1:n], [hh], [hcar])
                cp('dve', xhist.ap, xre[:, :, n:n + 3], [xre], [xhist])

            blocks_p1 = [('meta', MOFF, NMETA), ('halo', 0, NHALO)] + [('own', b * BW1, BW1) for b in range(NOWN // BW1)]
            memset('dve', sumr.ap, 0.0, [sumr])
            def norm_p1(i):
                if i >= len(blocks_p1):
                    return
                kind, c0, n = blocks_p1[i]
                ht = hnr[i % 2]
                if kind == 'halo':
                    srcf, sb = (lambda k: hTh[:, k, :]), [hTh]
                elif n == BW1:
                    srcf, sb = (lambda k: hT[:, k, c0:c0 + n]), [hblk(c0), hblk(c0 + BW)]
                else:
                    srcf, sb = (lambda k: hT[:, k, c0:c0 + n]), [hblk(c0)]
                rmsnorm(srcf, 8, n, 'g_ab', ht, lambda k: ht[:, k, 0:n], sb, Fn, Hn)
            norm_p1(0)
            for bi, (kind, c0, n) in enumerate(blocks_p1):
                hn = hnr[bi % 2]
                if kind == 'meta':
                    memset('dve', xre[:, :, 0:3], 0.0, [xre])
                    memset('dve', hcar.ap, 0.0, [hcar])
                elif kind == 'own':
                    cp('dve', xre[:, :, 0:3], xhist.ap, [xhist], [xre])
                lru_pass1(kind, c0, n, lambda: norm_p1(bi + 1))
                if kind == 'meta':
                    cp('dve', hmeta.ap, hcar.ap, [hcar], [hmeta])
                if kind == 'halo':
                    cp('dve', xhist.ap, xre[:, :, n:n + 3], [xre], [xhist])
                    memset('dve', hcar.ap, 0.0, [hcar])
            for m in range(4):
                act(pay[:, m:m + 1], sumr[:, m:m + 1], ACT.Exp, [sumr, lsc], [pay], scale=lsc[:, m:m + 1])
            cp('dve', pay[:, 4:8], hcar.ap, [hcar], [pay])
            cc1i = Buf("cc1i")
            cc1o = Buf("cc1o")
            dma('pool', cc1_in.ap(), pay.ap, [pay], [cc1i], 'cc1')
            P.op('pool', lambda e: e.collective_compute("AllGather", ALU.bypass, replica_groups=[[0, 1, 2, 3], [4, 5, 6, 7]],
                                                        ins=[cc1_in.ap().opt()], outs=[cc1_out.ap().opt()]), [cc1i], [cc1o])
            dma('pool', xg.ap, cc1_out.ap().rearrange("(r p) f -> p r f", p=128), [cc1o], [xg], 'cc1b')
            cp('dve', hin.ap, hmeta.ap, [hmeta], [hin])
            for c in range(4):
                ts('dve', ftmp[:, 0:4], xg[:, c, 0:4], tcol('fm', c, 1), tcol('fm', 4 + c, 1), ALU.mult, ALU.add, [xg, tab], [ftmp])
                ts('dve', ftmp[:, 4:8], xg[:, c, 4:8], tcol('fm', c, 1), None, ALU.mult, ALU.bypass, [xg, tab], [ftmp])
                tt('dve', hin.ap, hin.ap, ftmp[:, 0:4], ALU.mult, [hin, ftmp], [hin])
                tt('dve', hin.ap, hin.ap, ftmp[:, 4:8], ALU.add, [hin, ftmp], [hin])

            P.barrier()
            A.release(mp1)
            hnr = [A.alloc([8, BW], BF16, f"hn0b{i}") for i in range(2)]
            Fn = Ring([A.alloc([BW], F32, f"Fnb{i}") for i in range(2)])
            Hn = Ring([A.alloc([BW], BF16, f"Hnb{i}") for i in range(2)])
            yb = A.alloc([8, BW], BF16, "yb")
            QT = A.alloc([4, BW], BF16, "QT")
            KT = A.alloc([NOWN + NMETA + NHALO], BF16, "KT")
            Vt = A.alloc([18, 128], BF16, "Vt")
            atc = A.alloc([BW], F32, "atc")
            ats = A.alloc([BW], F32, "ats")
            F = Ring([A.alloc([BW], F32, f"F2{i}") for i in range(12)])
            H = Ring([A.alloc([BW], BF16, f"H2{i}") for i in range(8)])
            PT = Ring([A.alloc([512], BF16, f"PT{i}") for i in range(12)])
            FA = Ring([A.alloc([512], F32, f"FA{i}") for i in range(4)])
            abr = Ring([A.alloc([2, 4, BW], F32, f"ab{i}") for i in range(2)])
            dump('p1', hT.ap, HB_ + [hin])
            def qk_norm_rope_multi(items, n, ctab, stab, tabbufs):
                sqs, p2s, rss, qns, p3s = [], [], [], [], []
                for (ps, gname, out_ap, out_t) in items:
                    sq = H.next()
                    act(sq[:, 0:n], ps[:, 0:n], ACT.Square, [ps], [sq])
                    sqs.append(sq)
                for i, (ps, gname, out_ap, out_t) in enumerate(items):
                    p2 = PSR.next()
                    mm(p2[:, 0:n], bonesb.ap, sqs[i][:, 0:n], True, True, [bonesb, sqs[i]], [p2])
                    p2s.append(p2)
                for i in range(len(items)):
                    sd = F.next()
                    act(sd[:, 0:n], p2s[i][:, 0:n], ACT.Ln, [p2s[i], epsc], [sd], scale=1.0 / 64, bias=epsc[:, 0:1])
                    act(sd[:, 0:n], sd[:, 0:n], ACT.Exp, [sd], [sd], scale=-0.5)
                    rss.append(sd)
                for i, (ps, gname, out_ap, out_t) in enumerate(items):
                    qn = H.next()
                    stt(qn[:, 0:n], ps[:, 0:n], pcol(gname), rss[i][:, 0:n], ALU.mult, ALU.mult, [ps, rss[i], par], [qn])
                    qns.append(qn)
                for i in range(len(items)):
                    p3 = PSR.next()
                    mm(p3[:, 0:n], rpermb.ap, qns[i][:, 0:n], True, True, [rpermb, qns[i]], [p3])
                    p3s.append(p3)
                for i, (ps, gname, out_ap, out_t) in enumerate(items):
                    t1 = F.next()
                    tt('dve', t1[:, 0:n], p3s[i][:, 0:n], stab, ALU.mult, [p3s[i]] + tabbufs, [t1])
                    t2 = F.next()
                    tt('pool' if i % 2 == 0 else 'dve', t2[:, 0:n], qns[i][:, 0:n], ctab, ALU.mult, [qns[i]] + tabbufs, [t2])
                    tt('dve', out_ap, t1[:, 0:n], t2[:, 0:n], ALU.add, [t1, t2], [out_t])

            def qk_norm_rope(ps, n, gain_name, ctab, stab, tabbufs, out_ap, out_t):
                qk_norm_rope_multi([(ps, gain_name, out_ap, out_t)], n, ctab, stab, tabbufs)

            def attn_front(qn_cols, q0, kblocks):
                nq4 = 4 * qn_cols
                sps = []
                for g in range(2):
                    for (kc0, nk, vi, mk) in kblocks:
                        ps = PSR.next()
                        mm(ps.ap[0:nk, 0:nq4].rearrange("p (j q) -> p j q", j=4), KT[64 * g:64 * g + 64, kc0:kc0 + nk],
                           QT[64 * g:64 * g + 64, :, q0:q0 + qn_cols], True, True, [KT, QT], [ps])
                        sps.append((g, ps, nk, vi, mk))
                pts = {0: [], 1: []}
                for (g, ps, nk, vi, mk) in sps:
                    pt = PT.next()
                    act(pt[0:nk, 0:nq4], ps[0:nk, 0:nq4], ACT.Exp, [ps], [pt], scale=0.125)
                    pts[g].append((pt, nk, vi, mk))
                for g in range(2):
                    for (pt, nk, vi, mk) in pts[g]:
                        if mk is not None:
                            tt('pool' if g == 0 else 'dve', pt[0:nk, 0:nq4], pt[0:nk, 0:nq4], mk[0:nk, 0:nq4], ALU.mult, [pt, mk], [pt])
                return pts

            def attn_back(qn_cols, pts, ycols0):
                nq4 = 4 * qn_cols
                outs = []
                for g in range(2):
                    pv = PSR.next()
                    pdn = PSR.next()
                    for i, (pt, nk, vi, mk) in enumerate(pts[g]):
                        mm(pv[:, 0:nq4], Vt[0:nk, vi, :], pt[0:nk, 0:nq4], i == 0, i == len(pts[g]) - 1, [Vt, pt], [pv])
                    for i, (pt, nk, vi, mk) in enumerate(pts[g]):
                        mm(pdn[:, 0:nq4], onesb[0:nk, :], pt[0:nk, 0:nq4], i == 0, False, [onesb, pt], [pdn])
                    mm(pdn.ap[:, 0:nq4].rearrange("p (j q) -> p j q", j=4), onesf[0:1, :],
                       sink4[0:1, g, :].rearrange("p (j q) -> p j q", j=4)[:, :, 0:qn_cols], False, True, [onesf, sink4], [pdn])
                    outs.append((pv, pdn))
                for g in range(2):
                    pv, pdn = outs[g]
                    dn = FA.next()
                    act(dn[:, 0:nq4], pdn[:, 0:nq4], ACT.Ln, [pdn], [dn])
                    rd = FA.next()
                    act(rd[:, 0:nq4], dn[:, 0:nq4], ACT.Exp, [dn], [rd], scale=-1.0)
                    tt('dve', yb[64 * g:64 * g + 64, 4:8, ycols0:ycols0 + qn_cols],
                       pv.ap[64 * g:64 * g + 64, 0:nq4].rearrange("p (j q) -> p j q", j=4),
                       rd.ap[64 * g:64 * g + 64, 0:nq4].rearrange("p (j q) -> p j q", j=4), ALU.mult, [pv, rd], [yb])

            def attention(qn_cols, q0, kblocks, ycols0):
                attn_back(qn_cols, attn_front(qn_cols, q0, kblocks), ycols0)

            wq = w0[:, :, 1024:1536].rearrange("p k (j m) -> p k j m", j=4, m=128)
            blocks0 = [('meta', MOFF, NMETA), ('halo', 0, NHALO)] + [('own', b * BW, BW) for b in range(NBLK)]
            def norm_p2(i):
                if i >= len(blocks0):
                    return
                kind, c0, n = blocks0[i]
                ht = hnr[i % 2]
                srcf, sb = l0_src(kind, c0)
                rmsnorm(srcf, 8, n, 'g_ab', ht, lambda k: ht[:, k, 0:n], sb, Fn, Hn)
            norm_p2(0)
            for bi, (kind, c0, n) in enumerate(blocks0):
                hn = hnr[bi % 2]
                if kind == 'own':
                    dma('sp', atc[:, 0:n], attcs_d[0, :, c0:c0 + n], [], [atc], 'atc')
                    dma('sp', ats[:, 0:n], attcs_d[1, :, c0:c0 + n], [], [ats], 'atc')
                    ctab, stab, tbufs = atc[:, 0:n], ats[:, 0:n], [atc, ats]
                elif kind == 'meta':
                    ctab, stab, tbufs = tcol('attc_m'), tcol('atts_m'), [tab]
                else:
                    ctab, stab, tbufs = tcol('attc_h'), tcol('atts_h'), [tab]
                kcol0 = {'own': c0, 'meta': NOWN, 'halo': NOWN + NMETA}[kind]
                ps = proj_fm(w0, 8, lambda k: w0[:, k, 1536:1664], lambda k: hn[:, k, 0:n], n, [hn])
                qk_norm_rope(ps, n, 'kg', ctab, stab, tbufs, KT[:, kcol0:kcol0 + n], KT)
                nch = max(1, n // 128)
                for ci in range(nch):
                    nt = min(128, n)
                    vi = {'own': c0 // 128 + ci, 'meta': 16, 'halo': 17}[kind]
                    ps = PSR.next()
                    for k in range(8):
                        mm(ps[0:nt, 0:128], hn[:, k, ci * 128:ci * 128 + nt], w0[:, k, 1664:1792], k == 0, k == 7, [hn, w0], [ps])
                    cp('act', Vt[0:nt, vi, :], ps[0:nt, 0:128], [ps], [Vt])
                if kind == 'halo':
                    cp('dve', hcar.ap, hin.ap, [hin], [hcar])
                    norm_p2(bi + 1)
                    continue
                if kind == 'meta':
                    memset('dve', hcar.ap, 0.0, [hcar])
                absem = f'lab{abr.i % 2}'
                ab = abr.next()
                dma('sp', ab[:, :, :, 0:n], scr_ab.ap()[:, :, :, c0:c0 + n], [sk_ab], [ab], absem)
                for m in range(4):
                    ps = proj_fm(w0, 8, lambda k: w0[:, k, 512 + 128 * m:512 + 128 * m + 128], lambda k: hn[:, k, 0:n], n, [hn])
                    act(yb[:, m, 0:n], ps[:, 0:n], ACT.Gelu_apprx_tanh, [ps], [yb])
                for m in range(4):
                    hh = F.next()
                    scan(hh[:, 0:n], ab[:, 0, m, 0:n], ab[:, 1, m, 0:n], hcar[:, m:m + 1], [ab, hcar], [hh])
                    cp('dve', hcar[:, m:m + 1], hh[:, n - 1:n], [hh], [hcar])
                    tt('pool', yb[:, m, 0:n], yb[:, m, 0:n], hh[:, 0:n], ALU.mult, [yb, hh], [yb])
                qitems = []
                for j in range(4):
                    ps = proj_fm(w0, 8, lambda k: wq[:, k, j], lambda k: hn[:, k, 0:n], n, [hn])
                    qitems.append((ps, 'qg', QT[:, j, 0:n], QT))
                qk_norm_rope_multi(qitems, n, ctab, stab, tbufs)
                norm_p2(bi + 1)
                if kind == 'meta':
                    attention(NMETA, 0, [(NOWN, NMETA, 16, mmeta)], 0)
                else:
                    fronts = []
                    for ci in range(n // 128):
                        ch = c0 // 128 + ci
                        if ch == 0:
                            prevb = (NOWN + NMETA, 128, 17, mprev0)
                        else:
                            prevb = ((ch - 1) * 128, 128, ch - 1, mprev)
                        fronts.append(attn_front(128, ci * 128, [(NOWN, NMETA, 16, None), prevb, (ch * 128, 128, ch, mcur)]))
                    for ci in range(n // 128):
                        attn_back(128, fronts[ci], ci * 128)
                for mo in range(8):
                    ps = proj_fm(wo0, 8, lambda k: wo0[:, k, 128 * mo:128 * mo + 128], lambda k: yb[:, k, 0:n], n, [yb])
                    tt('dve', hT[:, mo, c0:c0 + n], hT[:, mo, c0:c0 + n], ps[:, 0:n], ALU.add, [hblk(c0), ps], [hblk(c0)])
            P.barrier()
            A.release(m0)

            dump('h1', hT.ap, HB_)

            FCB = [(0, 512), (512, 512), (1024, 512), (1536, 512), (MOFF, NMETA)]

            def ffn(layer, gain_name, with_meta):
                m1 = A.mark()
                cbs = FCB if with_meta else FCB[:4]
                ncol = NCOL if with_meta else NOWN
                hnf = A.alloc([8, NCOL], BF16, "hnf")
                Ff = Ring([A.alloc([512], F32, f"Ff{i}") for i in range(2)])
                Hf = Ring([A.alloc([512], BF16, f"Hf{i}") for i in range(2)])
                hnfB = {c0: Tile(hnf.ap, f"hnf_{c0}") for (c0, n) in cbs}
                normed = set()

                def need_norm(c0, n):
                    if c0 in normed:
                        return
                    normed.add(c0)
                    rmsnorm(lambda k: hT[:, k, c0:c0 + n], 8, n, gain_name, hnfB[c0], lambda k: hnf[:, k, c0:c0 + n],
                            [hblk(c0)] if n < 512 else [hblk(c0), hblk(c0 + BW)], Ff, Hf)
                for (c0_, n_) in cbs:
                    need_norm(c0_, n_)
                actb = Ring([A.alloc([4, NCOL], BF16, f"actb{i}") for i in range(2)])
                wg = Ring([A.alloc([8, 512], BF16, f"wg{i}") for i in range(2)])
                wu = Ring([A.alloc([8, 512], BF16, f"wu{i}") for i in range(2)])
                wd = Ring([A.alloc([4, D], BF16, f"wd{i}") for i in range(2)])
                sgr = Ring([A.alloc([512], BF16, f"sg{i}") for i in range(2)])
                for q in range(6):
                    nt = 4 if q < 5 else 2
                    wgt, wut, wdt, ab = wg.next(), wu.next(), wd.next(), actb.next()
                    load_w(wgt, w_gu[layer, :, 512 * q:512 * q + 128 * nt], 8, 128 * nt, f'wg{q % 2}')
                    load_w(wut, w_gu[layer, :, DFF + 512 * q:DFF + 512 * q + 128 * nt], 8, 128 * nt, f'wu{q % 2}')
                    for t0 in range(0, nt, 2):
                        dma('pool', wdt[:, t0:t0 + 2, :], w_dn[layer, 512 * q + 128 * t0:512 * q + 128 * (t0 + 2), :].rearrange("(k p) c -> p k c", p=128),
                            [], [wdt], f'wd{q % 2}')
                    for t in range(nt):
                        for (c0, n) in cbs:
                            need_norm(c0, n)
                            pg = PSR.next()
                            for k in range(8):
                                mm(pg[:, 0:n], wgt[:, k, 128 * t:128 * t + 128], hnf[:, k, c0:c0 + n], k == 0, k == 7, [wgt, hnfB[c0]], [pg])
                            pu = PSR.next()
                            for k in range(8):
                                mm(pu[:, 0:n], wut[:, k, 128 * t:128 * t + 128], hnf[:, k, c0:c0 + n], k == 0, k == 7, [wut, hnfB[c0]], [pu])
                            sg = sgr.next()
                            act(sg[:, 0:n], pg[:, 0:n], ACT.Silu, [pg], [sg])
                            tt('dve', ab[:, t, c0:c0 + n], sg[:, 0:n], pu[:, 0:n], ALU.mult, [sg, pu], [ab])
                    for mo in range(8):
                        for (c0, n) in cbs:
                            ps = PSR.next()
                            for t in range(nt):
                                mm(ps[:, 0:n], wdt[:, t, 128 * mo:128 * mo + 128], ab[:, t, c0:c0 + n], t == 0, t == nt - 1, [wdt, ab], [ps])
                            hb = [hblk(c0)] if n < 512 else [hblk(c0), hblk(c0 + BW)]
                            tt('dve', hT[:, mo, c0:c0 + n], hT[:, mo, c0:c0 + n], ps[:, 0:n], ALU.add, hb + [ps], hb)
                P.barrier()
                A.release(m1)

            ffn(0, 'g_ffn0', True)
            dump('h2', hT.ap, HB_)

            m2 = A.mark()
            Sacc = A.alloc([8, 512], F32, "Sacc")
            mA = A.mark()
            sk_sm = Buf("sk_sm")
            mA2 = A.mark()
            wk = A.alloc([8, 1024], BF16, "wk")
            wv = A.alloc([8, 2048], BF16, "wv")
            load_w(wk, w_in1[:, 1024:2048], 8, 1024, 'wk')
            load_w(wv, w_in1[:, 2048:4096], 8, 2048, 'wv')
            BWA = 512
            hnr = [A.alloc([8, BWA], BF16, f"hn1{i}") for i in range(2)]
            Fn = Ring([A.alloc([BWA], F32, f"FnA{i}") for i in range(2)])
            Hn = Ring([A.alloc([BWA], BF16, f"HnA{i}") for i in range(2)])
            F = Ring([A.alloc([BWA], F32, f"G{i}") for i in range(6)])
            smst = Ring([A.alloc([512], F32, f"smst{i}") for i in range(2)])
            H = Ring([A.alloc([BWA], BF16, f"I{i}") for i in range(4)])
            KTb = A.alloc([8, BWA], BF16, "KTb")
            rc = A.alloc([BWA], F32, "rc")
            rs_ = A.alloc([BWA], F32, "rs")
            Kd = Ring([A.alloc([1024], BF16, f"Kd{i}") for i in range(2)])
            Kdf = Ring([A.alloc([1024], BF16, f"Kdf{i}") for i in range(2)])
            Vc = Ring([A.alloc([2048], BF16, f"Vc{i}") for i in range(2)])
            memset('pool', Sacc.ap, 0.0, [Sacc])
            sk_kt, sk_kd, sk_v, sk_on = Buf("skt"), Buf("skd"), Buf("sv"), Buf("son")

            def rope_ret(psa, psb, n, ctab, stab, tbufs, outa, outb, out_t):
                x1 = F.next()
                cp('act', x1[:, 0:n], psa[:, 0:n], [psa], [x1])
                x2 = F.next()
                cp('act', x2[:, 0:n], psb[:, 0:n], [psb], [x2])
                t1, t2, t3, t4 = F.next(), F.next(), F.next(), F.next()
                tt('dve', t1[:, 0:n], x1[:, 0:n], ctab, ALU.mult, [x1] + tbufs, [t1])
                tt('dve', t2[:, 0:n], x2[:, 0:n], stab, ALU.mult, [x2] + tbufs, [t2])
                tt('dve', t3[:, 0:n], x2[:, 0:n], ctab, ALU.mult, [x2] + tbufs, [t3])
                tt('dve', t4[:, 0:n], x1[:, 0:n], stab, ALU.mult, [x1] + tbufs, [t4])
                tt('dve', outa, t1[:, 0:n], t2[:, 0:n], ALU.subtract, [t1, t2], [out_t])
                tt('dve', outb, t3[:, 0:n], t4[:, 0:n], ALU.add, [t3, t4], [out_t])

            blocks1 = [('meta', MOFF, NMETA)] + [('own', b * BWA, BWA) for b in range(NOWN // BWA)]
            def norm_A(i):
                if i >= len(blocks1):
                    return
                kind, c0, n = blocks1[i]
                ht = hnr[i % 2]
                rmsnorm(lambda k: hT[:, k, c0:c0 + n], 8, n, 'g_ret', ht, lambda k: ht[:, k, 0:n], hblks(c0, n), Fn, Hn)
            norm_A(0)
            for bi, (kind, c0, n) in enumerate(blocks1):
                hn = hnr[bi % 2]
                if kind == 'own':
                    dma('sp', rc[:, 0:n], retcs_d[0, :, c0:c0 + n], [], [rc], 'rc')
                    dma('sp', rs_[:, 0:n], retcs_d[1, :, c0:c0 + n], [], [rs_], 'rc')
                    ctab, stab, tbufs = rc[:, 0:n], rs_[:, 0:n], [rc, rs_]
                else:
                    ctab, stab, tbufs = tcol('retc_m'), tcol('rets_m'), [tab]
                for h in range(4):
                    pa = proj_fm(wk, 8, lambda k: wk[:, k, 256 * h:256 * h + 128], lambda k: hn[:, k, 0:n], n, [hn])
                    pb = proj_fm(wk, 8, lambda k: wk[:, k, 256 * h + 128:256 * h + 256], lambda k: hn[:, k, 0:n], n, [hn])
                    rope_ret(pa, pb, n, ctab, stab, tbufs, KTb[:, 2 * h, 0:n], KTb[:, 2 * h + 1, 0:n], KTb)
                if kind == 'own':
                    dma('sp', scr_kt.ap()[:, :, c0:c0 + n], KTb[:, :, 0:n], [KTb], [sk_kt], 'skt')
                norm_A(bi + 1)
                nt = min(128, n)
                for ci in range(max(1, n // 128)):
                    ch = c0 // 128 + ci
                    ps = PSR.next()
                    psb16 = ps.ap.bitcast(BF16)
                    for f in range(8):
                        tr(psb16[0:nt, 128 * f:128 * f + 128], KTb[:, f, ci * 128:ci * 128 + nt], identb.ap, [KTb, identb], [ps])
                    kd, kdf = Kd.next(), Kdf.next()
                    for h in range(4):
                        if kind == 'own':
                            act(kd[0:nt, 256 * h:256 * h + 256], psb16[0:nt, 256 * h:256 * h + 256], ACT.Copy, [ps, tab], [kd],
                                scale=tcol('dk', h, 1)[0:nt])
                            ts('dve', kdf[0:nt, 256 * h:256 * h + 256], psb16[0:nt, 256 * h:256 * h + 256],
                               tcol('dkf', 4 * ch + h, 1)[0:nt], None, ALU.mult, ALU.bypass, [ps, tab], [kdf])
                        else:
                            ts('dve', kdf[0:nt, 256 * h:256 * h + 256], psb16[0:nt, 256 * h:256 * h + 256],
                               tcol('dkm', h, 1)[0:nt], None, ALU.mult, ALU.bypass, [ps, tab], [kdf])
                    vc = Vc.next()
                    for h in range(4):
                        ps = PSR.next()
                        for k in range(8):
                            mm(ps[0:nt, :], hn[:, k, ci * 128:ci * 128 + nt], wv[:, k, 512 * h:512 * h + 512], k == 0, k == 7, [hn, wv], [ps])
                        cp('act' if h % 2 == 0 else 'dve', vc[0:nt, 512 * h:512 * h + 512], ps[0:nt, :], [ps], [vc])
                    if kind == 'own':
                        dma('sp', scr_kd.ap()[:, ch, :], kd.ap, [kd], [sk_kd], f'skd{ch % 2}')
                        dma('sp', scr_v.ap()[:, ch, :], vc.ap, [vc], [sk_v], f'sv{ch % 2}')
                    for h in range(4):
                        for dtl in range(2):
                            ps = PSR.next()
                            mm(ps[:, :], kdf[0:nt, 256 * h + 128 * dtl:256 * h + 128 * dtl + 128], vc[0:nt, 512 * h:512 * h + 512],
                               True, True, [kdf, vc], [ps])
                            if kind == 'own':
                                tt('dve', Sacc[:, 2 * h + dtl, :], Sacc[:, 2 * h + dtl, :], ps[:, :], ALU.add, [Sacc, ps], [Sacc])
                            else:
                                st_ = smst.next()
                                cp('dve', st_.ap, ps[:, :], [ps], [st_])
                                dma('sp', scr_sm.ap()[:, 2 * h + dtl, :], st_.ap, [st_], [sk_sm], f'ssm{(2 * h + dtl) % 2}')
            dump('pa', hT.ap, HB_ + [Sacc])
            cc2i, cc2o = Buf("cc2i"), Buf("cc2o")
            P.barrier()
            A.release(mA2)
            wqr = A.alloc([8, 1024], BF16, "wqr")
            mB = A.mark()
            Sbf = A.alloc([8, 512], BF16, "Sbf")
            cp('act', Sbf[:, 0:4, :], Sacc[:, 0:4, :], [Sacc], [Sbf])
            cp('dve', Sbf[:, 4:8, :], Sacc[:, 4:8, :], [Sacc], [Sbf])
            dma('sp', cc2_in.ap().rearrange("g p (t b) -> p g t b", t=2), Sbf.ap.rearrange("p (g t) b -> p g t b", t=2),
                [Sbf], [cc2i], 'cc2')
            cc2og = [Buf(f"cc2o{g}") for g in range(4)]
            for g4 in range(4):
                def _cc(e, g4=g4):
                    return e.collective_compute("AllGather", ALU.bypass, replica_groups=[[0, 1, 2, 3], [4, 5, 6, 7]],
                                                ins=[cc2_in.ap()[g4].opt()], outs=[cc2_out.ap()[g4].opt()])
                P.op('pool', _cc, [cc2i], [cc2og[g4]])
            load_w(wqr, w_in1[:, 0:1024], 8, 1024, 'wq1')
            Sin = Sacc
            sgm = A.alloc([8, 512], F32, "sgm")
            sgr = [A.alloc([4, 2, 512], BF16, f"sgr{i}") for i in range(2)]
            dma('sp', sgm.ap, scr_sm.ap(), [sk_sm], [sgm], 'cc2m')
            for h in range(4):
                for dtl in range(2):
                    f = 2 * h + dtl
                    ts('dve', Sin[:, f, :], sgm[:, f, :], tcol('cf', 16 + h, 1), None, ALU.mult, ALU.bypass, [sgm, tab], [Sin])
            for g4 in range(4):
                sgt = sgr[g4 % 2]
                dma('sp', sgt.ap, cc2_out.ap()[g4].rearrange("(r p) (t b) -> p r t b", p=128, t=2), [cc2og[g4]], [sgt], f'cc2b{g4 % 2}')
                for c in range(4):
                    for dtl in range(2):
                        f = 2 * g4 + dtl
                        stt(Sin[:, f, :], sgt[:, c, dtl, :], tcol('cf', 4 * c + g4, 1), Sin[:, f, :], ALU.mult, ALU.add, [sgt, tab, Sin], [Sin])
            P.barrier()
            A.release(mB)
            Sf = Sacc
            dump('ex2', hT.ap, HB_ + [Sacc])

            Sb = A.alloc([8, 512], BF16, "Sb")
            BWB = 512
            tabB = A.alloc([512], F32, "tabB")
            dma('sp', tabB.ap, tab_d[:, NTABP:NTABP + 512], [], [tabB], 'tabB')
            dqx = A.alloc([4, BWB], BF16, "dqx")
            hnr = [A.alloc([8, BWB], BF16, f"hn2{i}") for i in range(2)]
            Fn = Ring([A.alloc([BWB], F32, f"FnB{i}") for i in range(2)])
            Hn = Ring([A.alloc([BWB], BF16, f"HnB{i}") for i in range(2)])
            F = Ring([A.alloc([BWB], F32, f"J{i}") for i in range(6)])
            dq_tmp = F.next()
            dma('sp', dq_tmp.ap, tab_d[:, NTABP + 512:NTABP + 1024], [], [dq_tmp], 'tabB2')
            for h in range(4):
                for ci in range(BWB // 128):
                    cp('dve', dqx[:, h, 128 * ci:128 * ci + 128], dq_tmp[:, 128 * h:128 * h + 128], [dq_tmp], [dqx])
            QTb = A.alloc([8, BWB], BF16, "QTb")
            QDb = A.alloc([8, BWB], BF16, "QDb")
            rc = A.alloc([BWB], F32, "rc2")
            rs_ = A.alloc([BWB], F32, "rs2")
            KTc = Ring([A.alloc([8, 128], BF16, f"KTc{i}") for i in range(2)])
            Kdc = Ring([A.alloc([1024], BF16, f"Kdc{i}") for i in range(2)])
            Vcc = Ring([A.alloc([2048], BF16, f"Vcc{i}") for i in range(2)])
            PTr = Ring([A.alloc([512], BF16, f"PTr{i}") for i in range(2)])
            onr = Ring([A.alloc([2048], BF16, f"on{i}") for i in range(2)])
            junk = Ring([A.alloc([512], BF16, f"junk{i}") for i in range(4)])
            osbr = Ring([A.alloc([512], F32, f"osb{i}") for i in range(4)])
            ssq = A.alloc([8], F32, "ssq")
            SfB = [Buf(f"Sf{f}") for f in range(8)]
            SbB = [Buf(f"Sb{f}") for f in range(8)]
            cp('act', Sb.ap, Sf.ap, [Sf], [Sb] + SbB)
            DH = [float(np.exp(128.0 * LOGG[h])) for h in range(4)]
            def norm_B(i):
                if i >= NOWN // BWB:
                    return
                c0_, n_ = i * BWB, BWB
                ht = hnr[i % 2]
                rmsnorm(lambda k: hT[:, k, c0_:c0_ + n_], 8, n_, 'g_ret', ht, lambda k: ht[:, k, 0:n_], hblks(c0_, n_), Fn, Hn)
            norm_B(0)
            for b in range(NOWN // BWB):
                c0, n = b * BWB, BWB
                hn = hnr[b % 2]
                dma('sp', rc[:, 0:n], retcs_d[0, :, c0:c0 + n], [], [rc], 'rc')
                dma('sp', rs_[:, 0:n], retcs_d[1, :, c0:c0 + n], [], [rs_], 'rc')
                for h in range(4):
                    pa = proj_fm(wqr, 8, lambda k: wqr[:, k, 256 * h:256 * h + 128], lambda k: hn[:, k, 0:n], n, [hn])
                    pb = proj_fm(wqr, 8, lambda k: wqr[:, k, 256 * h + 128:256 * h + 256], lambda k: hn[:, k, 0:n], n, [hn])
                    rope_ret(pa, pb, n, rc[:, 0:n], rs_[:, 0:n], [rc, rs_], QTb[:, 2 * h, 0:n], QTb[:, 2 * h + 1, 0:n], QTb)
                    for dtl in range(2):
                        tt('dve', QDb[:, 2 * h + dtl, :].rearrange("p (c i) -> p c i", i=128),
                           QTb[:, 2 * h + dtl, :].rearrange("p (c i) -> p c i", i=128),
                           dqx[:, h, :].rearrange("p (c i) -> p c i", i=128), ALU.mult, [QTb, dqx], [QDb])
                norm_B(b + 1)
                def chunk_front(ci):
                    ch = c0 // 128 + ci
                    cs = slice(128 * ci, 128 * ci + 128)
                    ktc, kdc, vcc = KTc.next(), Kdc.next(), Vcc.next()
                    dma('sp', ktc.ap, scr_kt.ap()[:, :, ch * 128:ch * 128 + 128], [sk_kt], [ktc], f'lk{ch % 2}')
                    dma('sp', kdc.ap, scr_kd.ap()[:, ch, :], [sk_kd], [kdc], f'lkd{ch % 2}')
                    dma('sp', vcc.ap, scr_v.ap()[:, ch, :], [sk_v], [vcc], f'lv{ch % 2}')
                    pA = PSR.next()
                    for h in range(4):
                        for dtl in range(2):
                            mm(pA[:, 128 * h:128 * h + 128], ktc[:, 2 * h + dtl, :], QTb[:, 2 * h + dtl, cs], dtl == 0, dtl == 1, [ktc, QTb], [pA])
                    pt = PTr.next()
                    tt('dve', pt.ap, pA.ap, tabB[:, 0:512], ALU.mult, [pA, tabB], [pt])
                    return (ch, cs, kdc, vcc, pt)

                nxt = chunk_front(0)
                for ci in range(BWB // 128):
                    ch, cs, kdc, vcc, pt = nxt
                    on = onr.next()
                    osbs = []
                    for h in range(4):
                        po = PSR.next()
                        mm(po[:, :], pt[:, 128 * h:128 * h + 128], vcc[:, 512 * h:512 * h + 512], True, False, [pt, vcc], [po])
                        for dtl in range(2):
                            mm(po[:, :], QDb[:, 2 * h + dtl, cs], Sb[:, 2 * h + dtl, :], False, dtl == 1, [QDb, SbB[2 * h + dtl]], [po])
                        osb = osbr.next()
                        cp('act', osb.ap, po.ap, [po], [osb])
                        osbs.append(osb)
                    for h in range(4):
                        for dtl in range(2):
                            f = 2 * h + dtl
                            pS = PSR.next()
                            mm(pS[:, :], kdc[:, 256 * h + 128 * dtl:256 * h + 128 * dtl + 128], vcc[:, 512 * h:512 * h + 512], True, True, [kdc, vcc], [pS])
                            stt(Sf[:, f, :], Sf[:, f, :], DH[h], pS.ap, ALU.mult, ALU.add, [SfB[f], pS], [SfB[f]])
                            cp('act' if dtl == 0 else 'dve', Sb[:, f, :], Sf[:, f, :], [SfB[f]], [SbB[f]])
                    if ci + 1 < BWB // 128:
                        nxt = chunk_front(ci + 1)
                    for h in range(4):
                        jk = junk.next()
                        P.op('dve', (lambda e, jk=jk, ob=osbs[h], h=h: e.scalar_tensor_tensor(
                            out=jk.ap, in0=ob.ap, scalar=1.0, in1=ob.ap, op0=ALU.mult, op1=ALU.mult,
                            accum_out=ssq[:, h:h + 1])), [osbs[h]], [jk, ssq])
                    act(ssq[:, 4:8], ssq[:, 0:4], ACT.Sqrt, [ssq, epsc], [ssq], scale=1.0 / 512, bias=epsc[:, 0:1])
                    recip(ssq[:, 4:8], ssq[:, 4:8], [ssq], [ssq])
                    for h in range(4):
                        act(on[:, 512 * h:512 * h + 512], osbs[h].ap, ACT.Copy, [osbs[h], ssq], [on], scale=ssq[:, 4 + h:5 + h])
                    dma('act', scr_on.ap()[:, ch, :], on.ap, [on], [sk_on], f'son{ch % 2}')
            P.barrier()
            A.release(m2)
            dump('h2b', hT.ap, HB_)

            wgr = A.alloc([8, 2048], BF16, "wgr")
            wor = A.alloc([16, D], BF16, "wor")
            load_w(wgr, w_in1[:, 4096:6144], 8, 2048, 'wg1')
            load_w(wor, w_out1, 16, D, 'wo1')
            BW2 = 512
            hnr = [A.alloc([8, BW2], BF16, f"hn3{i}") for i in range(2)]
            Fn = Ring([A.alloc([BW2], F32, f"FnC{i}") for i in range(2)])
            Hn = Ring([A.alloc([BW2], BF16, f"HnC{i}") for i in range(2)])
            sgT = A.alloc([16, BW2], BF16, "sgT")
            gT = A.alloc([16, BW2], BF16, "gT")
            onb = Ring([A.alloc([2048], BF16, f"onb{i}") for i in range(2)])

            def norm_B2(i):
                if i >= NOWN // BW2:
                    return
                c0_, n_ = i * BW2, BW2
                ht = hnr[i % 2]
                rmsnorm(lambda k: hT[:, k, c0_:c0_ + n_], 8, n_, 'g_ret', ht, lambda k: ht[:, k, 0:n_], hblks(c0_, n_), Fn, Hn)
            norm_B2(0)
            for b in range(NOWN // BW2):
                c0, n = b * BW2, BW2
                hn = hnr[b % 2]
                for ft in range(16):
                    ps = proj_fm(wgr, 8, lambda k: wgr[:, k, 128 * ft:128 * ft + 128], lambda k: hn[:, k, 0:n], n, [hn])
                    act(sgT[:, ft, 0:n], ps[:, 0:n], ACT.Silu, [ps], [sgT])
                norm_B2(b + 1)
                for ci in range(BW2 // 128):
                    ch = c0 // 128 + ci
                    ob = onb.next()
                    dma('sp', ob.ap, scr_on.ap()[:, ch, :], [sk_on], [ob], f'lon{ch % 2}')
                    for g2 in range(2):
                        ps = PSR.next()
                        psb16 = ps.ap.bitcast(BF16)
                        for f8 in range(8):
                            ft = 8 * g2 + f8
                            tr(psb16[:, 128 * f8:128 * f8 + 128], ob[:, 128 * ft:128 * ft + 128], identb.ap, [ob, identb], [ps])
                        tt('dve', gT[:, 8 * g2:8 * g2 + 8, 128 * ci:128 * ci + 128], psb16.rearrange("p (a b) -> p a b", a=8),
                           sgT[:, 8 * g2:8 * g2 + 8, 128 * ci:128 * ci + 128], ALU.mult, [ps, sgT], [gT])
                for mo in range(8):
                    ps = proj_fm(wor, 16, lambda k: wor[:, k, 128 * mo:128 * mo + 128], lambda k: gT[:, k, 0:n], n, [gT])
                    tt('dve', hT[:, mo, c0:c0 + n], hT[:, mo, c0:c0 + n], ps[:, 0:n], ALU.add, hblks(c0, n) + [ps], hblks(c0, n))
            P.barrier()
            A.release(m2)
            dump('h3', hT.ap, HB_)

            ffn(1, 'g_ffn1', False)

            ot = Ring([A.alloc([D], F32, f"ot{i}") for i in range(2)])
            outb = Buf("out")
            for ch in range(16):
                o = ot.next()
                for g in range(2):
                    ps = PSR.next()
                    for kk in range(4):
                        k = 4 * g + kk
                        tr(ps[:, 128 * kk:128 * kk + 128], hT[:, k, ch * 128:ch * 128 + 128], ident_f, [hblk(ch * 128), tab], [ps])
                    cp('act' if g == 0 else 'dve', o[:, 512 * g:512 * g + 512], ps.ap, [ps], [o])
                dma('sp', out_d[ch * 128:ch * 128 + 128, :], o.ap, [o], [outb], f'out{ch % 2}')

        except _Stop:
            pass
        P.barrier()
        P.build()
    return nc


def _core_inputs(c, inp):
    b, r = c // 4, c % 4
    x = inp['x']
    meta = inp['meta_tokens']
    own = x[b, NOWN * r:NOWN * (r + 1)]
    if r > 0:
        halo = x[b, NOWN * r - NHALO:NOWN * r]
        hpos = 16 + NOWN * r - NHALO + np.arange(NHALO)
    else:
        halo = np.concatenate([np.zeros((NHALO - NMETA, D), np.float32), meta], 0)
        hpos = np.maximum(np.arange(NHALO) - (NHALO - NMETA), 0)
    xin = np.ascontiguousarray(np.concatenate([own, meta, halo], 0), dtype=np.float32)
    opos = 16 + NOWN * r + np.arange(NOWN)
    mpos = np.arange(NMETA)

    par = np.zeros((128, NPAR), np.float32)

    def put(name, arr):
        o, w = PAR[name]
        par[:, o:o + w] = arr
    put('g_ab', inp['mix_norm_ab'][0].reshape(8, 128).T)
    put('g_ffn0', inp['ffn_norm'][0].reshape(8, 128).T)
    put('g_ret', inp['mix_norm_ret'][0].reshape(8, 128).T)
    put('g_ffn1', inp['ffn_norm'][1].reshape(8, 128).T)
    cw = inp['lru_conv_w'][0]
    put('convw', cw.reshape(4, 4, 128).transpose(2, 1, 0).reshape(128, 16))
    put('convb', inp['lru_conv_b'][0].reshape(4, 128).T)
    put('ba', inp['lru_b_a'][0].reshape(4, 128).T)
    put('bi', inp['lru_b_i'][0].reshape(4, 128).T)
    put('lam', inp['lru_lambda'][0].reshape(4, 128).T)
    put('qg', np.tile(inp['q_norm'][0], 2)[:, None])
    put('kg', np.tile(inp['k_norm'][0], 2)[:, None])
    put('sink', np.broadcast_to(inp['attn_sinks'][0][None, :], (128, 8)))

    tab, tab2 = _host_tables(r)

    def tput(name, arr):
        o, w = TAB[name]
        tab[:arr.shape[0], o:o + w] = arr
    C, S = _att_tables(mpos)
    tput('attc_m', C)
    tput('atts_m', S)
    C, S = _att_tables(hpos)
    tput('attc_h', C)
    tput('atts_h', S)
    rc, rs = _ret_tables(mpos)
    tput('retc_m', rc)
    tput('rets_m', rs)
    C, S = _att_tables(opos)
    attcs = np.stack([C, S], 0)
    rc, rs = _ret_tables(opos)
    retcs = np.stack([rc, rs], 0)
    return {
        "xin": xin, "params": par, "tables": tab, "tables2": tab2, "attcs": np.ascontiguousarray(attcs), "retcs": np.ascontiguousarray(retcs),
        "ab_w_in": inp['ab_w_in'][0], "ab_w_out": inp['ab_w_out'][0], "lru_w_a": inp['lru_w_a'][0], "lru_w_i": inp['lru_w_i'][0],
        "ret_w_in": inp['ret_w_in'][0], "ret_w_out": inp['ret_w_out'][0], "ffn_w_gu": inp['ffn_w_gu'], "ffn_w_down": inp['ffn_w_down'],
    }


_NC_CACHE = {}


def kernel(dbg=(), stop=None, **inputs):
    inp = {k: np.asarray(v) for k, v in inputs.items()}
    key = (tuple(dbg), stop)
    if key not in _NC_CACHE:
        _NC_CACHE[key] = build_program(dbg, stop)
    nc = _NC_CACHE[key]
    in_maps = [_core_inputs(c, inp) for c in range(8)]
    res = run_bass_kernel_spmd(nc, in_maps, core_ids=list(range(8)))
    out = np.zeros((2, 8192, D), np.float32)
    for c in range(8):
        b, r = c // 4, c % 4
        out[b, NOWN * r:NOWN * (r + 1)] = res.results[c]["out"]
    if dbg:
        return out, res
    return out
```
